# Optimizing a Trainium2 kernel written in Bass

```python
import jax, jax.numpy as jnp
from jax import lax
import numpy as np

D_MODEL = 2048
BATCH = 16
SEQ = 2048
DEPTH = 4

N_MIXERS = 3
N_A = (DEPTH + 2) // 3
N_B = (DEPTH + 1) // 3
N_C = DEPTH // 3
RMS_EPS = 1e-6
FFN_DIM = D_MODEL * 11 // 4
FFN_CONV = 3
A_DIM = D_MODEL
A_CHUNK = 128
A_GROUP_DIM = 128
A_GROUPS = A_DIM // A_GROUP_DIM
B_HEAD_DIM = 128
B_HEADS = D_MODEL // B_HEAD_DIM
B_DIM = B_HEADS * B_HEAD_DIM
B_BLOCK = 128
C_HEAD_DIM = 128
C_K_HEADS = D_MODEL // C_HEAD_DIM
C_V_HEADS = 2 * C_K_HEADS
C_DK = C_K_HEADS * C_HEAD_DIM
C_DV = C_V_HEADS * C_HEAD_DIM
C_CONV = 4
C_CHUNK = 64

kernel_name = 'hybrid_gmlp_fox_gdn_convffn'


def rmsnorm(x, g):
    xf = x.astype(jnp.float32)
    y = xf * lax.rsqrt(jnp.mean(xf * xf, axis=-1, keepdims=True) + RMS_EPS)
    return (y * g.astype(jnp.float32)).astype(x.dtype)


def l2norm(x):
    xf = x.astype(jnp.float32)
    return xf * lax.rsqrt(jnp.sum(xf * xf, axis=-1, keepdims=True) + RMS_EPS)


def causal_dwconv(x, w):
    k, c = w.shape
    return lax.conv_general_dilated(
        x, w.astype(x.dtype)[:, None, :], window_strides=(1,), padding=[(k - 1, 0)],
        dimension_numbers=('NWC', 'WIO', 'NWC'), feature_group_count=c)


def conv_ffn(x, w_gate, w_up, conv_w, conv_b, w_down):
    gate = causal_dwconv(x @ w_gate, conv_w) + conv_b
    return (jax.nn.silu(gate) * (x @ w_up)) @ w_down


def mixer_gmlp(x, w_in, b_in, v_norm, w_s, b_s, w_out):
    bsz, s, _ = x.shape
    h = jax.nn.gelu(x @ w_in + b_in)
    u, v = jnp.split(h, 2, axis=-1)
    v = rmsnorm(v, v_norm).reshape(bsz, s // A_CHUNK, A_CHUNK, A_GROUPS, A_GROUP_DIM)
    tri = jnp.tril(jnp.ones((A_CHUNK, A_CHUNK), dtype=bool))
    w_causal = jnp.where(tri, w_s, 0)
    sv = jnp.einsum('gts,bnsgd->bntgd', w_causal.astype(v.dtype), v) + b_s.T[None, None, :, :, None]
    return (u * sv.reshape(bsz, s, A_DIM)) @ w_out


def mixer_fox(x, w_in, b_f, q_norm, k_norm, w_out):
    bsz, s, _ = x.shape
    proj = x @ w_in
    q, k, v, og, fl = jnp.split(proj, [B_DIM, 2 * B_DIM, 3 * B_DIM, 4 * B_DIM], axis=-1)
    q = rmsnorm(q.reshape(bsz, s, B_HEADS, B_HEAD_DIM), q_norm).transpose(0, 2, 1, 3)
    k = rmsnorm(k.reshape(bsz, s, B_HEADS, B_HEAD_DIM), k_norm).transpose(0, 2, 1, 3)
    v = v.reshape(bsz, s, B_HEADS, B_HEAD_DIM).transpose(0, 2, 1, 3)
    log_f = jax.nn.log_sigmoid((fl + b_f).astype(jnp.float32))
    c = jnp.cumsum(log_f, axis=1).transpose(0, 2, 1)
    scale = B_HEAD_DIM ** -0.5
    tri = jnp.tril(jnp.ones((B_BLOCK, B_BLOCK), dtype=bool))
    outs = []
    for i in range(s // B_BLOCK):
        lo, hi = i * B_BLOCK, (i + 1) * B_BLOCK
        sc = jnp.einsum('bhqd,bhkd->bhqk', q[:, :, lo:hi], k[:, :, :hi]).astype(jnp.float32) * scale
        sc = sc + c[:, :, lo:hi, None] - c[:, :, None, :hi]
        mask = jnp.concatenate([jnp.ones((B_BLOCK, lo), dtype=bool), tri], axis=1)
        sc = jnp.where(mask, sc, -jnp.inf)
        p = jax.nn.softmax(sc, axis=-1).astype(v.dtype)
        outs.append(jnp.einsum('bhqk,bhkd->bhqd', p, v[:, :, :hi]))
    o = jnp.concatenate(outs, axis=2).transpose(0, 2, 1, 3).reshape(bsz, s, B_DIM)
    return (o * jax.nn.sigmoid(og)) @ w_out


def gated_delta_rule_chunked(q, k, v, beta, g):
    bsz, s, h, dk = q.shape
    dv = v.shape[-1]
    n = s // C_CHUNK

    def chunks(t):
        return t.reshape(bsz, n, C_CHUNK, h, -1).transpose(1, 0, 3, 2, 4)

    q, k, v = chunks(q), chunks(k), chunks(v)
    beta = beta.reshape(bsz, n, C_CHUNK, h).transpose(1, 0, 3, 2)
    g = jnp.cumsum(g.reshape(bsz, n, C_CHUNK, h).transpose(1, 0, 3, 2), axis=-1)
    tri = jnp.tril(jnp.ones((C_CHUNK, C_CHUNK), dtype=bool))
    strict = jnp.tril(jnp.ones((C_CHUNK, C_CHUNK), dtype=bool), -1)
    decay = jnp.exp(jnp.where(tri, g[..., :, None] - g[..., None, :], -jnp.inf))
    kb = k * beta[..., None]
    a_mat = jnp.where(strict, jnp.einsum('nbhtd,nbhsd->nbhts', kb, k) * decay, 0.0)
    rhs = jnp.concatenate([v * beta[..., None], kb * jnp.exp(g)[..., None]], axis=-1)
    sol = lax.linalg.triangular_solve(a_mat + jnp.eye(C_CHUNK, dtype=a_mat.dtype), rhs,
                                      left_side=True, lower=True, unit_diagonal=True)
    u, w = sol[..., :dv], sol[..., dv:]
    attn = jnp.einsum('nbhtd,nbhsd->nbhts', q, k) * decay
    q_dec = q * jnp.exp(g)[..., None]
    g_last = g[..., -1]
    k_dec = k * jnp.exp(g_last[..., None] - g)[..., None]

    def step(state, xs):
        u_n, w_n, attn_n, q_n, k_n, gl = xs
        v_new = u_n - jnp.einsum('bhcd,bhde->bhce', w_n, state)
        o = jnp.einsum('bhcd,bhde->bhce', q_n, state) + jnp.einsum('bhts,bhse->bhte', attn_n, v_new)
        state = state * jnp.exp(gl)[..., None, None] + jnp.einsum('bhcd,bhce->bhde', k_n, v_new)
        return state, o

    state0 = jnp.zeros((bsz, h, dk, dv), dtype=jnp.float32)
    _, o = lax.scan(step, state0, (u, w, attn, q_dec, k_dec, g_last))
    return o.transpose(1, 0, 3, 2, 4).reshape(bsz, s, h, dv)


def mixer_gdn(x, w_in, conv_w, a_log, dt_bias, out_norm, w_out):
    bsz, s, _ = x.shape
    proj = x @ w_in
    qkv, z, b, a = jnp.split(proj, [2 * C_DK + C_DV, 2 * C_DK + 2 * C_DV, 2 * C_DK + 2 * C_DV + C_V_HEADS], axis=-1)
    qkv = jax.nn.silu(causal_dwconv(qkv, conv_w))
    q, k, v = jnp.split(qkv, [C_DK, 2 * C_DK], axis=-1)
    rep = C_V_HEADS // C_K_HEADS
    q = jnp.repeat(l2norm(q.reshape(bsz, s, C_K_HEADS, C_HEAD_DIM)), rep, axis=2) * (C_HEAD_DIM ** -0.5)
    k = jnp.repeat(l2norm(k.reshape(bsz, s, C_K_HEADS, C_HEAD_DIM)), rep, axis=2)
    v = v.reshape(bsz, s, C_V_HEADS, C_HEAD_DIM).astype(jnp.float32)
    beta = jax.nn.sigmoid(b.astype(jnp.float32))
    g = -jnp.exp(a_log.astype(jnp.float32)) * jax.nn.softplus(a.astype(jnp.float32) + dt_bias.astype(jnp.float32))
    o = gated_delta_rule_chunked(q, k, v, beta, g)
    o = rmsnorm(o, out_norm) * jax.nn.silu(z.reshape(bsz, s, C_V_HEADS, C_HEAD_DIM).astype(jnp.float32))
    return o.reshape(bsz, s, C_DV).astype(x.dtype) @ w_out


def setup_inputs(seed: int = 0) -> dict:
    key = jax.random.key(seed)
    keys = list(jax.random.split(key, 32))
    ctr = [0]

    def nk():
        ctr[0] += 1
        return keys[ctr[0] - 1]

    def nrm(shape, scale):
        return scale * jax.random.normal(nk(), shape, jnp.float32)

    def gain(shape):
        return 1.0 + 0.05 * jax.random.normal(nk(), shape, jnp.float32)

    def unif(shape, lo, hi):
        return jax.random.uniform(nk(), shape, jnp.float32, lo, hi)

    d = D_MODEL
    dt = jnp.exp(unif((N_C, C_V_HEADS), float(np.log(1e-3)), float(np.log(1e-1))))
    return {
        'x': nrm((BATCH, SEQ, d), 1.0),
        'norm_mix': gain((DEPTH, d)),
        'norm_ffn': gain((DEPTH, d)),
        'ffn_w_gate': nrm((DEPTH, d, FFN_DIM), d ** -0.5),
        'ffn_w_up': nrm((DEPTH, d, FFN_DIM), d ** -0.5),
        'ffn_conv_w': nrm((DEPTH, FFN_CONV, FFN_DIM), FFN_CONV ** -0.5),
        'ffn_conv_b': nrm((DEPTH, FFN_DIM), 0.02),
        'ffn_w_down': nrm((DEPTH, FFN_DIM, d), FFN_DIM ** -0.5),
        'a_w_in': nrm((N_A, d, 2 * A_DIM), d ** -0.5),
        'a_b_in': nrm((N_A, 2 * A_DIM), 0.02),
        'a_v_norm': gain((N_A, A_DIM)),
        'a_w_s': nrm((N_A, A_GROUPS, A_CHUNK, A_CHUNK), A_CHUNK ** -0.5),
        'a_b_s': gain((N_A, A_GROUPS, A_CHUNK)),
        'a_w_out': nrm((N_A, A_DIM, d), A_DIM ** -0.5),
        'b_w_in': nrm((N_B, d, 4 * B_DIM + B_HEADS), d ** -0.5),
        'b_b_f': unif((N_B, B_HEADS), 1.0, 4.0),
        'b_q_norm': gain((N_B, B_HEAD_DIM)),
        'b_k_norm': gain((N_B, B_HEAD_DIM)),
        'b_w_out': nrm((N_B, B_DIM, d), B_DIM ** -0.5),
        'c_w_in': nrm((N_C, d, 2 * C_DK + 2 * C_DV + 2 * C_V_HEADS), d ** -0.5),
        'c_conv_w': nrm((N_C, C_CONV, 2 * C_DK + C_DV), C_CONV ** -0.5),
        'c_a_log': jnp.log(unif((N_C, C_V_HEADS), 1.0, 16.0)),
        'c_dt_bias': dt + jnp.log(-jnp.expm1(-dt)),
        'c_out_norm': gain((N_C, C_HEAD_DIM)),
        'c_w_out': nrm((N_C, C_DV, d), C_DV ** -0.5),
    }


def reference(x, norm_mix, norm_ffn, ffn_w_gate, ffn_w_up, ffn_conv_w, ffn_conv_b, ffn_w_down,
              a_w_in, a_b_in, a_v_norm, a_w_s, a_b_s, a_w_out,
              b_w_in, b_b_f, b_q_norm, b_k_norm, b_w_out,
              c_w_in, c_conv_w, c_a_log, c_dt_bias, c_out_norm, c_w_out):
    for i in range(DEPTH):
        kind, j = i % N_MIXERS, i // N_MIXERS
        h = rmsnorm(x, norm_mix[i])
        if kind == 0:
            m = mixer_gmlp(h, a_w_in[j], a_b_in[j], a_v_norm[j], a_w_s[j], a_b_s[j], a_w_out[j])
        elif kind == 1:
            m = mixer_fox(h, b_w_in[j], b_b_f[j], b_q_norm[j], b_k_norm[j], b_w_out[j])
        else:
            m = mixer_gdn(h, c_w_in[j], c_conv_w[j], c_a_log[j], c_dt_bias[j], c_out_norm[j], c_w_out[j])
        x = x + m
        x = x + conv_ffn(rmsnorm(x, norm_ffn[i]), ffn_w_gate[i], ffn_w_up[i], ffn_conv_w[i],
                         ffn_conv_b[i], ffn_w_down[i])
    return x
```

```python
import numpy as np
from contextlib import ExitStack
import concourse.bass as bass
import concourse.mybir as mybir
from concourse.bass_utils import run_bass_kernel_spmd

F32, BF16 = mybir.dt.float32, mybir.dt.bfloat16
AF = mybir.ActivationFunctionType
ALU = mybir.AluOpType
AX = mybir.AxisListType

NCORES = 8
D = 2048
NT = 4096
SEQ = 2048
FF = 5632
NFC = FF // 128
EPS = 1e-6
DEBUG = False
CSTOP = 99
ASTOP = 99
QMODE = 1
ILV = 2
STAG = 3
KHN = 16


class Dep:
    __slots__ = ("w", "r")

    def __init__(self):
        self.w = {}
        self.r = {}


class T(Dep):
    __slots__ = ("t", "dsem")

    def __init__(self, t):
        Dep.__init__(self)
        self.t = t
        self.dsem = None

    def __getitem__(self, k):
        return self.t[k]


class Eng:
    def __init__(self, name, sem, is_pe=False):
        self.name, self.sem, self.is_pe = name, sem, is_pe
        self.count = 0
        self.seen = {}
        self.q = []


class DSem:
    def __init__(self, sem):
        self.sem = sem
        self.count = 0


class Ring:
    def __init__(self, tiles):
        self.tiles = tiles
        self.i = 0

    def next(self):
        t = self.tiles[self.i % len(self.tiles)]
        self.i += 1
        return t


def _merge(a, b):
    for k, v in b.items():
        if a.get(k, 0) < v:
            a[k] = v


class KB:
    def __init__(self, nc, es):
        self.nc, self.es = nc, es
        self.sems = {}
        self.E = {}
        for n in ("pe", "act", "dve", "pool", "sp"):
            self.E[n] = Eng(n, self.newsem("prog_" + n), is_pe=(n == "pe"))
        self.init_arena()

    def newsem(self, name):
        name = f"{name}_{len(self.sems)}"
        s = self.es.enter_context(self.nc.semaphore(name))
        self.sems[id(s)] = s
        return s

    ARENA_BYTES = 212480

    def init_arena(self):
        self.arena = self.es.enter_context(self.nc.sbuf_tensor("arena", [128, self.ARENA_BYTES // 2], BF16))
        self.off = 0
        self.dsem_pool = []
        self.live = []

    def sb(self, shape, dtype, name=None):
        esz = 4 if dtype == F32 else 2
        n = 1
        for d in shape[1:]:
            n *= d
        nbytes = (n * esz + 63) // 64 * 64
        assert self.off + nbytes <= self.ARENA_BYTES, f"SBUF arena overflow {name} {self.off + nbytes}"
        ap = self.arena[0:shape[0], self.off // 2:(self.off + n * esz) // 2]
        self.off += nbytes
        if dtype == F32:
            ap = ap.bitcast(F32)
        if len(shape) == 3:
            ap = ap.rearrange("p (a b) -> p a b", a=shape[1])
        elif len(shape) == 4:
            ap = ap.rearrange("p (a b c) -> p a b c", a=shape[1], b=shape[2])
        t = T(ap)
        self.live.append(t)
        return t

    def ring(self, n, shape, dtype, name=None):
        return Ring([self.sb(shape, dtype, name) for _ in range(n)])

    def ps(self, name):
        t = self.es.enter_context(self.nc.psum_tensor(name, [128, 512], F32))
        return T(t)

    def get_dsem(self):
        if self.dsem_pool:
            return self.dsem_pool.pop()
        return DSem(self.newsem("d"))

    def _waits(self, E, reads, writes, pwrites, own_sid=None):
        need = {}
        for d in reads:
            _merge(need, d.w)
        for d in writes:
            _merge(need, d.w)
            _merge(need, d.r)
        for d in pwrites:
            _merge(need, d.r)
            for k, v in d.w.items():
                if k != own_sid and need.get(k, 0) < v:
                    need[k] = v
        for sid, val in need.items():
            if E.seen.get(sid, 0) >= val:
                continue
            if E.is_pe and sid == id(E.sem):
                continue
            E.seen[sid] = val
            E.q.append(("w", self.sems[sid], val))

    def _stamp(self, sid, val, reads, writes, pwrites):
        for d in reads:
            if d.r.get(sid, 0) < val:
                d.r[sid] = val
        for d in writes:
            d.w = {sid: val}
            d.r = {}
        for d in pwrites:
            if d.w.get(sid, 0) < val:
                d.w[sid] = val

    def op(self, en, emit, reads=(), writes=(), pwrites=(), mark=True):
        E = self.E[en]
        self._waits(E, reads, writes, pwrites, id(E.sem))
        if mark:
            E.count += 1
            E.q.append(("i", emit, E.sem, 1))
            val = E.count
        else:
            E.q.append(("i", emit, None, 0))
            val = E.count + 1
        self._stamp(id(E.sem), val, reads, writes, pwrites)

    def dma(self, qn, out, in_, sem_tile, reads=(), writes=(), pwrites=(), slow=False):
        E = self.E[qn]
        if sem_tile.dsem is None:
            sem_tile.dsem = self.get_dsem()
        ds = sem_tile.dsem
        self._waits(E, reads, writes, pwrites, id(ds.sem))
        ds.count += 16
        if slow:
            E.q.append(("i", lambda e, o=out, i=in_: e.dma_start(out=o, in_=i, allow_slow_non_contiguous=True), ds.sem, 16))
        else:
            E.q.append(("i", lambda e, o=out, i=in_: e.dma_start(out=o, in_=i), ds.sem, 16))
        self._stamp(id(ds.sem), ds.count, reads, writes, pwrites)

    def mm(self, out, lhsT, rhs, start, stop, reads, bank, mark=None):
        if mark is None:
            mark = stop
        self.op("pe", lambda e: e.matmul(out, lhsT, rhs, start=start, stop=stop),
                reads=reads, pwrites=[bank] if not start else (), writes=[bank] if start else (), mark=mark)

    def finish(self, final_deps):
        E = self.E["sp"]
        need = {}
        for d in final_deps:
            _merge(need, d.w)
        for sid, val in need.items():
            E.q.append(("w", self.sems[sid], val))
        nc = self.nc
        with nc.Block() as block:
            def run(E, e):
                for it in E.q:
                    if it[0] == "w":
                        e.wait_ge(it[1], it[2])
                    else:
                        ins = it[1](e)
                        if it[2] is not None:
                            ins.then_inc(it[2], it[3])

            @block.sync
            def _(e):
                run(self.E["sp"], e)

            @block.scalar
            def _(e):
                run(self.E["act"], e)

            @block.vector
            def _(e):
                run(self.E["dve"], e)

            @block.gpsimd
            def _(e):
                run(self.E["pool"], e)

            @block.tensor
            def _(e):
                run(self.E["pe"], e)


class Resid:
    def __init__(self, ap):
        self.ap = ap
        self.deps = [Dep() for _ in range(NT // 128)]


class Prog:
    def __init__(self, nc, es, sublayers, ntb_seq=16, nseq=2):
        self.nc, self.es = nc, es
        self.K = KB(nc, es)
        self.sublayers = sublayers
        self.TBS = ntb_seq
        self.NSEQ = nseq
        self.declare()

    def declare(self):
        nc = self.nc

        def inp(name, shape):
            return nc.dram_tensor(name, list(shape), F32, kind="ExternalInput").ap()

        self.x = inp("x", [NT, D])
        self.norm_mix = inp("norm_mix", [4, D])
        self.norm_ffn = inp("norm_ffn", [4, D])
        self.ffn_w_gate = inp("ffn_w_gate", [4, D, FF])
        self.ffn_w_up = inp("ffn_w_up", [4, D, FF])
        self.ffn_conv_w = inp("ffn_conv_w", [4, 3, FF])
        self.ffn_conv_b = inp("ffn_conv_b", [4, FF])
        self.ffn_w_down = inp("ffn_w_down", [4, FF, D])
        self.a_w_in = inp("a_w_in", [2, D, 2 * D])
        self.a_b_in = inp("a_b_in", [2, 2 * D])
        self.a_v_norm = inp("a_v_norm", [2, D])
        self.a_w_s = inp("a_w_s", [2, 16, 128, 128])
        self.a_b_s = inp("a_b_s", [2, 16, 128])
        self.a_w_out = inp("a_w_out", [2, D, D])
        self.b_w_in = inp("b_w_in", [1, D, 8208])
        self.b_b_f = inp("b_b_f", [1, 16])
        self.b_q_norm = inp("b_q_norm", [1, 128])
        self.b_k_norm = inp("b_k_norm", [1, 128])
        self.b_w_out = inp("b_w_out", [1, D, D])
        self.c_w_in = inp("c_w_in", [1, D, 12352])
        self.c_conv_w = inp("c_conv_w", [1, 4, 8192])
        self.c_a_log = inp("c_a_log", [1, 32])
        self.c_dt_bias = inp("c_dt_bias", [1, 32])
        self.c_out_norm = inp("c_out_norm", [1, 128])
        self.c_w_out = inp("c_w_out", [1, 2 * D, D])
        self.consts = inp("consts", [128, 1024])
        self.out = nc.dram_tensor("out", [NT, D], F32, kind="ExternalOutput").ap()
        self.XA = nc.dram_tensor("XA", [NT, D], F32).ap()
        self.AT = nc.dram_tensor("AT", [FF, NT], BF16).ap()
        self.OT = nc.dram_tensor("OT", [2 * D, NT], BF16).ap()
        self.CS = nc.dram_tensor("CS", [6, 16, SEQ], BF16).ap()
        if DEBUG:
            self.dbgf = nc.dram_tensor("dbgf", [8, 128, 2048], F32, kind="ExternalOutput").ap()
            self.dbgb = nc.dram_tensor("dbgb", [8, 128, 2048], BF16, kind="ExternalOutput").ap()
            self.dbg_deps = []

    def dump(self, t, ap, idx, f32=False):
        if not DEBUG:
            return
        dst = self.dbgf if f32 else self.dbgb
        shp = ap.shape
        d = Dep()
        self.dbg_deps.append(d)
        self.K.dma("sp", dst[idx, 0:shp[0], 0:shp[1]], ap, t, reads=[t], writes=[d])

    def build(self):
        K = self.K
        self.banks = [K.ps(f"bank{i}") for i in range(8)]
        self.ident_f = K.sb([128, 128], F32, "identf")
        self.ident = K.sb([128, 128], BF16, "ident")
        self.epsT = K.sb([128, 1], F32, "eps")
        K.dma("sp", self.ident_f[:, :], self.consts[:, 0:128], self.ident_f, writes=[self.ident_f])
        K.op("dve", lambda e: e.tensor_copy(out=self.ident[:, :], in_=self.ident_f[:, :]),
             reads=[self.ident_f], writes=[self.ident])
        K.op("dve", lambda e: e.memset(self.epsT[:, :], EPS), writes=[self.epsT])

        res_in = Resid(self.x)
        res_a = Resid(self.XA)
        res_out = Resid(self.out)
        n = len(self.sublayers)
        cur = res_in
        for idx, (kind, li) in enumerate(self.sublayers):
            dst = res_out if idx == n - 1 else res_a
            mk = self.mark()
            if kind == "ffn":
                self.ffn(li, cur, dst)
            elif kind == "a":
                self.mixer_a(li // 3, li, cur, dst)
            elif kind == "b":
                self.mixer_b(li, cur, dst)
            elif kind == "c":
                self.mixer_c(li, cur, dst)
            else:
                raise NotImplementedError(kind)
            self.release(mk)
            cur = dst
        K.finish(res_out.deps + (self.dbg_deps if DEBUG else []))

    def mark(self):
        return (self.K.off, len(self.K.live))

    def release(self, mk):
        self.barrier_free(mk[1])
        self.K.off = mk[0]

    def barrier_free(self, mark_live):
        K = self.K
        dead = K.live[mark_live:]
        del K.live[mark_live:]
        need = {}
        for O in K.E.values():
            if O.count:
                need[id(O.sem)] = O.count
        for t in dead:
            _merge(need, t.w)
            _merge(need, t.r)
            if t.dsem is not None:
                need[id(t.dsem.sem)] = max(need.get(id(t.dsem.sem), 0), t.dsem.count)
                K.dsem_pool.append(t.dsem)
                t.dsem = None
        for en, E in K.E.items():
            for sid, val in need.items():
                if sid == id(E.sem):
                    continue
                if E.seen.get(sid, 0) < val:
                    E.seen[sid] = val
                    E.q.append(("w", K.sems[sid], val))

    def sb(self, shape, dtype, name=None):
        return self.K.sb(shape, dtype, name)

    def ring(self, n, shape, dtype, name=None):
        return self.K.ring(n, shape, dtype, name)

    def norm_setup(self, gamma_row):
        K = self.K
        st = {}
        st["gam"] = self.sb([128, D], F32, "gam")
        K.dma("sp", st["gam"][:, :], gamma_row.partition_broadcast(128), st["gam"], writes=[st["gam"]])
        st["xs"] = self.ring(2, [128, D], F32, "xs")
        st["hn"] = self.ring(2, [128, D], BF16, "hn")
        st["junk"] = self.sb([128, D], BF16, "junk")
        st["ss"] = self.ring(4, [128, 4], F32, "ss")
        return st

    def norm_T(self, st, src, tb_glob, hT, hdep, tcol):
        K = self.K
        xs = st["xs"].next()
        hn = st["hn"].next()
        ss = st["ss"].next()
        gam, junk = st["gam"], st["junk"]
        K.dma("sp", xs[:, :], src.ap[tb_glob * 128:(tb_glob + 1) * 128, :], xs,
              reads=[src.deps[tb_glob]], writes=[xs])
        K.op("dve", lambda e: e.memset(ss[:, :], 0.0), writes=[ss])
        K.op("act", lambda e: e.activation(out=junk[:, :], in_=xs[:, :], func=AF.Square, accum_out=ss[:, 0:1]),
             reads=[xs], writes=[junk], pwrites=[ss])
        K.op("act", lambda e: e.activation(out=ss[:, 1:2], in_=ss[:, 0:1], func=AF.Sqrt,
                                           bias=self.epsT[:, 0:1], scale=1.0 / D),
             reads=[ss, self.epsT], pwrites=[ss])
        K.op("dve", lambda e: e.reciprocal(out=ss[:, 2:3], in_=ss[:, 1:2]), reads=[ss], pwrites=[ss])
        K.op("dve", lambda e: e.scalar_tensor_tensor(out=hn[:, :], in0=xs[:, :], scalar=ss[:, 2:3], in1=gam[:, :],
                                                      op0=ALU.mult, op1=ALU.mult),
             reads=[xs, ss, gam], writes=[hn])
        for half in range(2):
            bank = self.banks[self._tbank % 8]
            self._tbank += 1
            bv = bank.t[:, :].bitcast(BF16)
            for j in range(8):
                kc = half * 8 + j
                K.op("pe", lambda e, o=bv[:, j * 128:(j + 1) * 128], i=hn[:, kc * 128:(kc + 1) * 128]:
                     e.transpose(o, i, self.ident[:, :]),
                     reads=[hn, self.ident], writes=[bank] if j == 0 else (), pwrites=[bank] if j else (),
                     mark=(j == 7))
            eng = "act" if half == 0 else "dve"
            o = hT[:, half * 8:(half + 1) * 8, tcol:tcol + 128]
            i = bv.rearrange("p (k t) -> p k t", k=8)
            if eng == "act":
                K.op("act", lambda e, o=o, i=i: e.copy(out=o, in_=i), reads=[bank], pwrites=[hdep])
            else:
                K.op("dve", lambda e, o=o, i=i: e.tensor_copy(out=o, in_=i), reads=[bank], pwrites=[hdep])

    _tbank = 0

    def ffn(self, li, src, dst):
        K = self.K
        TBS, NSEQ = self.TBS, self.NSEQ
        TS = TBS * 128
        HW = min(1024, TS)
        NH = TS // HW
        mk0 = self.mark()
        st = self.norm_setup(self.norm_ffn[li, :])
        hT = self.sb([128, 16, TS], BF16, "hT")
        hdeps = [Dep() for _ in range(TBS)]
        cw = self.sb([128, 3, NFC], F32, "cw")
        cb = self.sb([128, NFC], F32, "cb")
        for k in range(3):
            K.dma("sp", cw[:, k, :], self.ffn_conv_w[li, k, :].rearrange("(c p) -> p c", p=128), cw, pwrites=[cw], slow=True)
        K.dma("sp", cb[:, :], self.ffn_conv_b[li, :].rearrange("(c p) -> p c", p=128), cb, writes=[cb], slow=True)
        PW = 256
        wgr = self.ring(2, [128, 16, PW], BF16, "wg")
        wur = self.ring(2, [128, 16, PW], BF16, "wu")
        gsr = self.ring(2, [128, HW + 2], F32, "gs")
        t1r = self.ring(2, [128, HW], F32, "t1")
        sgr = self.ring(2, [128, HW], F32, "sg")
        aTr = self.ring(3, [128, HW], BF16, "aT")
        NTT = NT // 256
        at_deps = [Dep() for _ in range(NTT)]
        wgv = self.ffn_w_gate[li].rearrange("(kc p) n -> p kc n", p=128)
        wuv = self.ffn_w_up[li].rearrange("(kc p) n -> p kc n", p=128)
        bset = 0
        for s in range(NSEQ):
            for tb in range(TBS):
                self.norm_T(st, src, s * 16 + tb, hT, hdeps[tb], tb * 128)
            for fp in range(FF // PW):
                wg, wu = wgr.next(), wur.next()
                K.dma("pool", wg[:, :, :], wgv[:, :, fp * PW:(fp + 1) * PW], wg, writes=[wg])
                K.dma("pool", wu[:, :, :], wuv[:, :, fp * PW:(fp + 1) * PW], wu, writes=[wu])
                for fcl in range(PW // 128):
                    fc = fp * (PW // 128) + fcl
                    prev_gs = None
                    for h in range(NH):
                        nb = HW // 512
                        bks = self.banks[bset * 4:(bset + 1) * 4]
                        bset ^= 1
                        psG, psU = bks[0:nb], bks[2:2 + nb]
                        for (w, ps) in ((wg, psG), (wu, psU)):
                            for kc in range(16):
                                for b in range(nb):
                                    t0 = h * HW + b * 512
                                    K.mm(ps[b][:, :], w[:, kc, fcl * 128:(fcl + 1) * 128], hT[:, kc, t0:t0 + 512],
                                         start=(kc == 0), stop=(kc == 15),
                                         reads=[w] + hdeps[t0 // 128:t0 // 128 + 4], bank=ps[b])
                        gs, t1, sg, aT = gsr.next(), t1r.next(), sgr.next(), aTr.next()
                        for b in range(nb):
                            K.op("act", lambda e, o=gs[:, 2 + b * 512:2 + (b + 1) * 512], i=psG[b][:, :]: e.copy(out=o, in_=i),
                                 reads=[psG[b]], writes=[gs] if b == 0 else (), pwrites=[gs] if b else ())
                        if h == 0:
                            K.op("dve", lambda e, o=gs[:, 0:2]: e.memset(o, 0.0), pwrites=[gs])
                        else:
                            K.op("dve", lambda e, o=gs[:, 0:2], i=prev_gs[:, HW:HW + 2]: e.tensor_copy(out=o, in_=i),
                                 reads=[prev_gs], pwrites=[gs])
                        prev_gs = gs
                        K.op("dve", lambda e, o=t1[:, :], i=gs[:, 2:HW + 2], a=cw[:, 2, fc:fc + 1], b_=cb[:, fc:fc + 1]:
                             e.tensor_scalar(out=o, in0=i, scalar1=a, scalar2=b_, op0=ALU.mult, op1=ALU.add),
                             reads=[gs, cw, cb], writes=[t1])
                        K.op("dve", lambda e, o=t1[:, :], i=gs[:, 1:HW + 1], a=cw[:, 1, fc:fc + 1]:
                             e.scalar_tensor_tensor(out=o, in0=i, scalar=a, in1=o, op0=ALU.mult, op1=ALU.add),
                             reads=[gs, cw], writes=[t1])
                        K.op("dve", lambda e, o=t1[:, :], i=gs[:, 0:HW], a=cw[:, 0, fc:fc + 1]:
                             e.scalar_tensor_tensor(out=o, in0=i, scalar=a, in1=o, op0=ALU.mult, op1=ALU.add),
                             reads=[gs, cw], writes=[t1])
                        K.op("act", lambda e, o=sg[:, :], i=t1[:, :]: e.activation(out=o, in_=i, func=AF.Silu),
                             reads=[t1], writes=[sg])
                        for b in range(nb):
                            K.op("dve", lambda e, o=aT[:, b * 512:(b + 1) * 512], i0=psU[b][:, :], i1=sg[:, b * 512:(b + 1) * 512]:
                                 e.tensor_tensor(out=o, in0=i0, in1=i1, op=ALU.mult),
                                 reads=[psU[b], sg], writes=[aT] if b == 0 else (), pwrites=[aT] if b else ())
                        g0 = s * SEQ + h * HW
                        K.dma("sp", self.AT[fc * 128:(fc + 1) * 128, g0:g0 + HW], aT[:, :], aT,
                              reads=[aT], pwrites=at_deps[g0 // 256:(g0 + HW) // 256])
        self.release(mk0)
        wdr = self.ring(2, [128, NFC, 512], BF16, "wd")
        atr = self.ring(2, [128, NFC, 256], BF16, "at")
        xrr = self.ring(3, [128, 512], F32, "xr")
        xor_ = self.ring(3, [128, 512], F32, "xo")
        wdv = self.ffn_w_down[li].rearrange("(fc p) n -> p fc n", p=128)
        atv = self.AT.rearrange("(fc p) t -> p fc t", p=128)
        bi = 0
        for dp in range(4):
            wd = wdr.next()
            for q in range(4):
                K.dma("pool", wd[:, q * 11:(q + 1) * 11, :], wdv[:, q * 11:(q + 1) * 11, dp * 512:(dp + 1) * 512], wd,
                      writes=[wd] if q == 0 else (), pwrites=[wd] if q else ())
            for s in range(NSEQ):
                for tt in range(TS // 256):
                    g0 = s * SEQ + tt * 256
                    at = atr.next()
                    K.dma("sp", at[:, :, :], atv[:, :, g0:g0 + 256], at, reads=[at_deps[g0 // 256]], writes=[at])
                    for tb in range(2):
                        gb = g0 // 128 + tb
                        bank = self.banks[bi % 8]
                        bi += 1
                        for fc in range(NFC):
                            K.mm(bank[:, :], at[:, fc, tb * 128:(tb + 1) * 128], wd[:, fc, :], start=(fc == 0),
                                 stop=(fc == NFC - 1), reads=[at, wd], bank=bank)
                        xr, xo = xrr.next(), xor_.next()
                        K.dma("sp", xr[:, :], src.ap[gb * 128:(gb + 1) * 128, dp * 512:(dp + 1) * 512], xr,
                              reads=[src.deps[gb]], writes=[xr])
                        K.op("dve", lambda e, o=xo[:, :], i0=bank[:, :], i1=xr[:, :]: e.tensor_tensor(out=o, in0=i0, in1=i1, op=ALU.add),
                             reads=[bank, xr], writes=[xo])
                        K.dma("act", dst.ap[gb * 128:(gb + 1) * 128, dp * 512:(dp + 1) * 512], xo[:, :], xo,
                              reads=[xo], pwrites=[dst.deps[gb]])

    def out_proj_tile(self, wv, KC, uT, ntb, g_tb0, src, dst, wring, xrr, xor_):
        K = self.K
        for p in range(4):
            w = wring.next()
            K.dma("pool", w[:, 0:KC, :], wv[:, :, p * 512:(p + 1) * 512], w, writes=[w])
            for tb in range(ntb):
                gb = g_tb0 + tb
                bank = self.banks[self._tbank % 8]
                self._tbank += 1
                for kc in range(KC):
                    K.mm(bank[:, :], uT[:, kc, tb * 128:(tb + 1) * 128], w[:, kc, :], start=(kc == 0), stop=(kc == KC - 1),
                         reads=[uT, w], bank=bank)
                xr, xo = xrr.next(), xor_.next()
                K.dma("sp", xr[:, :], src.ap[gb * 128:(gb + 1) * 128, p * 512:(p + 1) * 512], xr,
                      reads=[src.deps[gb]], writes=[xr])
                K.op("dve", lambda e, o=xo[:, :], i0=bank[:, :], i1=xr[:, :]: e.tensor_tensor(out=o, in0=i0, in1=i1, op=ALU.add),
                     reads=[bank, xr], writes=[xo])
                K.dma("act", dst.ap[gb * 128:(gb + 1) * 128, p * 512:(p + 1) * 512], xo[:, :], xo,
                      reads=[xo], pwrites=[dst.deps[gb]])

    def mixer_a(self, j, li, src, dst):
        K = self.K
        TBS, NSEQ = self.TBS, self.NSEQ
        st = self.norm_setup(self.norm_mix[li, :])
        hT = self.sb([128, 16, 512], BF16, "hT")
        hdeps = [Dep() for _ in range(4)]
        uT = self.sb([128, 16, 512], BF16, "uT")
        vr = [self.sb([128, D], BF16, "v") for _ in range(4)]
        wring = self.ring(2, [128, 16, 512], BF16, "wpan")
        bs_bc = self.sb([128, D], F32, "bs_bc")
        bv_bc = self.sb([128, D], F32, "bv_bc")
        b_u = self.sb([128, 16], F32, "b_u")
        vn_col = self.sb([128, 16], F32, "vn_col")
        WcT = self.sb([128, D], BF16, "WcT")
        WcSr = self.ring(2, [128, D], BF16, "WcS")
        triu = self.sb([128, 128], F32, "triu")
        xbr = self.ring(2, [128, 512], F32, "xb")
        v32r = self.ring(2, [128, 512], F32, "v32")
        t32r = self.ring(2, [128, 512], F32, "t32")
        xrr = self.ring(3, [128, 512], F32, "xr")
        xor_ = self.ring(3, [128, 512], F32, "xo")
        ssr = self.ring(4, [128, 8], F32, "ssv")
        junk = st["junk"]
        K.dma("sp", bs_bc[:, :], self.a_b_s[j].rearrange("g t -> (g t)").partition_broadcast(128), bs_bc, writes=[bs_bc])
        K.dma("sp", bv_bc[:, :], self.a_b_in[j, D:2 * D].partition_broadcast(128), bv_bc, writes=[bv_bc])
        K.dma("sp", b_u[:, :], self.a_b_in[j, 0:D].rearrange("(c p) -> p c", p=128), b_u, writes=[b_u], slow=True)
        K.dma("sp", vn_col[:, :], self.a_v_norm[j, :].rearrange("(c p) -> p c", p=128), vn_col, writes=[vn_col], slow=True)
        K.dma("sp", triu[:, :], self.consts[:, 256:384], triu, writes=[triu])
        mk = self.mark()
        wsraw = self.sb([128, 16, 128], F32, "wsraw")
        K.dma("sp", wsraw[:, :, :], self.a_w_s[j].rearrange("g t s -> t g s"), wsraw, writes=[wsraw])
        for g in range(16):
            bank = self.banks[self._tbank % 8]
            self._tbank += 1
            K.op("pe", lambda e, o=bank[:, 0:128], i=wsraw[:, g, :]: e.transpose(o, i, self.ident_f[:, :]),
                 reads=[wsraw, self.ident_f], writes=[bank])
            K.op("dve", lambda e, o=WcT[:, g * 128:(g + 1) * 128], i0=bank[:, 0:128], i1=triu[:, :]:
                 e.tensor_tensor(out=o, in0=i0, in1=i1, op=ALU.mult), reads=[bank, triu], pwrites=[WcT])
        self.release(mk)
        wiv = self.a_w_in[j].rearrange("(kc p) n -> p kc n", p=128)
        wov = self.a_w_out[j].rearrange("(kc p) n -> p kc n", p=128)
        for s in range(NSEQ):
            for tt in range(TBS // 4):
                gtb0 = s * 16 + tt * 4
                for tb in range(4):
                    self.norm_T(st, src, gtb0 + tb, hT, hdeps[tb], tb * 128)
                for p in range(4):
                    w = wring.next()
                    K.dma("pool", w[:, :, :], wiv[:, :, p * 512:(p + 1) * 512], w, writes=[w])
                    for fcl in range(4):
                        fc = p * 4 + fcl
                        bank = self.banks[self._tbank % 8]
                        self._tbank += 1
                        for kc in range(16):
                            K.mm(bank[:, :], w[:, kc, fcl * 128:(fcl + 1) * 128], hT[:, kc, :], start=(kc == 0), stop=(kc == 15),
                                 reads=[w] + hdeps, bank=bank)
                        K.op("act", lambda e, o=uT[:, fc, :], i=bank[:, :], b=b_u[:, fc:fc + 1]:
                             e.activation(out=o, in_=i, func=AF.Gelu_apprx_tanh, bias=b),
                             reads=[bank, b_u], pwrites=[uT])
                sss = [ssr.next() for _ in range(4)]
                for tb in range(4):
                    K.op("dve", lambda e, o=sss[tb][:, :]: e.memset(o, 0.0), writes=[sss[tb]])
                for p in range(4):
                    w = wring.next()
                    K.dma("pool", w[:, :, :], wiv[:, :, D + p * 512:D + (p + 1) * 512], w, writes=[w])
                    for tb in range(4):
                        bank = self.banks[self._tbank % 8]
                        self._tbank += 1
                        for kc in range(16):
                            K.mm(bank[:, :], hT[:, kc, tb * 128:(tb + 1) * 128], w[:, kc, :], start=(kc == 0), stop=(kc == 15),
                                 reads=[w, hdeps[tb]], bank=bank)
                        xb, v32 = xbr.next(), v32r.next()
                        K.op("dve", lambda e, o=xb[:, :], i0=bank[:, :], i1=bv_bc[:, p * 512:(p + 1) * 512]:
                             e.tensor_tensor(out=o, in0=i0, in1=i1, op=ALU.add), reads=[bank, bv_bc], writes=[xb])
                        K.op("act", lambda e, o=v32[:, :], i=xb[:, :]: e.activation(out=o, in_=i, func=AF.Gelu_apprx_tanh),
                             reads=[xb], writes=[v32])
                        K.op("act", lambda e, o=junk[:, 0:512], i=v32[:, :], a=sss[tb][:, p:p + 1]:
                             e.activation(out=o, in_=i, func=AF.Square, accum_out=a),
                             reads=[v32], writes=[junk], pwrites=[sss[tb]])
                        K.op("dve", lambda e, o=vr[tb][:, p * 512:(p + 1) * 512], i=v32[:, :]: e.tensor_copy(out=o, in_=i),
                             reads=[v32], pwrites=[vr[tb]])
                for tb in range(4):
                    ss = sss[tb]
                    K.op("dve", lambda e, o=ss[:, 4:5], i=ss[:, 0:4]: e.reduce_sum(out=o, in_=i, axis=AX.X),
                         reads=[ss], pwrites=[ss])
                    K.op("act", lambda e, o=ss[:, 5:6], i=ss[:, 4:5]: e.activation(out=o, in_=i, func=AF.Sqrt,
                                                                                 bias=self.epsT[:, 0:1], scale=1.0 / D),
                         reads=[ss, self.epsT], pwrites=[ss])
                    K.op("dve", lambda e, o=ss[:, 6:7], i=ss[:, 5:6]: e.reciprocal(out=o, in_=i), reads=[ss], pwrites=[ss])
                    WcS = WcSr.next()
                    K.op("dve", lambda e, o=WcS[:, :], i=WcT[:, :], a=ss[:, 6:7]: e.tensor_scalar(out=o, in0=i, scalar1=a, scalar2=None, op0=ALU.mult),
                         reads=[WcT, ss], writes=[WcS])
                    for gq in range(4):
                        bank = self.banks[self._tbank % 8]
                        self._tbank += 1
                        t32 = t32r.next()
                        for gl in range(4):
                            g = gq * 4 + gl
                            K.op("pe", lambda e, o=bank[:, gl * 128:(gl + 1) * 128], l=vr[tb][:, g * 128:(g + 1) * 128], r=WcS[:, g * 128:(g + 1) * 128]:
                                 e.matmul(o, l, r, start=True, stop=True),
                                 reads=[vr[tb], WcS], writes=[bank] if gl == 0 else (), pwrites=[bank] if gl else (), mark=(gl == 3))
                        for gl in range(4):
                            g = gq * 4 + gl
                            K.op("dve", lambda e, o=t32[:, gl * 128:(gl + 1) * 128], i0=bank[:, gl * 128:(gl + 1) * 128], a=vn_col[:, g:g + 1], i1=bs_bc[:, g * 128:(g + 1) * 128]:
                                 e.scalar_tensor_tensor(out=o, in0=i0, scalar=a, in1=i1, op0=ALU.mult, op1=ALU.add),
                                 reads=[bank, vn_col, bs_bc], writes=[t32] if gl == 0 else (), pwrites=[t32] if gl else ())
                        uv = uT[:, gq * 4:(gq + 1) * 4, tb * 128:(tb + 1) * 128]
                        K.op("dve", lambda e, o=uv, i0=uv, i1=t32[:, :].rearrange("p (g t) -> p g t", g=4):
                             e.tensor_tensor(out=o, in0=i0, in1=i1, op=ALU.mult), reads=[t32, uT], pwrites=[uT])
                self.out_proj_tile(wov, 16, uT, 4, gtb0, src, dst, wring, xrr, xor_)

    def mixer_b(self, li, src, dst):
        K = self.K
        TBS, NSEQ = self.TBS, self.NSEQ
        TS = TBS * 128
        NQB = TS // 512
        bS, bP, bO = self.banks[0:2], self.banks[2:4], self.banks[4:8]
        cnt = {"S": 0, "P": 0}

        def nbank(role):
            lst = bS if role == "S" else bP
            b = lst[cnt[role] % 2]
            cnt[role] += 1
            return b

        hT = self.sb([128, 16, TS], BF16, "hT")
        hdeps = [Dep() for _ in range(TBS)]
        qn_col = self.sb([128, 1], F32, "qn_col")
        kn_col = self.sb([128, 1], F32, "kn_col")
        negbf = self.sb([16, 1], F32, "negbf")
        ones_bf = self.sb([128, 128], BF16, "ones")
        negmask = self.sb([128, 128], F32, "negmask")
        K.dma("sp", qn_col[:, :], self.b_q_norm[0, :].rearrange("(p o) -> p o", o=1), qn_col, writes=[qn_col], slow=True)
        K.dma("sp", kn_col[:, :], self.b_k_norm[0, :].rearrange("(p o) -> p o", o=1), kn_col, writes=[kn_col], slow=True)
        K.dma("sp", negbf[:, :], self.b_b_f[0, :].rearrange("(p o) -> p o", o=1), negbf, writes=[negbf], slow=True)
        K.dma("sp", negmask[:, :], self.consts[:, 256:384], negmask, writes=[negmask])
        K.op("dve", lambda e: e.tensor_scalar(out=qn_col[:, :], in0=qn_col[:, :], scalar1=128.0 ** -0.5, scalar2=None, op0=ALU.mult),
             reads=[qn_col], writes=[qn_col])
        K.op("dve", lambda e: e.tensor_scalar(out=negbf[:, :], in0=negbf[:, :], scalar1=-1.0, scalar2=None, op0=ALU.mult),
             reads=[negbf], writes=[negbf])
        K.op("dve", lambda e: e.tensor_scalar(out=negmask[:, :], in0=negmask[:, :], scalar1=-1.0, scalar2=30000.0, op0=ALU.add, op1=ALU.mult),
             reads=[negmask], writes=[negmask])
        K.op("dve", lambda e: e.memset(ones_bf[:, :], 1.0), writes=[ones_bf])
        wiv = self.b_w_in[0].rearrange("(kc p) n -> p kc n", p=128)
        cs_dep = Dep()
        ot_deps = [Dep() for _ in range(NT // 512)]
        for s in range(NSEQ):
            mk = self.mark()
            st = self.norm_setup(self.norm_mix[li, :])
            for tb in range(TBS):
                self.norm_T(st, src, s * 16 + tb, hT, hdeps[tb], tb * 128)
            self.release(mk)
            mk = self.mark()
            wfl = self.sb([128, 16, 16], BF16, "wfl")
            K.dma("pool", wfl[:, :, :], wiv[:, :, 8192:8208], wfl, writes=[wfl])
            spt = self.sb([16, TS], F32, "spt")
            ct = self.sb([16, TS], F32, "ct")
            r1 = self.sb([16, TS], F32, "r1")
            parts = [self.sb([16, TS], BF16, "cp") for _ in range(6)]
            for tq in range(NQB):
                bank = nbank("P")
                for kc in range(16):
                    K.mm(bank[0:16, :], wfl[:, kc, :], hT[:, kc, tq * 512:(tq + 1) * 512], start=(kc == 0), stop=(kc == 15),
                         reads=[wfl] + hdeps[tq * 4:tq * 4 + 4], bank=bank)
                K.op("act", lambda e, o=spt[:, tq * 512:(tq + 1) * 512], i=bank[0:16, :]:
                     e.activation(out=o, in_=i, func=AF.Softplus, bias=negbf[:, 0:1], scale=-1.0),
                     reads=[bank, negbf], pwrites=[spt])
            K.op("dve", lambda e: e.tensor_scalar(out=spt[:, :], in0=spt[:, :], scalar1=-0.5, scalar2=None, op0=ALU.mult),
                 reads=[spt], writes=[spt])
            K.op("dve", lambda e: e.tensor_tensor_scan(out=ct[:, :], data0=spt[:, :], data1=spt[:, :], initial=0.0, op0=ALU.add, op1=ALU.add),
                 reads=[spt], writes=[ct])
            self.dump(ct, ct[:, :], 0, f32=True)
            cur = ct
            for i3 in range(3):
                p_ = parts[3 + i3]
                K.op("dve", lambda e, o=p_[:, :], i=cur[:, :]: e.tensor_copy(out=o, in_=i), reads=[cur], writes=[p_])
                K.op("dve", lambda e, o=parts[i3][:, :], i=p_[:, :]: e.tensor_scalar(out=o, in0=i, scalar1=-1.0, scalar2=None, op0=ALU.mult),
                     reads=[p_], writes=[parts[i3]])
                if i3 < 2:
                    K.op("dve", lambda e, o=r1[:, :], i0=cur[:, :], i1=p_[:, :]: e.tensor_tensor(out=o, in0=i0, in1=i1, op=ALU.subtract),
                         reads=[cur, p_], writes=[r1])
                    cur = r1
            for i6 in range(6):
                K.dma("sp", self.CS[i6, :, 0:TS], parts[i6][:, :], parts[i6], reads=[parts[i6]],
                      writes=[cs_dep] if i6 == 0 else (), pwrites=[cs_dep] if i6 else ())
            self.release(mk)
            mk = self.mark()
            whr = self.ring(2, [128, 4, 16, 128], BF16, "wh")
            LKr = self.ring(2, [6, TS], BF16, "LK")
            RQr = self.ring(2, [6, TS], BF16, "RQ")
            for t_ in LKr.tiles + RQr.tiles:
                K.op("dve", lambda e, o=t_[:, :]: e.memset(o, 1.0), writes=[t_])
            qTr = self.ring(2, [128, TS], BF16, "qT")
            kTr = self.ring(2, [128, TS], BF16, "kT")
            vaugr = self.ring(2, [128, TBS, 129], BF16, "vaug")
            for t_ in vaugr.tiles:
                K.op("dve", lambda e, o=t_[:, :, :]: e.memset(o, 1.0), writes=[t_])
            sgr = self.ring(2, [128, TBS, 128], BF16, "sg")
            PTr = self.ring(4, [128, 512], BF16, "PT")
            oThr = self.ring(2, [128, TS], BF16, "oTh")
            sqr = self.ring(4, [128, 512], BF16, "sq")
            sdr = self.ring(4, [128, 512], F32, "sd")
            rsr = self.ring(4, [128, 512], F32, "rs")
            dtr = self.ring(2, [128, 128], F32, "dtmp")
            recr = self.ring(4, [128, 1], F32, "rec")
            ogor = self.ring(2, [128, 128], BF16, "ogo")
            for h in range(16):
                wh = whr.next()
                for m in range(4):
                    K.dma("pool", wh[:, m, :, :], wiv[:, :, m * 2048 + h * 128:m * 2048 + (h + 1) * 128], wh,
                          writes=[wh] if m == 0 else (), pwrites=[wh] if m else ())
                LK, RQ = LKr.next(), RQr.next()
                K.dma("sp", LK[0:3, :], self.CS[0:3, h, 0:TS], LK, reads=[cs_dep], pwrites=[LK])
                K.dma("sp", RQ[3:6, :], self.CS[3:6, h, 0:TS], RQ, reads=[cs_dep], pwrites=[RQ])
                qT, kT = qTr.next(), kTr.next()

                def qk_chain(m, dT, col, bks, delay):
                    for _ in range(delay):
                        yield
                    for tq in range(NQB):
                        bank, bank2 = bks
                        for kc in range(16):
                            K.mm(bank[:, :], wh[:, m, kc, :], hT[:, kc, tq * 512:(tq + 1) * 512], start=(kc == 0), stop=(kc == 15),
                                 reads=[wh] + hdeps[tq * 4:tq * 4 + 4], bank=bank)
                        sq, sd, rs = sqr.next(), sdr.next(), rsr.next()
                        K.op("act", lambda e, o=sq[:, :], i=bank[:, :]: e.activation(out=o, in_=i, func=AF.Square),
                             reads=[bank], writes=[sq])
                        yield
                        K.mm(bank2[:, :], ones_bf[:, :], sq[:, :], start=True, stop=True, reads=[ones_bf, sq], bank=bank2)
                        K.op("act", lambda e, o=sd[:, :], i=bank2[:, :]: e.activation(out=o, in_=i, func=AF.Sqrt, bias=self.epsT[:, 0:1], scale=1.0 / 128),
                             reads=[bank2, self.epsT], writes=[sd])
                        yield
                        K.op("dve", lambda e, o=rs[:, :], i=sd[:, :]: e.reciprocal(out=o, in_=i), reads=[sd], writes=[rs])
                        K.op("dve", lambda e, o=dT[:, tq * 512:(tq + 1) * 512], i0=bank[:, :], a=col[:, 0:1], i1=rs[:, :]:
                             e.scalar_tensor_tensor(out=o, in0=i0, scalar=a, in1=i1, op0=ALU.mult, op1=ALU.mult),
                             reads=[bank, col, rs], pwrites=[dT])
                        yield
                live = [qk_chain(0, qT, qn_col, (bP[0], bP[1]), 0), qk_chain(1, kT, kn_col, (bO[0], bO[1]), 1)]
                while live:
                    for g_ in list(live):
                        try:
                            next(g_)
                        except StopIteration:
                            live.remove(g_)
                vaug, sg = vaugr.next(), sgr.next()
                vcnt = 0
                for (m, which) in ((2, "v"), (3, "g")):
                    for tb4 in range(TBS // 4):
                        bank = bO[2 + vcnt % 2]
                        vcnt += 1
                        for tl in range(4):
                            tb = tb4 * 4 + tl
                            for kc in range(16):
                                K.op("pe", lambda e, o=bank[:, tl * 128:(tl + 1) * 128], l=hT[:, kc, tb * 128:(tb + 1) * 128], r=wh[:, m, kc, :], st_=(kc == 0), sp_=(kc == 15):
                                     e.matmul(o, l, r, start=st_, stop=sp_),
                                     reads=[wh, hdeps[tb]], writes=[bank] if (tl == 0 and kc == 0) else (),
                                     pwrites=() if (tl == 0 and kc == 0) else [bank], mark=(tl == 3 and kc == 15))
                        bv = bank[:, :].rearrange("p (a b) -> p a b", a=4)
                        if which == "v":
                            K.op("act", lambda e, o=vaug[:, tb4 * 4:(tb4 + 1) * 4, 0:128], i=bv: e.copy(out=o, in_=i),
                                 reads=[bank], pwrites=[vaug])
                        else:
                            K.op("act", lambda e, o=sg[:, tb4 * 4:(tb4 + 1) * 4, :], i=bv: e.activation(out=o, in_=i, func=AF.Sigmoid),
                                 reads=[bank], pwrites=[sg])
                oTh = oThr.next()
                for qb in range(NQB):
                    def front(kb, qb=qb):
                        o_ = max(0, kb - 4 * qb)
                        c0 = o_ * 128
                        sbk = nbank("S")
                        K.mm(sbk[:, c0:512], kT[:, kb * 128:(kb + 1) * 128], qT[:, qb * 512 + c0:(qb + 1) * 512], start=True, stop=False,
                             reads=[kT, qT], bank=sbk)
                        K.mm(sbk[:, c0:512], LK[0:6, kb * 128:(kb + 1) * 128], RQ[0:6, qb * 512 + c0:(qb + 1) * 512], start=False, stop=True,
                             reads=[LK, RQ], bank=sbk)
                        PT = PTr.next()
                        if kb >= 4 * qb:
                            dt_ = dtr.next()
                            K.op("dve", lambda e, o=dt_[:, :], i0=sbk[:, c0:c0 + 128]: e.tensor_tensor(out=o, in0=i0, in1=negmask[:, :], op=ALU.add),
                                 reads=[sbk, negmask], writes=[dt_])
                            K.op("act", lambda e, o=PT[:, c0:c0 + 128], i=dt_[:, :]: e.activation(out=o, in_=i, func=AF.Exp),
                                 reads=[dt_], writes=[PT])
                            if c0 + 128 < 512:
                                K.op("act", lambda e, o=PT[:, c0 + 128:512], i=sbk[:, c0 + 128:512]: e.activation(out=o, in_=i, func=AF.Exp),
                                     reads=[sbk], pwrites=[PT])
                        else:
                            K.op("act", lambda e, o=PT[:, :], i=sbk[:, :]: e.activation(out=o, in_=i, func=AF.Exp),
                                 reads=[sbk], writes=[PT])
                        return PT

                    def back(kb, PT, qb=qb):
                        for qs in range(4):
                            if kb <= 4 * qb + qs:
                                K.mm(bO[qs][:, 0:129], PT[:, qs * 128:(qs + 1) * 128], vaug[:, kb, :], start=(kb == 0), stop=(kb == 4 * qb + qs),
                                     reads=[PT, vaug], bank=bO[qs])
                    pend = None
                    for kb in range(4 * qb + 4):
                        PT = front(kb)
                        if pend is not None:
                            back(*pend)
                        pend = (kb, PT)
                    back(*pend)
                    for qs in range(4):
                        tb = 4 * qb + qs
                        rec, ogo = recr.next(), ogor.next()
                        K.op("dve", lambda e, o=rec[:, :], i=bO[qs][:, 128:129]: e.reciprocal(out=o, in_=i), reads=[bO[qs]], writes=[rec])
                        K.op("dve", lambda e, o=ogo[:, :], i0=bO[qs][:, 0:128], a=rec[:, 0:1], i1=sg[:, tb, :]:
                             e.scalar_tensor_tensor(out=o, in0=i0, scalar=a, in1=i1, op0=ALU.mult, op1=ALU.mult),
                             reads=[bO[qs], rec, sg], writes=[ogo])
                        if h == 0 and s == 0 and tb == 0:
                            self.dump(ogo, ogo[:, :], 3)
                        bank = nbank("P")
                        bvw = bank.t[:, :].bitcast(BF16)
                        K.op("pe", lambda e, o=bvw[:, 0:128], i=ogo[:, :]: e.transpose(o, i, self.ident[:, :]),
                             reads=[ogo, self.ident], writes=[bank])
                        K.op("act", lambda e, o=oTh[:, tb * 128:(tb + 1) * 128], i=bvw[:, 0:128]: e.copy(out=o, in_=i),
                             reads=[bank], pwrites=[oTh])
                g0 = s * SEQ
                K.dma("sp", self.OT[h * 128:(h + 1) * 128, g0:g0 + TS], oTh[:, :], oTh, reads=[oTh],
                      pwrites=ot_deps[g0 // 512:(g0 + TS) // 512])
            self.release(mk)
        mk = self.mark()
        wring = self.ring(2, [128, 16, 512], BF16, "wpan")
        uTr = self.ring(2, [128, 16, 512], BF16, "uT")
        xrr = self.ring(3, [128, 512], F32, "xr")
        xor_ = self.ring(3, [128, 512], F32, "xo")
        wov = self.b_w_out[0].rearrange("(kc p) n -> p kc n", p=128)
        otv = self.OT.rearrange("(kc p) t -> p kc t", p=128)
        for s in range(NSEQ):
            for tt in range(TBS // 4):
                g0 = s * SEQ + tt * 512
                uT = uTr.next()
                K.dma("sp", uT[:, :, :], otv[:, 0:16, g0:g0 + 512], uT, reads=[ot_deps[g0 // 512]], writes=[uT])
                self.out_proj_tile(wov, 16, uT, 4, g0 // 128, src, dst, wring, xrr, xor_)
        self.release(mk)

    def mixer_c(self, li, src, dst):
        K = self.K
        TBS, NSEQ = self.TBS, self.NSEQ
        TS = TBS * 128
        NQB = TS // 512
        bP = self.banks[0:2]
        cnt = {"P": 0, "Q": 0, "P2": 0}

        def nbank():
            b = bP[cnt["P"] % 2]
            cnt["P"] += 1
            return b

        qdeps = [[Dep() for _ in range(4)] for _ in range(8)]

        def nq():
            if QMODE == 1:
                i = cnt["Q"] % 6
                cnt["Q"] += 1
                return self.banks[2 + i].t[:, 0:128], qdeps[2 + i][0]
            i = cnt["Q"] % 24
            cnt["Q"] += 1
            b, q = 2 + i // 4, i % 4
            return self.banks[b].t[:, q * 128:(q + 1) * 128], qdeps[b][q]

        wiv = self.c_w_in[0].rearrange("(kc p) n -> p kc n", p=128)
        mk_top = self.mark()
        hT = self.sb([128, 16, TS], BF16, "hT")
        hdeps = [Dep() for _ in range(TBS)]
        ones_bf = self.sb([128, 128], BF16, "ones")
        ones_f = self.sb([128, 128], F32, "onesf")
        triu = self.sb([128, 128], F32, "triu")
        nmask = self.sb([128, 128], F32, "nmask")
        strict = self.sb([128, 128], F32, "strict")
        onw_bc = self.sb([128, 128], F32, "onw")
        dtb_bc = self.sb([128, 32], F32, "dtb")
        nea_bc = self.sb([128, 32], F32, "nea")
        cwc = self.sb([128, 4, 64], F32, "cwc")
        K.op("dve", lambda e: e.memset(ones_bf[:, :], 1.0), writes=[ones_bf])
        K.op("dve", lambda e: e.memset(ones_f[:, :], 1.0), writes=[ones_f])
        K.dma("sp", triu[:, :], self.consts[:, 256:384], triu, writes=[triu])
        K.op("dve", lambda e: e.tensor_scalar(out=nmask[:, :], in0=triu[:, :], scalar1=-1.0, scalar2=30000.0, op0=ALU.add, op1=ALU.mult),
             reads=[triu], writes=[nmask])
        K.op("dve", lambda e: e.tensor_tensor(out=strict[:, :], in0=triu[:, :], in1=self.ident_f[:, :], op=ALU.subtract),
             reads=[triu, self.ident_f], writes=[strict])
        K.dma("sp", onw_bc[:, :], self.c_out_norm[0, :].partition_broadcast(128), onw_bc, writes=[onw_bc])
        K.dma("sp", dtb_bc[:, :], self.c_dt_bias[0, :].partition_broadcast(128), dtb_bc, writes=[dtb_bc])
        K.dma("sp", nea_bc[:, :], self.c_a_log[0, :].partition_broadcast(128), nea_bc, writes=[nea_bc])
        K.op("act", lambda e: e.activation(out=nea_bc[:, :], in_=nea_bc[:, :], func=AF.Exp), reads=[nea_bc], writes=[nea_bc])
        K.op("dve", lambda e: e.tensor_scalar(out=nea_bc[:, :], in0=nea_bc[:, :], scalar1=-1.0, scalar2=None, op0=ALU.mult),
             reads=[nea_bc], writes=[nea_bc])
        mkc = self.mark()
        tmpw = self.sb([64, 4, 128], F32, "tmpw")
        K.dma("sp", tmpw[:, :, :], self.c_conv_w[0].rearrange("k (c p) -> c k p", p=128), tmpw, writes=[tmpw])
        for k in range(4):
            bank = nbank()
            K.op("pe", lambda e, o=bank[:, 0:64], i=tmpw[:, k, :]: e.transpose(o, i, self.ident_f[0:64, 0:64]),
                 reads=[tmpw, self.ident_f], writes=[bank])
            K.op("act", lambda e, o=cwc[:, k, :], i=bank[:, 0:64]: e.copy(out=o, in_=i), reads=[bank], pwrites=[cwc])
        self.release(mkc)
        ot_deps = [Dep() for _ in range(NT // 512)]
        for s in range(NSEQ):
            mk = self.mark()
            st = self.norm_setup(self.norm_mix[li, :])
            for tb in range(TBS):
                self.norm_T(st, src, s * 16 + tb, hT, hdeps[tb], tb * 128)
            self.release(mk)
            mk = self.mark()
            if CSTOP <= 1:
                self.release(mk)
                continue
            names = ("bet", "gcs", "egc", "ekd", "egl")
            S_ = {n: self.sb([128, TBS, 32], F32, n) for n in names}
            tmp32 = self.ring(2, [128, 32], F32, "tmp32")
            g32 = self.ring(2, [128, 32], F32, "g32")
            mkW = self.mark()
            wab = self.sb([128, 16, 64], BF16, "wab")
            K.dma("pool", wab[:, :, :], wiv[:, :, 12288:12352], wab, writes=[wab])
            for tb in range(TBS):
                bank = nbank()
                for kc in range(16):
                    K.mm(bank[:, 0:64], hT[:, kc, tb * 128:(tb + 1) * 128], wab[:, kc, :], start=(kc == 0), stop=(kc == 15),
                         reads=[wab, hdeps[tb]], bank=bank)
                K.op("act", lambda e, o=S_["bet"][:, tb, :], i=bank[:, 0:32]: e.activation(out=o, in_=i, func=AF.Sigmoid),
                     reads=[bank], pwrites=[S_["bet"]])
                t_ = tmp32.next()
                K.op("dve", lambda e, o=t_[:, :], i0=bank[:, 32:64]: e.tensor_tensor(out=o, in0=i0, in1=dtb_bc[:, :], op=ALU.add),
                     reads=[bank, dtb_bc], writes=[t_])
                K.op("act", lambda e, o=t_[:, :]: e.activation(out=o, in_=o, func=AF.Softplus), reads=[t_], writes=[t_])
                gt = g32.next()
                K.op("dve", lambda e, o=gt[:, :], i0=t_[:, :]: e.tensor_tensor(out=o, in0=i0, in1=nea_bc[:, :], op=ALU.mult),
                     reads=[t_, nea_bc], writes=[gt])
                bank2 = nbank()
                K.op("pe", lambda e, o=bank2[:, 0:32], r=gt[:, :]: e.matmul(o, triu[:, :], r, start=True, stop=True),
                     reads=[triu, gt], writes=[bank2], mark=False)
                K.op("pe", lambda e, o=bank2[:, 32:64], r=gt[:, :]: e.matmul(o, ones_f[:, :], r, start=True, stop=True),
                     reads=[ones_f, gt], pwrites=[bank2])
                K.op("act", lambda e, o=S_["gcs"][:, tb, :], i=bank2[:, 0:32]: e.copy(out=o, in_=i), reads=[bank2], pwrites=[S_["gcs"]])
                K.op("act", lambda e, o=S_["egc"][:, tb, :], i=bank2[:, 0:32]: e.activation(out=o, in_=i, func=AF.Exp), reads=[bank2], pwrites=[S_["egc"]])
                K.op("act", lambda e, o=S_["egl"][:, tb, :], i=bank2[:, 32:64]: e.activation(out=o, in_=i, func=AF.Exp), reads=[bank2], pwrites=[S_["egl"]])
                t2 = tmp32.next()
                K.op("dve", lambda e, o=t2[:, :], i0=bank2[:, 32:64], i1=S_["gcs"][:, tb, :]: e.tensor_tensor(out=o, in0=i0, in1=i1, op=ALU.subtract),
                     reads=[bank2, S_["gcs"]], writes=[t2])
                K.op("act", lambda e, o=S_["ekd"][:, tb, :], i=t2[:, :]: e.activation(out=o, in_=i, func=AF.Exp), reads=[t2], pwrites=[S_["ekd"]])
            self.release(mkW)
            if CSTOP <= 2:
                self.release(mk)
                continue
            wh = self.sb([128, 16, 768], BF16, "wh")

            def load_wh(kh):
                for (c0, c1, w0) in ((kh * 128, kh * 128 + 128, 0), (2048 + kh * 128, 2048 + kh * 128 + 128, 128),
                                     (4096 + kh * 256, 4096 + kh * 256 + 256, 256), (8192 + kh * 256, 8192 + kh * 256 + 256, 512)):
                    K.dma("pool", wh[:, :, w0:w0 + (c1 - c0)], wiv[:, :, c0:c1], wh,
                          writes=[wh] if w0 == 0 else (), pwrites=[wh] if w0 else ())
            load_wh(0)
            qT = self.sb([128, TS], BF16, "qT")
            kT = self.sb([128, TS], BF16, "kT")
            ktm = self.sb([128, TBS, 128], BF16, "ktm")
            vtm = self.sb([128, TBS, 256], BF16, "vtm")
            zs = self.sb([128, TBS, 256], BF16, "zs")
            S32 = self.sb([128, 2, 128], F32, "S32")
            Sbr = self.ring(2, [128, 2, 128], BF16, "Sb")
            oall = self.sb([128, TBS, 2, 128], BF16, "oall")
            ssq = self.sb([128, TBS * 2 + 64], F32, "ssq")
            stgr = self.ring(1, [128, 1024], BF16, "stg")
            junkb = self.sb([128, 128], BF16, "junkb")
            WS = [128, 2, 2, 128]
            vnr = self.ring(2, [128, 2, 128], BF16, "vnew")
            o1t = self.sb([128, 2, 128], F32, "o1")
            ot = self.sb([128, 2, 128], F32, "o")
            cbanks = self.banks[2:8]

            def nqb():
                b = self.banks[cnt["Q"] % 5]
                cnt["Q"] += 1
                return b

            def nqb2():
                b = self.banks[5 + cnt["P2"] % 3]
                cnt["P2"] += 1
                return b

            def bc4(ap3):
                return ap3.unsqueeze(3).to_broadcast(WS)

            def bcj(ap3):
                return ap3.unsqueeze(2).to_broadcast(WS)

            def bcall(ap2):
                return ap2.unsqueeze(1).unsqueeze(1).to_broadcast(WS)

            def pe_quads(bank, fn):
                for ci in range(4):
                    emit, reads = fn(ci, bank.t[:, ci * 128:(ci + 1) * 128])
                    K.op("pe", emit, reads=reads, writes=[bank] if ci == 0 else (), pwrites=[bank] if ci else (), mark=(ci == 3))

            def w4(bank):
                return bank.t[:, :].rearrange("p (a b c) -> p a b c", a=2, b=2)

            for kh in range(KHN):
                mkP = self.mark()
                def abank():
                    b = self.banks[cnt["P"] % 8]
                    cnt["P"] += 1
                    return b

                def proj_chain(ci, w0, cch, delay):
                    for _ in range(delay):
                        yield
                    gs2 = [self.sb([128, 515], F32, "gs") for _ in range(2)]
                    t1 = self.sb([128, 512], F32, "t1")
                    sg = self.sb([128, 512], F32 if ci < 2 else BF16, "sg")
                    if ci < 2:
                        sq = self.sb([128, 512], BF16, "sq")
                        sd = self.sb([128, 512], F32, "sd")
                        rs = self.sb([128, 512], F32, "rs")
                    prev_gs = None
                    for tq in range(NQB):
                        bank = abank()
                        for kc in range(16):
                            K.mm(bank[:, :], wh[:, kc, w0:w0 + 128], hT[:, kc, tq * 512:(tq + 1) * 512], start=(kc == 0), stop=(kc == 15),
                                 reads=[wh] + hdeps[tq * 4:tq * 4 + 4], bank=bank)
                        gs = gs2[tq % 2]
                        K.op("act", lambda e, o=gs[:, 3:515], i=bank[:, :]: e.copy(out=o, in_=i), reads=[bank], writes=[gs])
                        if tq == 0:
                            K.op("dve", lambda e, o=gs[:, 0:3]: e.memset(o, 0.0), pwrites=[gs])
                        else:
                            K.op("dve", lambda e, o=gs[:, 0:3], i=prev_gs[:, 512:515]: e.tensor_copy(out=o, in_=i), reads=[prev_gs], pwrites=[gs])
                        prev_gs = gs
                        yield
                        K.op("dve", lambda e, o=t1[:, :], i=gs[:, 3:515], a=cwc[:, 3, cch:cch + 1]: e.tensor_scalar(out=o, in0=i, scalar1=a, scalar2=None, op0=ALU.mult),
                             reads=[gs, cwc], writes=[t1])
                        for k in (2, 1, 0):
                            K.op("dve", lambda e, o=t1[:, :], i=gs[:, k:k + 512], a=cwc[:, k, cch:cch + 1]:
                                 e.scalar_tensor_tensor(out=o, in0=i, scalar=a, in1=o, op0=ALU.mult, op1=ALU.add),
                                 reads=[gs, cwc], writes=[t1])
                        yield
                        if ci < 2:
                            dT = qT if ci == 0 else kT
                            K.op("act", lambda e, o=sg[:, :], i=t1[:, :]: e.activation(out=o, in_=i, func=AF.Silu), reads=[t1], writes=[sg])
                            K.op("act", lambda e, o=sq[:, :], i=sg[:, :]: e.activation(out=o, in_=i, func=AF.Square), reads=[sg], writes=[sq])
                            bank2 = abank()
                            K.mm(bank2[:, :], ones_bf[:, :], sq[:, :], start=True, stop=True, reads=[ones_bf, sq], bank=bank2)
                            K.op("act", lambda e, o=sd[:, :], i=bank2[:, :]: e.activation(out=o, in_=i, func=AF.Sqrt, bias=self.epsT[:, 0:1], scale=1.0),
                                 reads=[bank2, self.epsT], writes=[sd])
                            yield
                            K.op("dve", lambda e, o=rs[:, :], i=sd[:, :]: e.reciprocal(out=o, in_=i), reads=[sd], writes=[rs])
                            sc = (128.0 ** -0.5) if ci == 0 else 1.0
                            K.op("dve", lambda e, o=dT[:, tq * 512:(tq + 1) * 512], i0=sg[:, :], i1=rs[:, :], sc=sc:
                                 e.scalar_tensor_tensor(out=o, in0=i0, scalar=sc, in1=i1, op0=ALU.mult, op1=ALU.mult),
                                 reads=[sg, rs], pwrites=[dT])
                            if ci == 1:
                                bank3 = abank()
                                bvw = bank3.t[:, :].bitcast(BF16)
                                for tl in range(4):
                                    K.op("pe", lambda e, o=bvw[:, tl * 128:(tl + 1) * 128], i=kT[:, tq * 512 + tl * 128:tq * 512 + (tl + 1) * 128]:
                                         e.transpose(o, i, self.ident[:, :]), reads=[kT, self.ident],
                                         writes=[bank3] if tl == 0 else (), pwrites=[bank3] if tl else (), mark=(tl == 3))
                                K.op("act", lambda e, o=ktm[:, tq * 4:(tq + 1) * 4, :], i=bvw[:, 0:512].rearrange("p (a b) -> p a b", a=4): e.copy(out=o, in_=i),
                                     reads=[bank3], pwrites=[ktm])
                            yield
                        else:
                            K.op("act", lambda e, o=sg[:, :], i=t1[:, :]: e.activation(out=o, in_=i, func=AF.Silu), reads=[t1], writes=[sg])
                            bank3 = abank()
                            bvw = bank3.t[:, :].bitcast(BF16)
                            for tl in range(4):
                                K.op("pe", lambda e, o=bvw[:, tl * 128:(tl + 1) * 128], i=sg[:, tl * 128:(tl + 1) * 128]:
                                     e.transpose(o, i, self.ident[:, :]), reads=[sg, self.ident],
                                     writes=[bank3] if tl == 0 else (), pwrites=[bank3] if tl else (), mark=(tl == 3))
                            j = ci - 2
                            K.op("act", lambda e, o=vtm[:, tq * 4:(tq + 1) * 4, j * 128:(j + 1) * 128], i=bvw[:, 0:512].rearrange("p (a b) -> p a b", a=4): e.copy(out=o, in_=i),
                                 reads=[bank3], pwrites=[vtm])
                            yield
                            yield

                def z_chain():
                    for tb in range(TBS):
                        bank = abank()
                        for kc in range(16):
                            K.mm(bank[:, 0:256], hT[:, kc, tb * 128:(tb + 1) * 128], wh[:, kc, 512:768], start=(kc == 0), stop=(kc == 15),
                                 reads=[wh, hdeps[tb]], bank=bank)
                        K.op("act", lambda e, o=zs[:, tb, :], i=bank[:, 0:256]: e.activation(out=o, in_=i, func=AF.Silu), reads=[bank], pwrites=[zs])
                        K.op("dve", lambda e, o=zs[:, tb, :].rearrange("p (a b) -> p a b", a=2), i1=onw_bc[:, :].unsqueeze(1).to_broadcast([128, 2, 128]):
                             e.tensor_tensor(out=o, in0=o, in1=i1, op=ALU.mult), reads=[onw_bc], pwrites=[zs])
                        yield

                chains = [proj_chain(ci, w0, cch, ci) for ci, (w0, cch) in
                          enumerate(((0, kh), (128, 16 + kh), (256, 32 + 2 * kh), (384, 32 + 2 * kh + 1)))]
                chains.append(z_chain())
                live = list(chains)
                while live:
                    for g_ in list(live):
                        try:
                            next(g_)
                        except StopIteration:
                            live.remove(g_)
                self.release(mkP)
                if kh + 1 < KHN:
                    load_wh(kh + 1)
                if CSTOP <= 3:
                    continue
                mkR = self.mark()
                P1s = []
                for _ in range(2):
                    d_ = {n: self.sb(WS, F32, n) for n in ("GG", "dg", "tmp", "X", "XT", "Ra", "Rb", "Pa", "PTa")}
                    d_["Pb"], d_["PTb"] = d_["dg"], d_["GG"]
                    d_.update({n: self.sb(WS, BF16, n) for n in ("Rbf", "ke")})
                    P1s.append(d_)
                P2 = [dict(attnT=self.sb(WS, BF16, "attnT"), ub=self.sb(WS, F32, "ub"), wTb=self.sb(WS, BF16, "wTb"),
                           kd=self.sb(WS, BF16, "kd")) for _ in range(4)]
                hv0 = 2 * kh
                K.op("dve", lambda e: e.memset(ssq[:, :], 0.0), writes=[ssq])
                K.op("dve", lambda e: e.memset(S32[:, :, :], 0.0), writes=[S32])
                Sb0 = Sbr.next()
                K.op("dve", lambda e, o=Sb0[:, :, :]: e.memset(o, 0.0), writes=[Sb0])
                sbcur = [Sb0]

                def phase1(g, hv0=hv0):
                    tb0 = 2 * g
                    t = P1s[g % 2]
                    q = P2[g % 4]

                    def sv(name):
                        return S_[name][:, tb0:tb0 + 2, hv0:hv0 + 2]
                    bk = nqb()
                    for tbl in range(2):
                        ksl = slice((tb0 + tbl) * 128, (tb0 + tbl + 1) * 128)
                        K.op("pe", lambda e, o=bk.t[:, tbl * 128:(tbl + 1) * 128], l=kT[:, ksl]: e.matmul(o, l, l, start=True, stop=True),
                             reads=[kT], writes=[bk] if tbl == 0 else (), pwrites=[bk] if tbl else (), mark=False)
                    for tbl in range(2):
                        ksl = slice((tb0 + tbl) * 128, (tb0 + tbl + 1) * 128)
                        K.op("pe", lambda e, o=bk.t[:, (2 + tbl) * 128:(3 + tbl) * 128], l=kT[:, ksl], r=qT[:, ksl]: e.matmul(o, l, r, start=True, stop=True),
                             reads=[kT, qT], pwrites=[bk], mark=(tbl == 1))
                    K.op("act", lambda e, o=t["GG"][:, :, :, :], i=w4(bk): e.copy(out=o, in_=i), reads=[bk], writes=[t["GG"]])
                    yield
                    K.op("dve", lambda e, o=t["GG"][:, 0, :, :]: e.tensor_tensor(out=o, in0=o, in1=strict[:, :].unsqueeze(1).to_broadcast([128, 2, 128]), op=ALU.mult),
                         reads=[strict], writes=[t["GG"]])
                    K.op("dve", lambda e, o=t["dg"][:, :, :, :], i1=bc4(sv("gcs")): e.tensor_tensor(out=o, in0=bcall(self.ident_f[:, :]), in1=i1, op=ALU.mult),
                         reads=[self.ident_f, S_["gcs"]], writes=[t["dg"]])
                    bk2 = nqb()
                    pe_quads(bk2, lambda ci, o: ((lambda e, o=o, r=t["dg"][:, ci // 2, ci % 2, :]: e.matmul(o, ones_f[:, :], r, start=True, stop=True)),
                                                 [ones_f, t["dg"]]))
                    K.op("dve", lambda e, o=t["tmp"][:, :, :, :], i0=w4(bk2): e.tensor_tensor(out=o, in0=i0, in1=bcall(nmask[:, :]), op=ALU.add),
                         reads=[bk2, nmask], writes=[t["tmp"]])
                    K.op("dve", lambda e, o=t["tmp"][:, :, :, :], i1=bc4(sv("gcs")): e.tensor_tensor(out=o, in0=o, in1=i1, op=ALU.subtract),
                         reads=[S_["gcs"]], writes=[t["tmp"]])
                    K.op("act", lambda e, o=t["tmp"][:, :, :, :]: e.activation(out=o, in_=o, func=AF.Exp), reads=[], writes=[t["tmp"]])
                    yield
                    E = t["tmp"]
                    K.op("dve", lambda e, o=q["attnT"][:, :, :, :], i0=bcj(t["GG"][:, 1, :, :]), i1=E[:, :, :, :]: e.tensor_tensor(out=o, in0=i0, in1=i1, op=ALU.mult),
                         reads=[t["GG"], E], writes=[q["attnT"]])
                    K.op("dve", lambda e, o=t["X"][:, :, :, :], i0=bcj(t["GG"][:, 0, :, :]), i1=E[:, :, :, :]: e.tensor_tensor(out=o, in0=i0, in1=i1, op=ALU.mult),
                         reads=[t["GG"], E], writes=[t["X"]])
                    K.op("dve", lambda e, o=t["X"][:, :, :, :], i1=bc4(sv("bet")): e.tensor_tensor(out=o, in0=o, in1=i1, op=ALU.mult),
                         reads=[S_["bet"]], writes=[t["X"]])
                    bk3 = nqb()
                    pe_quads(bk3, lambda ci, o: ((lambda e, o=o, i=t["X"][:, ci // 2, ci % 2, :]: e.transpose(o, i, self.ident_f[:, :])),
                                                 [t["X"], self.ident_f]))
                    K.op("act", lambda e, o=t["XT"][:, :, :, :], i=w4(bk3): e.copy(out=o, in_=i), reads=[bk3], writes=[t["XT"]])
                    K.op("dve", lambda e, o=t["Ra"][:, :, :, :], i1=t["X"][:, :, :, :]: e.tensor_tensor(out=o, in0=bcall(self.ident_f[:, :]), in1=i1, op=ALU.subtract),
                         reads=[self.ident_f, t["X"]], writes=[t["Ra"]])
                    yield
                    P, PT, R = t["X"], t["XT"], t["Ra"]

                    def square(P, PT, Pn, PTn, need_p):
                        bkT = nqb()
                        pe_quads(bkT, lambda ci, o: ((lambda e, o=o, l=P[:, ci // 2, ci % 2, :], r=PT[:, ci // 2, ci % 2, :]: e.matmul(o, l, r, start=True, stop=True)),
                                                     [P, PT]))
                        bkP = None
                        if need_p:
                            bkP = nqb()
                            pe_quads(bkP, lambda ci, o: ((lambda e, o=o, l=PT[:, ci // 2, ci % 2, :], r=P[:, ci // 2, ci % 2, :]: e.matmul(o, l, r, start=True, stop=True)),
                                                         [P, PT]))
                        return bkT, bkP

                    def evac(bkT, bkP, Pn, PTn):
                        K.op("act", lambda e, o=PTn[:, :, :, :], i=w4(bkT): e.copy(out=o, in_=i), reads=[bkT], writes=[PTn])
                        if bkP is not None:
                            K.op("act", lambda e, o=Pn[:, :, :, :], i=w4(bkP): e.copy(out=o, in_=i), reads=[bkP], writes=[Pn])
                    bkT, bkP = square(P, PT, t["Pa"], t["PTa"], True)
                    evac(bkT, bkP, t["Pa"], t["PTa"])
                    P, PT = t["Pa"], t["PTa"]
                    yield
                    for it in range(6):
                        Pn = t["Pb"] if it % 2 == 0 else t["Pa"]
                        PTn = t["PTb"] if it % 2 == 0 else t["PTa"]
                        Rn = t["Rb"] if it % 2 == 0 else t["Ra"]
                        bkR = nqb()
                        pe_quads(bkR, lambda ci, o: ((lambda e, o=o, l=PT[:, ci // 2, ci % 2, :], r=R[:, ci // 2, ci % 2, :]: e.matmul(o, l, r, start=True, stop=True)),
                                                     [PT, R]))
                        if it < 5:
                            bkT, bkP = square(P, PT, Pn, PTn, it < 4)
                        K.op("dve", lambda e, o=Rn[:, :, :, :], i0=w4(bkR), i1=R[:, :, :, :]: e.tensor_tensor(out=o, in0=i0, in1=i1, op=ALU.add),
                             reads=[bkR, R], writes=[Rn])
                        if it < 5:
                            evac(bkT, bkP, Pn, PTn)
                        P, PT, R = Pn, PTn, Rn
                        yield
                    K.op("act", lambda e, o=t["Rbf"][:, :, :, :], i=R[:, :, :, :]: e.copy(out=o, in_=i), reads=[R], writes=[t["Rbf"]])
                    K.op("dve", lambda e, o=t["ke"][:, :, :, :], i0=bcj(ktm[:, tb0:tb0 + 2, :]), i1=bc4(sv("egc")): e.tensor_tensor(out=o, in0=i0, in1=i1, op=ALU.mult),
                         reads=[ktm, S_["egc"]], writes=[t["ke"]])
                    K.op("dve", lambda e, o=q["kd"][:, :, :, :], i0=bcj(ktm[:, tb0:tb0 + 2, :]), i1=bc4(sv("ekd")): e.tensor_tensor(out=o, in0=i0, in1=i1, op=ALU.mult),
                         reads=[ktm, S_["ekd"]], writes=[q["kd"]])
                    yield
                    bkU = nqb()
                    pe_quads(bkU, lambda ci, o: ((lambda e, o=o, l=t["Rbf"][:, ci // 2, ci % 2, :], r=vtm[:, tb0 + ci // 2, (ci % 2) * 128:(ci % 2 + 1) * 128]:
                                                  e.matmul(o, l, r, start=True, stop=True)), [t["Rbf"], vtm]))
                    bkW = nqb()
                    pe_quads(bkW, lambda ci, o: ((lambda e, o=o, l=t["ke"][:, ci // 2, ci % 2, :], r=t["Rbf"][:, ci // 2, ci % 2, :]:
                                                  e.matmul(o, l, r, start=True, stop=True)), [t["Rbf"], t["ke"]]))
                    K.op("dve", lambda e, o=q["ub"][:, :, :, :], i0=w4(bkU), i1=bc4(sv("bet")): e.tensor_tensor(out=o, in0=i0, in1=i1, op=ALU.mult),
                         reads=[bkU, S_["bet"]], writes=[q["ub"]])
                    K.op("act", lambda e, o=q["wTb"][:, :, :, :], i=w4(bkW): e.copy(out=o, in_=i), reads=[bkW], writes=[q["wTb"]])
                    yield

                def phase2(g, hv0=hv0):
                    q = P2[g % 4]
                    for tbl in range(2):
                        tb = 2 * g + tbl
                        ksl = slice(tb * 128, (tb + 1) * 128)

                        def col(name, j, tb=tb):
                            return S_[name][:, tb, hv0 + j:hv0 + j + 1]
                        Sold = sbcur[0]
                        bkV = nqb2()
                        for j in range(2):
                            K.op("pe", lambda e, o=bkV.t[:, j * 128:(j + 1) * 128], l=q["wTb"][:, tbl, j, :], r=Sold[:, j, :]: e.matmul(o, l, r, start=True, stop=True),
                                 reads=[q["wTb"], Sold], writes=[bkV] if j == 0 else (), pwrites=[bkV] if j else (), mark=(j == 1))
                        vnew = vnr.next()
                        for j in range(2):
                            K.op("dve", lambda e, o=vnew[:, j, :], i0=bkV.t[:, j * 128:(j + 1) * 128], a=col("bet", j), i1=q["ub"][:, tbl, j, :]:
                                 e.scalar_tensor_tensor(out=o, in0=i0, scalar=a, in1=i1, op0=ALU.mult, op1=ALU.subtract),
                                 reads=[bkV, S_["bet"], q["ub"]], writes=[vnew] if j == 0 else (), pwrites=[vnew] if j else ())
                        bkS = bkV
                        for j in range(2):
                            K.op("pe", lambda e, o=bkS.t[:, j * 128:(j + 1) * 128], l=q["kd"][:, tbl, j, :], r=vnew[:, j, :]: e.matmul(o, l, r, start=True, stop=True),
                                 reads=[q["kd"], vnew], writes=[bkS] if j == 0 else (), pwrites=[bkS] if j else (), mark=(j == 1))
                        bkO = nqb2()
                        for j in range(2):
                            K.op("pe", lambda e, o=bkO.t[:, j * 128:(j + 1) * 128], l=qT[:, ksl], r=Sold[:, j, :]: e.matmul(o, l, r, start=True, stop=True),
                                 reads=[qT, Sold], writes=[bkO] if j == 0 else (), pwrites=[bkO] if j else (), mark=False)
                        for j in range(2):
                            K.op("pe", lambda e, o=bkO.t[:, (2 + j) * 128:(3 + j) * 128], l=q["attnT"][:, tbl, j, :], r=vnew[:, j, :]: e.matmul(o, l, r, start=True, stop=True),
                                 reads=[q["attnT"], vnew], pwrites=[bkO], mark=(j == 1))
                        for j in range(2):
                            K.op("dve", lambda e, o=S32[:, j, :], i1=bkS.t[:, j * 128:(j + 1) * 128], a=col("egl", j):
                                 e.scalar_tensor_tensor(out=o, in0=o, scalar=a, in1=i1, op0=ALU.mult, op1=ALU.subtract),
                                 reads=[bkS, S_["egl"]], writes=[S32])
                        Snew = Sbr.next()
                        K.op("act", lambda e, o=Snew[:, :, :], i=S32[:, :, :]: e.copy(out=o, in_=i), reads=[S32], writes=[Snew])
                        sbcur[0] = Snew
                        yield
                        for j in range(2):
                            K.op("act", lambda e, o=o1t[:, j, :], i=bkO.t[:, j * 128:(j + 1) * 128], a=col("egc", j): e.activation(out=o, in_=i, func=AF.Copy, scale=a),
                                 reads=[bkO, S_["egc"]], writes=[o1t] if j == 0 else (), pwrites=[o1t] if j else ())
                        K.op("dve", lambda e, o=oall[:, tb, :, :], i1=bkO.t[:, 256:512].rearrange("p (a b) -> p a b", a=2), i0=o1t[:, :, :]: e.tensor_tensor(out=o, in0=i0, in1=i1, op=ALU.subtract),
                             reads=[bkO, o1t], pwrites=[oall])
                        for j in range(2):
                            K.op("act", lambda e, o=junkb[:, :], i=oall[:, tb, j, :], a=ssq[:, 2 * tb + j:2 * tb + j + 1]: e.activation(out=o, in_=i, func=AF.Square, accum_out=a),
                                 reads=[oall], writes=[junkb], pwrites=[ssq])
                        yield

                NG = TBS // 2

                def lockstep(gens):
                    live = list(gens)
                    while live:
                        for g_ in list(live):
                            try:
                                next(g_)
                            except StopIteration:
                                live.remove(g_)
                        yield

                def chain(gens):
                    for g_ in gens:
                        yield from g_

                def delayed(gen_, n_):
                    for _ in range(n_):
                        yield
                    yield from gen_

                for _ in lockstep([phase1(0), delayed(phase1(1), STAG)]):
                    pass
                for p_ in range(NG // 2):
                    a = chain([phase2(2 * p_), phase2(2 * p_ + 1)])
                    b = lockstep([phase1(2 * p_ + 2), delayed(phase1(2 * p_ + 3), STAG)]) if 2 * p_ + 2 < NG else iter(())
                    a_live, b_live = True, True
                    while a_live or b_live:
                        if a_live:
                            try:
                                next(a)
                            except StopIteration:
                                a_live = False
                        for _ in range(ILV if a_live else 1000):
                            if b_live:
                                try:
                                    next(b)
                                except StopIteration:
                                    b_live = False
                            else:
                                break
                g0 = s * SEQ
                NC2 = TBS * 2
                K.op("act", lambda e, o=ssq[:, NC2:2 * NC2], i=ssq[:, 0:NC2]: e.activation(out=o, in_=i, func=AF.Ln, bias=self.epsT[:, 0:1], scale=1.0 / 128),
                     reads=[ssq, self.epsT], pwrites=[ssq])
                K.op("act", lambda e, o=ssq[:, NC2:2 * NC2]: e.activation(out=o, in_=o, func=AF.Exp, scale=-0.5), reads=[ssq], pwrites=[ssq])
                ov = oall[:, :, :, :].rearrange("p t j e -> p (t j) e")
                K.op("dve", lambda e, o=ov, i1=ssq[:, NC2:2 * NC2].unsqueeze(2).to_broadcast([128, NC2, 128]): e.tensor_tensor(out=o, in0=o, in1=i1, op=ALU.mult),
                     reads=[ssq], writes=[oall])
                K.op("dve", lambda e, o=oall[:, :, :, :].rearrange("p t j e -> p t (j e)"), i1=zs[:, :, :]: e.tensor_tensor(out=o, in0=o, in1=i1, op=ALU.mult),
                     reads=[zs], writes=[oall])
                for j in range(2):
                    hv = 2 * kh + j
                    G8 = min(8, TBS)
                    for t8 in range(TBS // G8):
                        bank = nqb2()
                        bvw = bank.t[:, :].bitcast(BF16)
                        for tl in range(G8):
                            K.op("pe", lambda e, o=bvw[:, tl * 128:(tl + 1) * 128], i=oall[:, t8 * G8 + tl, j, :]: e.transpose(o, i, self.ident[:, :]),
                                 reads=[oall, self.ident], writes=[bank] if tl == 0 else (), pwrites=[bank] if tl else (), mark=(tl == G8 - 1))
                        stg = stgr.next()
                        K.op("act", lambda e, o=stg[:, 0:G8 * 128], i=bvw[:, 0:G8 * 128]: e.copy(out=o, in_=i), reads=[bank], writes=[stg])
                        c0 = g0 + t8 * G8 * 128
                        K.dma("sp", self.OT[hv * 128:(hv + 1) * 128, c0:c0 + G8 * 128], stg[:, 0:G8 * 128], stg, reads=[stg],
                              pwrites=ot_deps[c0 // 512:(c0 + G8 * 128) // 512])
                self.release(mkR)
            self.release(mk)
        self.release(mk_top)
        mk = self.mark()
        wring = self.ring(2, [128, 32, 512], BF16, "wpan")
        uTr = self.ring(2, [128, 32, 512], BF16, "uT")
        xrr = self.ring(3, [128, 512], F32, "xr")
        xor_ = self.ring(3, [128, 512], F32, "xo")
        wov = self.c_w_out[0].rearrange("(kc p) n -> p kc n", p=128)
        otv = self.OT.rearrange("(kc p) t -> p kc t", p=128)
        for s in range(NSEQ):
            for tt in range(TBS // 4):
                g0 = s * SEQ + tt * 512
                uT = uTr.next()
                K.dma("sp", uT[:, :, :], otv[:, :, g0:g0 + 512], uT, reads=[ot_deps[g0 // 512]], writes=[uT])
                self.out_proj_tile(wov, 32, uT, 4, g0 // 128, src, dst, wring, xrr, xor_)
        self.release(mk)

ALL_SUBLAYERS = [("a", 0), ("ffn", 0), ("b", 1), ("ffn", 1), ("c", 2), ("ffn", 2), ("a", 3), ("ffn", 3)]


def make_consts():
    c = np.zeros((128, 1024), np.float32)
    c[:, 0:128] = np.eye(128, dtype=np.float32)
    c[:, 128:256] = np.tril(np.ones((128, 128), np.float32))
    c[:, 256:384] = np.triu(np.ones((128, 128), np.float32))
    return c


def build_nc(sublayers=ALL_SUBLAYERS, ntb_seq=16, nseq=2):
    nc = bass.Bass("TRN2", target_bir_lowering=False)
    with ExitStack() as es:
        p = Prog(nc, es, sublayers, ntb_seq, nseq)
        p.build()
    return nc


def kernel(**inputs):
    x = np.ascontiguousarray(inputs["x"], dtype=np.float32)
    nc = build_nc()
    consts = make_consts()
    shared = {k: np.ascontiguousarray(v, dtype=np.float32) for k, v in inputs.items() if k != "x"}
    shared["consts"] = consts
    in_maps = []
    for c in range(NCORES):
        m = dict(shared)
        m["x"] = x[2 * c:2 * c + 2].reshape(NT, D)
        in_maps.append(m)
    res = run_bass_kernel_spmd(nc, in_maps, core_ids=list(range(NCORES)))
    out = np.stack([r["out"].reshape(2, SEQ, D) for r in res.results], axis=0).reshape(16, SEQ, D)
    return out.astype(np.float32)
```

```python
import numpy as np
from contextlib import ExitStack
import concourse.bass as bass
import concourse.mybir as mybir
from concourse.bass_utils import run_bass_kernel_spmd

F32, BF16 = mybir.dt.float32, mybir.dt.bfloat16
AF = mybir.ActivationFunctionType
ALU = mybir.AluOpType
AX = mybir.AxisListType

NCORES = 8
D = 2048
NT = 4096
SEQ = 2048
FF = 5632
NFC = FF // 128
EPS = 1e-6
DEBUG = False
CSTOP = 99
ASTOP = 99
QMODE = 1
POOLENG = "dve"
ILV = 2
STAG = 3
KHN = 16


class Dep:
    __slots__ = ("w", "r")

    def __init__(self):
        self.w = {}
        self.r = {}


class T(Dep):
    __slots__ = ("t", "dsem")

    def __init__(self, t):
        Dep.__init__(self)
        self.t = t
        self.dsem = None

    def __getitem__(self, k):
        return self.t[k]


class Eng:
    def __init__(self, name, sem, is_pe=False):
        self.name, self.sem, self.is_pe = name, sem, is_pe
        self.count = 0
        self.seen = {}
        self.q = []


class DSem:
    def __init__(self, sem):
        self.sem = sem
        self.count = 0


class Ring:
    def __init__(self, tiles):
        self.tiles = tiles
        self.i = 0

    def next(self):
        t = self.tiles[self.i % len(self.tiles)]
        self.i += 1
        return t


def _merge(a, b):
    for k, v in b.items():
        if a.get(k, 0) < v:
            a[k] = v


class KB:
    def __init__(self, nc, es):
        self.nc, self.es = nc, es
        self.sems = {}
        self.E = {}
        for n in ("pe", "act", "dve", "pool", "sp"):
            self.E[n] = Eng(n, self.newsem("prog_" + n), is_pe=(n == "pe"))
        self.init_arena()

    def newsem(self, name):
        name = f"{name}_{len(self.sems)}"
        s = self.es.enter_context(self.nc.semaphore(name))
        self.sems[id(s)] = s
        return s

    ARENA_BYTES = 212480

    def init_arena(self):
        self.arena = self.es.enter_context(self.nc.sbuf_tensor("arena", [128, self.ARENA_BYTES // 2], BF16))
        self.off = 0
        self.dsem_pool = []
        self.live = []

    def sb(self, shape, dtype, name=None):
        esz = 4 if dtype == F32 else 2
        n = 1
        for d in shape[1:]:
            n *= d
        nbytes = (n * esz + 63) // 64 * 64
        assert self.off + nbytes <= self.ARENA_BYTES, f"SBUF arena overflow {name} {self.off + nbytes}"
        ap = self.arena[0:shape[0], self.off // 2:(self.off + n * esz) // 2]
        self.off += nbytes
        if dtype == F32:
            ap = ap.bitcast(F32)
        if len(shape) == 3:
            ap = ap.rearrange("p (a b) -> p a b", a=shape[1])
        elif len(shape) == 4:
            ap = ap.rearrange("p (a b c) -> p a b c", a=shape[1], b=shape[2])
        t = T(ap)
        self.live.append(t)
        return t

    def ring(self, n, shape, dtype, name=None):
        return Ring([self.sb(shape, dtype, name) for _ in range(n)])

    def ps(self, name):
        t = self.es.enter_context(self.nc.psum_tensor(name, [128, 512], F32))
        return T(t)

    def get_dsem(self):
        if self.dsem_pool:
            return self.dsem_pool.pop()
        return DSem(self.newsem("d"))

    def _waits(self, E, reads, writes, pwrites, own_sid=None):
        need = {}
        for d in reads:
            _merge(need, d.w)
        for d in writes:
            _merge(need, d.w)
            _merge(need, d.r)
        for d in pwrites:
            _merge(need, d.r)
            for k, v in d.w.items():
                if k != own_sid and need.get(k, 0) < v:
                    need[k] = v
        for sid, val in need.items():
            if E.seen.get(sid, 0) >= val:
                continue
            if E.is_pe and sid == id(E.sem):
                continue
            E.seen[sid] = val
            E.q.append(("w", self.sems[sid], val))

    def _stamp(self, sid, val, reads, writes, pwrites):
        for d in reads:
            if d.r.get(sid, 0) < val:
                d.r[sid] = val
        for d in writes:
            d.w = {sid: val}
            d.r = {}
        for d in pwrites:
            if d.w.get(sid, 0) < val:
                d.w[sid] = val

    def op(self, en, emit, reads=(), writes=(), pwrites=(), mark=True):
        E = self.E[en]
        self._waits(E, reads, writes, pwrites, id(E.sem))
        if mark:
            E.count += 1
            E.q.append(("i", emit, E.sem, 1))
            val = E.count
        else:
            E.q.append(("i", emit, None, 0))
            val = E.count + 1
        self._stamp(id(E.sem), val, reads, writes, pwrites)

    def dma(self, qn, out, in_, sem_tile, reads=(), writes=(), pwrites=(), slow=False):
        E = self.E[qn]
        if sem_tile.dsem is None:
            sem_tile.dsem = self.get_dsem()
        ds = sem_tile.dsem
        self._waits(E, reads, writes, pwrites, id(ds.sem))
        ds.count += 16
        if slow:
            E.q.append(("i", lambda e, o=out, i=in_: e.dma_start(out=o, in_=i, allow_slow_non_contiguous=True), ds.sem, 16))
        else:
            E.q.append(("i", lambda e, o=out, i=in_: e.dma_start(out=o, in_=i), ds.sem, 16))
        self._stamp(id(ds.sem), ds.count, reads, writes, pwrites)

    def mm(self, out, lhsT, rhs, start, stop, reads, bank, mark=None):
        if mark is None:
            mark = stop
        self.op("pe", lambda e: e.matmul(out, lhsT, rhs, start=start, stop=stop),
                reads=reads, pwrites=[bank] if not start else (), writes=[bank] if start else (), mark=mark)

    def finish(self, final_deps):
        E = self.E["sp"]
        need = {}
        for d in final_deps:
            _merge(need, d.w)
        for sid, val in need.items():
            E.q.append(("w", self.sems[sid], val))
        nc = self.nc
        with nc.Block() as block:
            def run(E, e):
                for it in E.q:
                    if it[0] == "w":
                        e.wait_ge(it[1], it[2])
                    else:
                        ins = it[1](e)
                        if it[2] is not None:
                            ins.then_inc(it[2], it[3])

            @block.sync
            def _(e):
                run(self.E["sp"], e)

            @block.scalar
            def _(e):
                run(self.E["act"], e)

            @block.vector
            def _(e):
                run(self.E["dve"], e)

            @block.gpsimd
            def _(e):
                run(self.E["pool"], e)

            @block.tensor
            def _(e):
                run(self.E["pe"], e)


class Resid:
    def __init__(self, ap):
        self.ap = ap
        self.deps = [Dep() for _ in range(NT // 128)]


class Prog:
    def __init__(self, nc, es, sublayers, ntb_seq=16, nseq=2):
        self.nc, self.es = nc, es
        self.K = KB(nc, es)
        self.sublayers = sublayers
        self.TBS = ntb_seq
        self.NSEQ = nseq
        self.declare()

    def declare(self):
        nc = self.nc

        def inp(name, shape):
            return nc.dram_tensor(name, list(shape), F32, kind="ExternalInput").ap()

        self.x = inp("x", [NT, D])
        self.norm_mix = inp("norm_mix", [4, D])
        self.norm_ffn = inp("norm_ffn", [4, D])
        self.ffn_w_gate = inp("ffn_w_gate", [4, D, FF])
        self.ffn_w_up = inp("ffn_w_up", [4, D, FF])
        self.ffn_conv_w = inp("ffn_conv_w", [4, 3, FF])
        self.ffn_conv_b = inp("ffn_conv_b", [4, FF])
        self.ffn_w_down = inp("ffn_w_down", [4, FF, D])
        self.a_w_in = inp("a_w_in", [2, D, 2 * D])
        self.a_b_in = inp("a_b_in", [2, 2 * D])
        self.a_v_norm = inp("a_v_norm", [2, D])
        self.a_w_s = inp("a_w_s", [2, 16, 128, 128])
        self.a_b_s = inp("a_b_s", [2, 16, 128])
        self.a_w_out = inp("a_w_out", [2, D, D])
        self.b_w_in = inp("b_w_in", [1, D, 8208])
        self.b_b_f = inp("b_b_f", [1, 16])
        self.b_q_norm = inp("b_q_norm", [1, 128])
        self.b_k_norm = inp("b_k_norm", [1, 128])
        self.b_w_out = inp("b_w_out", [1, D, D])
        self.c_w_in = inp("c_w_in", [1, D, 12352])
        self.c_conv_w = inp("c_conv_w", [1, 4, 8192])
        self.c_a_log = inp("c_a_log", [1, 32])
        self.c_dt_bias = inp("c_dt_bias", [1, 32])
        self.c_out_norm = inp("c_out_norm", [1, 128])
        self.c_w_out = inp("c_w_out", [1, 2 * D, D])
        self.consts = inp("consts", [128, 1024])
        self.out = nc.dram_tensor("out", [NT, D], F32, kind="ExternalOutput").ap()
        self.XA = nc.dram_tensor("XA", [NT, D], F32).ap()
        self.AT = nc.dram_tensor("AT", [FF, NT], BF16).ap()
        self.OT = nc.dram_tensor("OT", [2 * D, NT], BF16).ap()
        self.CS = nc.dram_tensor("CS", [6, 16, SEQ], BF16).ap()
        if DEBUG:
            self.dbgf = nc.dram_tensor("dbgf", [8, 128, 2048], F32, kind="ExternalOutput").ap()
            self.dbgb = nc.dram_tensor("dbgb", [8, 128, 2048], BF16, kind="ExternalOutput").ap()
            self.dbg_deps = []

    def dump(self, t, ap, idx, f32=False):
        if not DEBUG:
            return
        dst = self.dbgf if f32 else self.dbgb
        shp = ap.shape
        d = Dep()
        self.dbg_deps.append(d)
        self.K.dma("sp", dst[idx, 0:shp[0], 0:shp[1]], ap, t, reads=[t], writes=[d])

    def build(self):
        K = self.K
        self.banks = [K.ps(f"bank{i}") for i in range(8)]
        self.ident_f = K.sb([128, 128], F32, "identf")
        self.ident = K.sb([128, 128], BF16, "ident")
        self.epsT = K.sb([128, 1], F32, "eps")
        K.dma("sp", self.ident_f[:, :], self.consts[:, 0:128], self.ident_f, writes=[self.ident_f])
        K.op("dve", lambda e: e.tensor_copy(out=self.ident[:, :], in_=self.ident_f[:, :]),
             reads=[self.ident_f], writes=[self.ident])
        K.op("dve", lambda e: e.memset(self.epsT[:, :], EPS), writes=[self.epsT])

        res_in = Resid(self.x)
        res_a = Resid(self.XA)
        res_out = Resid(self.out)
        n = len(self.sublayers)
        cur = res_in
        for idx, (kind, li) in enumerate(self.sublayers):
            dst = res_out if idx == n - 1 else res_a
            mk = self.mark()
            if kind == "ffn":
                self.ffn(li, cur, dst)
            elif kind == "a":
                self.mixer_a(li // 3, li, cur, dst)
            elif kind == "b":
                self.mixer_b(li, cur, dst)
            elif kind == "c":
                self.mixer_c(li, cur, dst)
            else:
                raise NotImplementedError(kind)
            self.release(mk)
            cur = dst
        K.finish(res_out.deps + (self.dbg_deps if DEBUG else []))

    def mark(self):
        return (self.K.off, len(self.K.live))

    def release(self, mk):
        self.barrier_free(mk[1])
        self.K.off = mk[0]

    def barrier_free(self, mark_live):
        K = self.K
        dead = K.live[mark_live:]
        del K.live[mark_live:]
        need = {}
        for O in K.E.values():
            if O.count:
                need[id(O.sem)] = O.count
        for t in dead:
            _merge(need, t.w)
            _merge(need, t.r)
            if t.dsem is not None:
                need[id(t.dsem.sem)] = max(need.get(id(t.dsem.sem), 0), t.dsem.count)
                K.dsem_pool.append(t.dsem)
                t.dsem = None
        for en, E in K.E.items():
            for sid, val in need.items():
                if sid == id(E.sem):
                    continue
                if E.seen.get(sid, 0) < val:
                    E.seen[sid] = val
                    E.q.append(("w", K.sems[sid], val))

    def sb(self, shape, dtype, name=None):
        return self.K.sb(shape, dtype, name)

    def ring(self, n, shape, dtype, name=None):
        return self.K.ring(n, shape, dtype, name)

    def norm_setup(self, gamma_row):
        K = self.K
        st = {}
        st["gam"] = self.sb([128, D], F32, "gam")
        K.dma("sp", st["gam"][:, :], gamma_row.partition_broadcast(128), st["gam"], writes=[st["gam"]])
        st["xs"] = self.ring(2, [128, D], F32, "xs")
        st["hn"] = self.ring(2, [128, D], BF16, "hn")
        st["junk"] = self.sb([128, D], BF16, "junk")
        st["ss"] = self.ring(4, [128, 4], F32, "ss")
        return st

    def norm_T(self, st, src, tb_glob, hT, hdep, tcol):
        K = self.K
        xs = st["xs"].next()
        hn = st["hn"].next()
        ss = st["ss"].next()
        gam, junk = st["gam"], st["junk"]
        K.dma("sp", xs[:, :], src.ap[tb_glob * 128:(tb_glob + 1) * 128, :], xs,
              reads=[src.deps[tb_glob]], writes=[xs])
        K.op("dve", lambda e: e.memset(ss[:, :], 0.0), writes=[ss])
        K.op("act", lambda e: e.activation(out=junk[:, :], in_=xs[:, :], func=AF.Square, accum_out=ss[:, 0:1]),
             reads=[xs], writes=[junk], pwrites=[ss])
        K.op("act", lambda e: e.activation(out=ss[:, 1:2], in_=ss[:, 0:1], func=AF.Sqrt,
                                           bias=self.epsT[:, 0:1], scale=1.0 / D),
             reads=[ss, self.epsT], pwrites=[ss])
        K.op("dve", lambda e: e.reciprocal(out=ss[:, 2:3], in_=ss[:, 1:2]), reads=[ss], pwrites=[ss])
        K.op("dve", lambda e: e.scalar_tensor_tensor(out=hn[:, :], in0=xs[:, :], scalar=ss[:, 2:3], in1=gam[:, :],
                                                      op0=ALU.mult, op1=ALU.mult),
             reads=[xs, ss, gam], writes=[hn])
        for half in range(2):
            bank = self.banks[self._tbank % 8]
            self._tbank += 1
            bv = bank.t[:, :].bitcast(BF16)
            for j in range(8):
                kc = half * 8 + j
                K.op("pe", lambda e, o=bv[:, j * 128:(j + 1) * 128], i=hn[:, kc * 128:(kc + 1) * 128]:
                     e.transpose(o, i, self.ident[:, :]),
                     reads=[hn, self.ident], writes=[bank] if j == 0 else (), pwrites=[bank] if j else (),
                     mark=(j == 7))
            eng = "act" if half == 0 else "dve"
            o = hT[:, half * 8:(half + 1) * 8, tcol:tcol + 128]
            i = bv.rearrange("p (k t) -> p k t", k=8)
            if eng == "act":
                K.op("act", lambda e, o=o, i=i: e.copy(out=o, in_=i), reads=[bank], pwrites=[hdep])
            else:
                K.op("dve", lambda e, o=o, i=i: e.tensor_copy(out=o, in_=i), reads=[bank], pwrites=[hdep])

    _tbank = 0

    def ffn(self, li, src, dst):
        K = self.K
        TBS, NSEQ = self.TBS, self.NSEQ
        TS = TBS * 128
        HW = min(1024, TS)
        NH = TS // HW
        mk0 = self.mark()
        st = self.norm_setup(self.norm_ffn[li, :])
        hT = self.sb([128, 16, TS], BF16, "hT")
        hdeps = [Dep() for _ in range(TBS)]
        cw = self.sb([128, 3, NFC], F32, "cw")
        cb = self.sb([128, NFC], F32, "cb")
        for k in range(3):
            K.dma("sp", cw[:, k, :], self.ffn_conv_w[li, k, :].rearrange("(c p) -> p c", p=128), cw, pwrites=[cw], slow=True)
        K.dma("sp", cb[:, :], self.ffn_conv_b[li, :].rearrange("(c p) -> p c", p=128), cb, writes=[cb], slow=True)
        PW = 256
        wgr = self.ring(2, [128, 16, PW], BF16, "wg")
        wur = self.ring(2, [128, 16, PW], BF16, "wu")
        gsr = self.ring(2, [128, HW + 2], F32, "gs")
        t1r = self.ring(2, [128, HW], F32, "t1")
        sgr = self.ring(2, [128, HW], F32, "sg")
        aTr = self.ring(3, [128, HW], BF16, "aT")
        NTT = NT // 256
        at_deps = [Dep() for _ in range(NTT)]
        wgv = self.ffn_w_gate[li].rearrange("(kc p) n -> p kc n", p=128)
        wuv = self.ffn_w_up[li].rearrange("(kc p) n -> p kc n", p=128)
        bset = 0
        for s in range(NSEQ):
            for tb in range(TBS):
                self.norm_T(st, src, s * 16 + tb, hT, hdeps[tb], tb * 128)
            for fp in range(FF // PW):
                wg, wu = wgr.next(), wur.next()
                K.dma("pool", wg[:, :, :], wgv[:, :, fp * PW:(fp + 1) * PW], wg, writes=[wg])
                K.dma("pool", wu[:, :, :], wuv[:, :, fp * PW:(fp + 1) * PW], wu, writes=[wu])
                for fcl in range(PW // 128):
                    fc = fp * (PW // 128) + fcl
                    prev_gs = None
                    for h in range(NH):
                        nb = HW // 512
                        bks = self.banks[bset * 4:(bset + 1) * 4]
                        bset ^= 1
                        psG, psU = bks[0:nb], bks[2:2 + nb]
                        for (w, ps) in ((wg, psG), (wu, psU)):
                            for kc in range(16):
                                for b in range(nb):
                                    t0 = h * HW + b * 512
                                    K.mm(ps[b][:, :], w[:, kc, fcl * 128:(fcl + 1) * 128], hT[:, kc, t0:t0 + 512],
                                         start=(kc == 0), stop=(kc == 15),
                                         reads=[w] + hdeps[t0 // 128:t0 // 128 + 4], bank=ps[b])
                        gs, t1, sg, aT = gsr.next(), t1r.next(), sgr.next(), aTr.next()
                        for b in range(nb):
                            K.op("act", lambda e, o=gs[:, 2 + b * 512:2 + (b + 1) * 512], i=psG[b][:, :]: e.copy(out=o, in_=i),
                                 reads=[psG[b]], writes=[gs] if b == 0 else (), pwrites=[gs] if b else ())
                        if h == 0:
                            K.op("dve", lambda e, o=gs[:, 0:2]: e.memset(o, 0.0), pwrites=[gs])
                        else:
                            K.op("dve", lambda e, o=gs[:, 0:2], i=prev_gs[:, HW:HW + 2]: e.tensor_copy(out=o, in_=i),
                                 reads=[prev_gs], pwrites=[gs])
                        prev_gs = gs
                        K.op("dve", lambda e, o=t1[:, :], i=gs[:, 2:HW + 2], a=cw[:, 2, fc:fc + 1], b_=cb[:, fc:fc + 1]:
                             e.tensor_scalar(out=o, in0=i, scalar1=a, scalar2=b_, op0=ALU.mult, op1=ALU.add),
                             reads=[gs, cw, cb], writes=[t1])
                        K.op("dve", lambda e, o=t1[:, :], i=gs[:, 1:HW + 1], a=cw[:, 1, fc:fc + 1]:
                             e.scalar_tensor_tensor(out=o, in0=i, scalar=a, in1=o, op0=ALU.mult, op1=ALU.add),
                             reads=[gs, cw], writes=[t1])
                        K.op("dve", lambda e, o=t1[:, :], i=gs[:, 0:HW], a=cw[:, 0, fc:fc + 1]:
                             e.scalar_tensor_tensor(out=o, in0=i, scalar=a, in1=o, op0=ALU.mult, op1=ALU.add),
                             reads=[gs, cw], writes=[t1])
                        K.op("act", lambda e, o=sg[:, :], i=t1[:, :]: e.activation(out=o, in_=i, func=AF.Silu),
                             reads=[t1], writes=[sg])
                        for b in range(nb):
                            K.op("dve", lambda e, o=aT[:, b * 512:(b + 1) * 512], i0=psU[b][:, :], i1=sg[:, b * 512:(b + 1) * 512]:
                                 e.tensor_tensor(out=o, in0=i0, in1=i1, op=ALU.mult),
                                 reads=[psU[b], sg], writes=[aT] if b == 0 else (), pwrites=[aT] if b else ())
                        g0 = s * SEQ + h * HW
                        K.dma("sp", self.AT[fc * 128:(fc + 1) * 128, g0:g0 + HW], aT[:, :], aT,
                              reads=[aT], pwrites=at_deps[g0 // 256:(g0 + HW) // 256])
        self.release(mk0)
        wdr = self.ring(2, [128, NFC, 512], BF16, "wd")
        atr = self.ring(2, [128, NFC, 256], BF16, "at")
        xrr = self.ring(3, [128, 512], F32, "xr")
        xor_ = self.ring(3, [128, 512], F32, "xo")
        wdv = self.ffn_w_down[li].rearrange("(fc p) n -> p fc n", p=128)
        atv = self.AT.rearrange("(fc p) t -> p fc t", p=128)
        bi = 0
        for dp in range(4):
            wd = wdr.next()
            for q in range(4):
                K.dma("pool", wd[:, q * 11:(q + 1) * 11, :], wdv[:, q * 11:(q + 1) * 11, dp * 512:(dp + 1) * 512], wd,
                      writes=[wd] if q == 0 else (), pwrites=[wd] if q else ())
            for s in range(NSEQ):
                for tt in range(TS // 256):
                    g0 = s * SEQ + tt * 256
                    at = atr.next()
                    K.dma("sp", at[:, :, :], atv[:, :, g0:g0 + 256], at, reads=[at_deps[g0 // 256]], writes=[at])
                    for tb in range(2):
                        gb = g0 // 128 + tb
                        bank = self.banks[bi % 8]
                        bi += 1
                        for fc in range(NFC):
                            K.mm(bank[:, :], at[:, fc, tb * 128:(tb + 1) * 128], wd[:, fc, :], start=(fc == 0),
                                 stop=(fc == NFC - 1), reads=[at, wd], bank=bank)
                        xr, xo = xrr.next(), xor_.next()
                        K.dma("sp", xr[:, :], src.ap[gb * 128:(gb + 1) * 128, dp * 512:(dp + 1) * 512], xr,
                              reads=[src.deps[gb]], writes=[xr])
                        K.op("dve", lambda e, o=xo[:, :], i0=bank[:, :], i1=xr[:, :]: e.tensor_tensor(out=o, in0=i0, in1=i1, op=ALU.add),
                             reads=[bank, xr], writes=[xo])
                        K.dma("act", dst.ap[gb * 128:(gb + 1) * 128, dp * 512:(dp + 1) * 512], xo[:, :], xo,
                              reads=[xo], pwrites=[dst.deps[gb]])

    def out_proj_tile(self, wv, KC, uT, ntb, g_tb0, src, dst, wring, xrr, xor_):
        K = self.K
        for p in range(4):
            w = wring.next()
            K.dma("pool", w[:, 0:KC, :], wv[:, :, p * 512:(p + 1) * 512], w, writes=[w])
            for tb in range(ntb):
                gb = g_tb0 + tb
                bank = self.banks[self._tbank % 8]
                self._tbank += 1
                for kc in range(KC):
                    K.mm(bank[:, :], uT[:, kc, tb * 128:(tb + 1) * 128], w[:, kc, :], start=(kc == 0), stop=(kc == KC - 1),
                         reads=[uT, w], bank=bank)
                xr, xo = xrr.next(), xor_.next()
                K.dma("sp", xr[:, :], src.ap[gb * 128:(gb + 1) * 128, p * 512:(p + 1) * 512], xr,
                      reads=[src.deps[gb]], writes=[xr])
                K.op("dve", lambda e, o=xo[:, :], i0=bank[:, :], i1=xr[:, :]: e.tensor_tensor(out=o, in0=i0, in1=i1, op=ALU.add),
                     reads=[bank, xr], writes=[xo])
                K.dma("act", dst.ap[gb * 128:(gb + 1) * 128, p * 512:(p + 1) * 512], xo[:, :], xo,
                      reads=[xo], pwrites=[dst.deps[gb]])

    def mixer_a(self, j, li, src, dst):
        K = self.K
        TBS, NSEQ = self.TBS, self.NSEQ
        st = self.norm_setup(self.norm_mix[li, :])
        hT = self.sb([128, 16, 512], BF16, "hT")
        hdeps = [Dep() for _ in range(4)]
        uT = self.sb([128, 16, 512], BF16, "uT")
        vr = [self.sb([128, D], BF16, "v") for _ in range(4)]
        wring = self.ring(2, [128, 16, 512], BF16, "wpan")
        bs_bc = self.sb([128, D], F32, "bs_bc")
        bv_bc = self.sb([128, D], F32, "bv_bc")
        b_u = self.sb([128, 16], F32, "b_u")
        vn_col = self.sb([128, 16], F32, "vn_col")
        WcT = self.sb([128, D], BF16, "WcT")
        WcSr = self.ring(2, [128, D], BF16, "WcS")
        triu = self.sb([128, 128], F32, "triu")
        xbr = self.ring(2, [128, 512], F32, "xb")
        v32r = self.ring(2, [128, 512], F32, "v32")
        t32r = self.ring(2, [128, 512], F32, "t32")
        xrr = self.ring(3, [128, 512], F32, "xr")
        xor_ = self.ring(3, [128, 512], F32, "xo")
        ssr = self.ring(4, [128, 8], F32, "ssv")
        junk = st["junk"]
        K.dma("sp", bs_bc[:, :], self.a_b_s[j].rearrange("g t -> (g t)").partition_broadcast(128), bs_bc, writes=[bs_bc])
        K.dma("sp", bv_bc[:, :], self.a_b_in[j, D:2 * D].partition_broadcast(128), bv_bc, writes=[bv_bc])
        K.dma("sp", b_u[:, :], self.a_b_in[j, 0:D].rearrange("(c p) -> p c", p=128), b_u, writes=[b_u], slow=True)
        K.dma("sp", vn_col[:, :], self.a_v_norm[j, :].rearrange("(c p) -> p c", p=128), vn_col, writes=[vn_col], slow=True)
        K.dma("sp", triu[:, :], self.consts[:, 256:384], triu, writes=[triu])
        mk = self.mark()
        wsraw = self.sb([128, 16, 128], F32, "wsraw")
        K.dma("sp", wsraw[:, :, :], self.a_w_s[j].rearrange("g t s -> t g s"), wsraw, writes=[wsraw])
        for g in range(16):
            bank = self.banks[self._tbank % 8]
            self._tbank += 1
            K.op("pe", lambda e, o=bank[:, 0:128], i=wsraw[:, g, :]: e.transpose(o, i, self.ident_f[:, :]),
                 reads=[wsraw, self.ident_f], writes=[bank])
            K.op("dve", lambda e, o=WcT[:, g * 128:(g + 1) * 128], i0=bank[:, 0:128], i1=triu[:, :]:
                 e.tensor_tensor(out=o, in0=i0, in1=i1, op=ALU.mult), reads=[bank, triu], pwrites=[WcT])
        self.release(mk)
        wiv = self.a_w_in[j].rearrange("(kc p) n -> p kc n", p=128)
        wov = self.a_w_out[j].rearrange("(kc p) n -> p kc n", p=128)
        for s in range(NSEQ):
            for tt in range(TBS // 4):
                gtb0 = s * 16 + tt * 4
                for tb in range(4):
                    self.norm_T(st, src, gtb0 + tb, hT, hdeps[tb], tb * 128)
                for p in range(4):
                    w = wring.next()
                    K.dma("pool", w[:, :, :], wiv[:, :, p * 512:(p + 1) * 512], w, writes=[w])
                    for fcl in range(4):
                        fc = p * 4 + fcl
                        bank = self.banks[self._tbank % 8]
                        self._tbank += 1
                        for kc in range(16):
                            K.mm(bank[:, :], w[:, kc, fcl * 128:(fcl + 1) * 128], hT[:, kc, :], start=(kc == 0), stop=(kc == 15),
                                 reads=[w] + hdeps, bank=bank)
                        K.op("act", lambda e, o=uT[:, fc, :], i=bank[:, :], b=b_u[:, fc:fc + 1]:
                             e.activation(out=o, in_=i, func=AF.Gelu_apprx_tanh, bias=b),
                             reads=[bank, b_u], pwrites=[uT])
                sss = [ssr.next() for _ in range(4)]
                for tb in range(4):
                    K.op("dve", lambda e, o=sss[tb][:, :]: e.memset(o, 0.0), writes=[sss[tb]])
                for p in range(4):
                    w = wring.next()
                    K.dma("pool", w[:, :, :], wiv[:, :, D + p * 512:D + (p + 1) * 512], w, writes=[w])
                    for tb in range(4):
                        bank = self.banks[self._tbank % 8]
                        self._tbank += 1
                        for kc in range(16):
                            K.mm(bank[:, :], hT[:, kc, tb * 128:(tb + 1) * 128], w[:, kc, :], start=(kc == 0), stop=(kc == 15),
                                 reads=[w, hdeps[tb]], bank=bank)
                        xb, v32 = xbr.next(), v32r.next()
                        K.op("dve", lambda e, o=xb[:, :], i0=bank[:, :], i1=bv_bc[:, p * 512:(p + 1) * 512]:
                             e.tensor_tensor(out=o, in0=i0, in1=i1, op=ALU.add), reads=[bank, bv_bc], writes=[xb])
                        K.op("act", lambda e, o=v32[:, :], i=xb[:, :]: e.activation(out=o, in_=i, func=AF.Gelu_apprx_tanh),
                             reads=[xb], writes=[v32])
                        K.op("act", lambda e, o=junk[:, 0:512], i=v32[:, :], a=sss[tb][:, p:p + 1]:
                             e.activation(out=o, in_=i, func=AF.Square, accum_out=a),
                             reads=[v32], writes=[junk], pwrites=[sss[tb]])
                        K.op("dve", lambda e, o=vr[tb][:, p * 512:(p + 1) * 512], i=v32[:, :]: e.tensor_copy(out=o, in_=i),
                             reads=[v32], pwrites=[vr[tb]])
                for tb in range(4):
                    ss = sss[tb]
                    K.op("dve", lambda e, o=ss[:, 4:5], i=ss[:, 0:4]: e.reduce_sum(out=o, in_=i, axis=AX.X),
                         reads=[ss], pwrites=[ss])
                    K.op("act", lambda e, o=ss[:, 5:6], i=ss[:, 4:5]: e.activation(out=o, in_=i, func=AF.Sqrt,
                                                                                 bias=self.epsT[:, 0:1], scale=1.0 / D),
                         reads=[ss, self.epsT], pwrites=[ss])
                    K.op("dve", lambda e, o=ss[:, 6:7], i=ss[:, 5:6]: e.reciprocal(out=o, in_=i), reads=[ss], pwrites=[ss])
                    WcS = WcSr.next()
                    K.op("dve", lambda e, o=WcS[:, :], i=WcT[:, :], a=ss[:, 6:7]: e.tensor_scalar(out=o, in0=i, scalar1=a, scalar2=None, op0=ALU.mult),
                         reads=[WcT, ss], writes=[WcS])
                    for gq in range(4):
                        bank = self.banks[self._tbank % 8]
                        self._tbank += 1
                        t32 = t32r.next()
                        for gl in range(4):
                            g = gq * 4 + gl
                            K.op("pe", lambda e, o=bank[:, gl * 128:(gl + 1) * 128], l=vr[tb][:, g * 128:(g + 1) * 128], r=WcS[:, g * 128:(g + 1) * 128]:
                                 e.matmul(o, l, r, start=True, stop=True),
                                 reads=[vr[tb], WcS], writes=[bank] if gl == 0 else (), pwrites=[bank] if gl else (), mark=(gl == 3))
                        for gl in range(4):
                            g = gq * 4 + gl
                            K.op("dve", lambda e, o=t32[:, gl * 128:(gl + 1) * 128], i0=bank[:, gl * 128:(gl + 1) * 128], a=vn_col[:, g:g + 1], i1=bs_bc[:, g * 128:(g + 1) * 128]:
                                 e.scalar_tensor_tensor(out=o, in0=i0, scalar=a, in1=i1, op0=ALU.mult, op1=ALU.add),
                                 reads=[bank, vn_col, bs_bc], writes=[t32] if gl == 0 else (), pwrites=[t32] if gl else ())
                        uv = uT[:, gq * 4:(gq + 1) * 4, tb * 128:(tb + 1) * 128]
                        K.op("dve", lambda e, o=uv, i0=uv, i1=t32[:, :].rearrange("p (g t) -> p g t", g=4):
                             e.tensor_tensor(out=o, in0=i0, in1=i1, op=ALU.mult), reads=[t32, uT], pwrites=[uT])
                self.out_proj_tile(wov, 16, uT, 4, gtb0, src, dst, wring, xrr, xor_)

    def mixer_b(self, li, src, dst):
        K = self.K
        TBS, NSEQ = self.TBS, self.NSEQ
        TS = TBS * 128
        NQB = TS // 512
        bS, bP, bO = self.banks[0:2], self.banks[2:4], self.banks[4:8]
        cnt = {"S": 0, "P": 0}

        def nbank(role):
            lst = bS if role == "S" else bP
            b = lst[cnt[role] % 2]
            cnt[role] += 1
            return b

        hT = self.sb([128, 16, TS], BF16, "hT")
        hdeps = [Dep() for _ in range(TBS)]
        qn_col = self.sb([128, 1], F32, "qn_col")
        kn_col = self.sb([128, 1], F32, "kn_col")
        negbf = self.sb([16, 1], F32, "negbf")
        ones_bf = self.sb([128, 128], BF16, "ones")
        negmask = self.sb([128, 128], F32, "negmask")
        K.dma("sp", qn_col[:, :], self.b_q_norm[0, :].rearrange("(p o) -> p o", o=1), qn_col, writes=[qn_col], slow=True)
        K.dma("sp", kn_col[:, :], self.b_k_norm[0, :].rearrange("(p o) -> p o", o=1), kn_col, writes=[kn_col], slow=True)
        K.dma("sp", negbf[:, :], self.b_b_f[0, :].rearrange("(p o) -> p o", o=1), negbf, writes=[negbf], slow=True)
        K.dma("sp", negmask[:, :], self.consts[:, 256:384], negmask, writes=[negmask])
        K.op("dve", lambda e: e.tensor_scalar(out=qn_col[:, :], in0=qn_col[:, :], scalar1=128.0 ** -0.5, scalar2=None, op0=ALU.mult),
             reads=[qn_col], writes=[qn_col])
        K.op("dve", lambda e: e.tensor_scalar(out=negbf[:, :], in0=negbf[:, :], scalar1=-1.0, scalar2=None, op0=ALU.mult),
             reads=[negbf], writes=[negbf])
        K.op("dve", lambda e: e.tensor_scalar(out=negmask[:, :], in0=negmask[:, :], scalar1=-1.0, scalar2=30000.0, op0=ALU.add, op1=ALU.mult),
             reads=[negmask], writes=[negmask])
        K.op("dve", lambda e: e.memset(ones_bf[:, :], 1.0), writes=[ones_bf])
        wiv = self.b_w_in[0].rearrange("(kc p) n -> p kc n", p=128)
        cs_dep = Dep()
        ot_deps = [Dep() for _ in range(NT // 512)]
        for s in range(NSEQ):
            mk = self.mark()
            st = self.norm_setup(self.norm_mix[li, :])
            for tb in range(TBS):
                self.norm_T(st, src, s * 16 + tb, hT, hdeps[tb], tb * 128)
            self.release(mk)
            mk = self.mark()
            wfl = self.sb([128, 16, 16], BF16, "wfl")
            K.dma("pool", wfl[:, :, :], wiv[:, :, 8192:8208], wfl, writes=[wfl])
            spt = self.sb([16, TS], F32, "spt")
            ct = self.sb([16, TS], F32, "ct")
            r1 = self.sb([16, TS], F32, "r1")
            parts = [self.sb([16, TS], BF16, "cp") for _ in range(6)]
            for tq in range(NQB):
                bank = nbank("P")
                for kc in range(16):
                    K.mm(bank[0:16, :], wfl[:, kc, :], hT[:, kc, tq * 512:(tq + 1) * 512], start=(kc == 0), stop=(kc == 15),
                         reads=[wfl] + hdeps[tq * 4:tq * 4 + 4], bank=bank)
                K.op("act", lambda e, o=spt[:, tq * 512:(tq + 1) * 512], i=bank[0:16, :]:
                     e.activation(out=o, in_=i, func=AF.Softplus, bias=negbf[:, 0:1], scale=-1.0),
                     reads=[bank, negbf], pwrites=[spt])
            K.op("dve", lambda e: e.tensor_scalar(out=spt[:, :], in0=spt[:, :], scalar1=-0.5, scalar2=None, op0=ALU.mult),
                 reads=[spt], writes=[spt])
            K.op("dve", lambda e: e.tensor_tensor_scan(out=ct[:, :], data0=spt[:, :], data1=spt[:, :], initial=0.0, op0=ALU.add, op1=ALU.add),
                 reads=[spt], writes=[ct])
            self.dump(ct, ct[:, :], 0, f32=True)
            cur = ct
            for i3 in range(3):
                p_ = parts[3 + i3]
                K.op("dve", lambda e, o=p_[:, :], i=cur[:, :]: e.tensor_copy(out=o, in_=i), reads=[cur], writes=[p_])
                K.op("dve", lambda e, o=parts[i3][:, :], i=p_[:, :]: e.tensor_scalar(out=o, in0=i, scalar1=-1.0, scalar2=None, op0=ALU.mult),
                     reads=[p_], writes=[parts[i3]])
                if i3 < 2:
                    K.op("dve", lambda e, o=r1[:, :], i0=cur[:, :], i1=p_[:, :]: e.tensor_tensor(out=o, in0=i0, in1=i1, op=ALU.subtract),
                         reads=[cur, p_], writes=[r1])
                    cur = r1
            for i6 in range(6):
                K.dma("sp", self.CS[i6, :, 0:TS], parts[i6][:, :], parts[i6], reads=[parts[i6]],
                      writes=[cs_dep] if i6 == 0 else (), pwrites=[cs_dep] if i6 else ())
            self.release(mk)
            mk = self.mark()
            whr = self.ring(2, [128, 4, 16, 128], BF16, "wh")
            LKr = self.ring(2, [6, TS], BF16, "LK")
            RQr = self.ring(2, [6, TS], BF16, "RQ")
            for t_ in LKr.tiles + RQr.tiles:
                K.op("dve", lambda e, o=t_[:, :]: e.memset(o, 1.0), writes=[t_])
            qTr = self.ring(2, [128, TS], BF16, "qT")
            kTr = self.ring(2, [128, TS], BF16, "kT")
            vaugr = self.ring(2, [128, TBS, 129], BF16, "vaug")
            for t_ in vaugr.tiles:
                K.op("dve", lambda e, o=t_[:, :, :]: e.memset(o, 1.0), writes=[t_])
            sgr = self.ring(2, [128, TBS, 128], BF16, "sg")
            PTr = self.ring(4, [128, 512], BF16, "PT")
            oThr = self.ring(2, [128, TS], BF16, "oTh")
            sqr = self.ring(4, [128, 512], BF16, "sq")
            sdr = self.ring(4, [128, 512], F32, "sd")
            rsr = self.ring(4, [128, 512], F32, "rs")
            dtr = self.ring(2, [128, 128], F32, "dtmp")
            recr = self.ring(4, [128, 1], F32, "rec")
            ogor = self.ring(2, [128, 128], BF16, "ogo")
            for h in range(16):
                wh = whr.next()
                for m in range(4):
                    K.dma("pool", wh[:, m, :, :], wiv[:, :, m * 2048 + h * 128:m * 2048 + (h + 1) * 128], wh,
                          writes=[wh] if m == 0 else (), pwrites=[wh] if m else ())
                LK, RQ = LKr.next(), RQr.next()
                K.dma("sp", LK[0:3, :], self.CS[0:3, h, 0:TS], LK, reads=[cs_dep], pwrites=[LK])
                K.dma("sp", RQ[3:6, :], self.CS[3:6, h, 0:TS], RQ, reads=[cs_dep], pwrites=[RQ])
                qT, kT = qTr.next(), kTr.next()

                def qk_chain(m, dT, col, bks, delay):
                    for _ in range(delay):
                        yield
                    for tq in range(NQB):
                        bank, bank2 = bks
                        for kc in range(16):
                            K.mm(bank[:, :], wh[:, m, kc, :], hT[:, kc, tq * 512:(tq + 1) * 512], start=(kc == 0), stop=(kc == 15),
                                 reads=[wh] + hdeps[tq * 4:tq * 4 + 4], bank=bank)
                        sq, sd, rs = sqr.next(), sdr.next(), rsr.next()
                        K.op("act", lambda e, o=sq[:, :], i=bank[:, :]: e.activation(out=o, in_=i, func=AF.Square),
                             reads=[bank], writes=[sq])
                        yield
                        K.mm(bank2[:, :], ones_bf[:, :], sq[:, :], start=True, stop=True, reads=[ones_bf, sq], bank=bank2)
                        K.op("act", lambda e, o=sd[:, :], i=bank2[:, :]: e.activation(out=o, in_=i, func=AF.Sqrt, bias=self.epsT[:, 0:1], scale=1.0 / 128),
                             reads=[bank2, self.epsT], writes=[sd])
                        yield
                        K.op("dve", lambda e, o=rs[:, :], i=sd[:, :]: e.reciprocal(out=o, in_=i), reads=[sd], writes=[rs])
                        K.op("dve", lambda e, o=dT[:, tq * 512:(tq + 1) * 512], i0=bank[:, :], a=col[:, 0:1], i1=rs[:, :]:
                             e.scalar_tensor_tensor(out=o, in0=i0, scalar=a, in1=i1, op0=ALU.mult, op1=ALU.mult),
                             reads=[bank, col, rs], pwrites=[dT])
                        yield
                live = [qk_chain(0, qT, qn_col, (bP[0], bP[1]), 0), qk_chain(1, kT, kn_col, (bO[0], bO[1]), 1)]
                while live:
                    for g_ in list(live):
                        try:
                            next(g_)
                        except StopIteration:
                            live.remove(g_)
                vaug, sg = vaugr.next(), sgr.next()
                vcnt = 0
                for (m, which) in ((2, "v"), (3, "g")):
                    for tb4 in range(TBS // 4):
                        bank = bO[2 + vcnt % 2]
                        vcnt += 1
                        for tl in range(4):
                            tb = tb4 * 4 + tl
                            for kc in range(16):
                                K.op("pe", lambda e, o=bank[:, tl * 128:(tl + 1) * 128], l=hT[:, kc, tb * 128:(tb + 1) * 128], r=wh[:, m, kc, :], st_=(kc == 0), sp_=(kc == 15):
                                     e.matmul(o, l, r, start=st_, stop=sp_),
                                     reads=[wh, hdeps[tb]], writes=[bank] if (tl == 0 and kc == 0) else (),
                                     pwrites=() if (tl == 0 and kc == 0) else [bank], mark=(tl == 3 and kc == 15))
                        bv = bank[:, :].rearrange("p (a b) -> p a b", a=4)
                        if which == "v":
                            K.op("act", lambda e, o=vaug[:, tb4 * 4:(tb4 + 1) * 4, 0:128], i=bv: e.copy(out=o, in_=i),
                                 reads=[bank], pwrites=[vaug])
                        else:
                            K.op("act", lambda e, o=sg[:, tb4 * 4:(tb4 + 1) * 4, :], i=bv: e.activation(out=o, in_=i, func=AF.Sigmoid),
                                 reads=[bank], pwrites=[sg])
                oTh = oThr.next()
                for qb in range(NQB):
                    def front(kb, qb=qb):
                        o_ = max(0, kb - 4 * qb)
                        c0 = o_ * 128
                        sbk = nbank("S")
                        K.mm(sbk[:, c0:512], kT[:, kb * 128:(kb + 1) * 128], qT[:, qb * 512 + c0:(qb + 1) * 512], start=True, stop=False,
                             reads=[kT, qT], bank=sbk)
                        K.mm(sbk[:, c0:512], LK[0:6, kb * 128:(kb + 1) * 128], RQ[0:6, qb * 512 + c0:(qb + 1) * 512], start=False, stop=True,
                             reads=[LK, RQ], bank=sbk)
                        PT = PTr.next()
                        if kb >= 4 * qb:
                            dt_ = dtr.next()
                            K.op("dve", lambda e, o=dt_[:, :], i0=sbk[:, c0:c0 + 128]: e.tensor_tensor(out=o, in0=i0, in1=negmask[:, :], op=ALU.add),
                                 reads=[sbk, negmask], writes=[dt_])
                            K.op("act", lambda e, o=PT[:, c0:c0 + 128], i=dt_[:, :]: e.activation(out=o, in_=i, func=AF.Exp),
                                 reads=[dt_], writes=[PT])
                            if c0 + 128 < 512:
                                K.op("act", lambda e, o=PT[:, c0 + 128:512], i=sbk[:, c0 + 128:512]: e.activation(out=o, in_=i, func=AF.Exp),
                                     reads=[sbk], pwrites=[PT])
                        else:
                            K.op("act", lambda e, o=PT[:, :], i=sbk[:, :]: e.activation(out=o, in_=i, func=AF.Exp),
                                 reads=[sbk], writes=[PT])
                        return PT

                    def back(kb, PT, qb=qb):
                        for qs in range(4):
                            if kb <= 4 * qb + qs:
                                K.mm(bO[qs][:, 0:129], PT[:, qs * 128:(qs + 1) * 128], vaug[:, kb, :], start=(kb == 0), stop=(kb == 4 * qb + qs),
                                     reads=[PT, vaug], bank=bO[qs])
                    pend = None
                    for kb in range(4 * qb + 4):
                        PT = front(kb)
                        if pend is not None:
                            back(*pend)
                        pend = (kb, PT)
                    back(*pend)
                    for qs in range(4):
                        tb = 4 * qb + qs
                        rec, ogo = recr.next(), ogor.next()
                        K.op("dve", lambda e, o=rec[:, :], i=bO[qs][:, 128:129]: e.reciprocal(out=o, in_=i), reads=[bO[qs]], writes=[rec])
                        K.op("dve", lambda e, o=ogo[:, :], i0=bO[qs][:, 0:128], a=rec[:, 0:1], i1=sg[:, tb, :]:
                             e.scalar_tensor_tensor(out=o, in0=i0, scalar=a, in1=i1, op0=ALU.mult, op1=ALU.mult),
                             reads=[bO[qs], rec, sg], writes=[ogo])
                        if h == 0 and s == 0 and tb == 0:
                            self.dump(ogo, ogo[:, :], 3)
                        bank = nbank("P")
                        bvw = bank.t[:, :].bitcast(BF16)
                        K.op("pe", lambda e, o=bvw[:, 0:128], i=ogo[:, :]: e.transpose(o, i, self.ident[:, :]),
                             reads=[ogo, self.ident], writes=[bank])
                        K.op("act", lambda e, o=oTh[:, tb * 128:(tb + 1) * 128], i=bvw[:, 0:128]: e.copy(out=o, in_=i),
                             reads=[bank], pwrites=[oTh])
                g0 = s * SEQ
                K.dma("sp", self.OT[h * 128:(h + 1) * 128, g0:g0 + TS], oTh[:, :], oTh, reads=[oTh],
                      pwrites=ot_deps[g0 // 512:(g0 + TS) // 512])
            self.release(mk)
        mk = self.mark()
        wring = self.ring(2, [128, 16, 512], BF16, "wpan")
        uTr = self.ring(2, [128, 16, 512], BF16, "uT")
        xrr = self.ring(3, [128, 512], F32, "xr")
        xor_ = self.ring(3, [128, 512], F32, "xo")
        wov = self.b_w_out[0].rearrange("(kc p) n -> p kc n", p=128)
        otv = self.OT.rearrange("(kc p) t -> p kc t", p=128)
        for s in range(NSEQ):
            for tt in range(TBS // 4):
                g0 = s * SEQ + tt * 512
                uT = uTr.next()
                K.dma("sp", uT[:, :, :], otv[:, 0:16, g0:g0 + 512], uT, reads=[ot_deps[g0 // 512]], writes=[uT])
                self.out_proj_tile(wov, 16, uT, 4, g0 // 128, src, dst, wring, xrr, xor_)
        self.release(mk)

    def mixer_c(self, li, src, dst):
        K = self.K
        TBS, NSEQ = self.TBS, self.NSEQ
        TS = TBS * 128
        NQB = TS // 512
        bP = self.banks[0:2]
        cnt = {"P": 0, "Q": 0, "P2": 0}

        def nbank():
            b = bP[cnt["P"] % 2]
            cnt["P"] += 1
            return b

        qdeps = [[Dep() for _ in range(4)] for _ in range(8)]

        def nq():
            if QMODE == 1:
                i = cnt["Q"] % 6
                cnt["Q"] += 1
                return self.banks[2 + i].t[:, 0:128], qdeps[2 + i][0]
            i = cnt["Q"] % 24
            cnt["Q"] += 1
            b, q = 2 + i // 4, i % 4
            return self.banks[b].t[:, q * 128:(q + 1) * 128], qdeps[b][q]

        wiv = self.c_w_in[0].rearrange("(kc p) n -> p kc n", p=128)
        mk_top = self.mark()
        hT = self.sb([128, 16, TS], BF16, "hT")
        hdeps = [Dep() for _ in range(TBS)]
        ones_bf = self.sb([128, 128], BF16, "ones")
        ones_f = self.sb([128, 128], F32, "onesf")
        triu = self.sb([128, 128], F32, "triu")
        nmask = self.sb([128, 128], F32, "nmask")
        strict = self.sb([128, 128], F32, "strict")
        onw_bc = self.sb([128, 128], F32, "onw")
        dtb_bc = self.sb([128, 32], F32, "dtb")
        nea_bc = self.sb([128, 32], F32, "nea")
        cwc = self.sb([128, 4, 64], F32, "cwc")
        K.op("dve", lambda e: e.memset(ones_bf[:, :], 1.0), writes=[ones_bf])
        K.op("dve", lambda e: e.memset(ones_f[:, :], 1.0), writes=[ones_f])
        K.dma("sp", triu[:, :], self.consts[:, 256:384], triu, writes=[triu])
        K.op("dve", lambda e: e.tensor_scalar(out=nmask[:, :], in0=triu[:, :], scalar1=-1.0, scalar2=30000.0, op0=ALU.add, op1=ALU.mult),
             reads=[triu], writes=[nmask])
        K.op("dve", lambda e: e.tensor_tensor(out=strict[:, :], in0=triu[:, :], in1=self.ident_f[:, :], op=ALU.subtract),
             reads=[triu, self.ident_f], writes=[strict])
        K.dma("sp", onw_bc[:, :], self.c_out_norm[0, :].partition_broadcast(128), onw_bc, writes=[onw_bc])
        K.dma("sp", dtb_bc[:, :], self.c_dt_bias[0, :].partition_broadcast(128), dtb_bc, writes=[dtb_bc])
        K.dma("sp", nea_bc[:, :], self.c_a_log[0, :].partition_broadcast(128), nea_bc, writes=[nea_bc])
        K.op("act", lambda e: e.activation(out=nea_bc[:, :], in_=nea_bc[:, :], func=AF.Exp), reads=[nea_bc], writes=[nea_bc])
        K.op("dve", lambda e: e.tensor_scalar(out=nea_bc[:, :], in0=nea_bc[:, :], scalar1=-1.0, scalar2=None, op0=ALU.mult),
             reads=[nea_bc], writes=[nea_bc])
        mkc = self.mark()
        tmpw = self.sb([64, 4, 128], F32, "tmpw")
        K.dma("sp", tmpw[:, :, :], self.c_conv_w[0].rearrange("k (c p) -> c k p", p=128), tmpw, writes=[tmpw])
        for k in range(4):
            bank = nbank()
            K.op("pe", lambda e, o=bank[:, 0:64], i=tmpw[:, k, :]: e.transpose(o, i, self.ident_f[0:64, 0:64]),
                 reads=[tmpw, self.ident_f], writes=[bank])
            K.op("act", lambda e, o=cwc[:, k, :], i=bank[:, 0:64]: e.copy(out=o, in_=i), reads=[bank], pwrites=[cwc])
        self.release(mkc)
        ot_deps = [Dep() for _ in range(NT // 512)]
        for s in range(NSEQ):
            mk = self.mark()
            st = self.norm_setup(self.norm_mix[li, :])
            for tb in range(TBS):
                self.norm_T(st, src, s * 16 + tb, hT, hdeps[tb], tb * 128)
            self.release(mk)
            mk = self.mark()
            if CSTOP <= 1:
                self.release(mk)
                continue
            names = ("bet", "gcs", "egc", "ekd", "egl")
            S_ = {n: self.sb([128, TBS, 32], F32, n) for n in names}
            tmp32 = self.ring(2, [128, 32], F32, "tmp32")
            g32 = self.ring(2, [128, 32], F32, "g32")
            mkW = self.mark()
            wab = self.sb([128, 16, 64], BF16, "wab")
            K.dma("pool", wab[:, :, :], wiv[:, :, 12288:12352], wab, writes=[wab])
            for tb in range(TBS):
                bank = nbank()
                for kc in range(16):
                    K.mm(bank[:, 0:64], hT[:, kc, tb * 128:(tb + 1) * 128], wab[:, kc, :], start=(kc == 0), stop=(kc == 15),
                         reads=[wab, hdeps[tb]], bank=bank)
                K.op("act", lambda e, o=S_["bet"][:, tb, :], i=bank[:, 0:32]: e.activation(out=o, in_=i, func=AF.Sigmoid),
                     reads=[bank], pwrites=[S_["bet"]])
                t_ = tmp32.next()
                K.op("dve", lambda e, o=t_[:, :], i0=bank[:, 32:64]: e.tensor_tensor(out=o, in0=i0, in1=dtb_bc[:, :], op=ALU.add),
                     reads=[bank, dtb_bc], writes=[t_])
                K.op("act", lambda e, o=t_[:, :]: e.activation(out=o, in_=o, func=AF.Softplus), reads=[t_], writes=[t_])
                gt = g32.next()
                K.op("dve", lambda e, o=gt[:, :], i0=t_[:, :]: e.tensor_tensor(out=o, in0=i0, in1=nea_bc[:, :], op=ALU.mult),
                     reads=[t_, nea_bc], writes=[gt])
                bank2 = nbank()
                K.op("pe", lambda e, o=bank2[:, 0:32], r=gt[:, :]: e.matmul(o, triu[:, :], r, start=True, stop=True),
                     reads=[triu, gt], writes=[bank2], mark=False)
                K.op("pe", lambda e, o=bank2[:, 32:64], r=gt[:, :]: e.matmul(o, ones_f[:, :], r, start=True, stop=True),
                     reads=[ones_f, gt], pwrites=[bank2])
                K.op("act", lambda e, o=S_["gcs"][:, tb, :], i=bank2[:, 0:32]: e.copy(out=o, in_=i), reads=[bank2], pwrites=[S_["gcs"]])
                K.op("act", lambda e, o=S_["egc"][:, tb, :], i=bank2[:, 0:32]: e.activation(out=o, in_=i, func=AF.Exp), reads=[bank2], pwrites=[S_["egc"]])
                K.op("act", lambda e, o=S_["egl"][:, tb, :], i=bank2[:, 32:64]: e.activation(out=o, in_=i, func=AF.Exp), reads=[bank2], pwrites=[S_["egl"]])
                t2 = tmp32.next()
                K.op("dve", lambda e, o=t2[:, :], i0=bank2[:, 32:64], i1=S_["gcs"][:, tb, :]: e.tensor_tensor(out=o, in0=i0, in1=i1, op=ALU.subtract),
                     reads=[bank2, S_["gcs"]], writes=[t2])
                K.op("act", lambda e, o=S_["ekd"][:, tb, :], i=t2[:, :]: e.activation(out=o, in_=i, func=AF.Exp), reads=[t2], pwrites=[S_["ekd"]])
            self.release(mkW)
            if CSTOP <= 2:
                self.release(mk)
                continue
            wh = self.sb([128, 16, 768], BF16, "wh")

            def load_wh(kh):
                for (c0, c1, w0) in ((kh * 128, kh * 128 + 128, 0), (2048 + kh * 128, 2048 + kh * 128 + 128, 128),
                                     (4096 + kh * 256, 4096 + kh * 256 + 256, 256), (8192 + kh * 256, 8192 + kh * 256 + 256, 512)):
                    K.dma("pool", wh[:, :, w0:w0 + (c1 - c0)], wiv[:, :, c0:c1], wh,
                          writes=[wh] if w0 == 0 else (), pwrites=[wh] if w0 else ())
            load_wh(0)
            qT = self.sb([128, TS], BF16, "qT")
            kT = self.sb([128, TS], BF16, "kT")
            ktm = self.sb([128, TBS, 128], BF16, "ktm")
            vtm = self.sb([128, TBS, 256], BF16, "vtm")
            zs = self.sb([128, TBS, 256], BF16, "zs")
            S32 = self.sb([128, 2, 128], F32, "S32")
            Sbr = self.ring(2, [128, 2, 128], BF16, "Sb")
            oall = self.sb([128, TBS, 2, 128], BF16, "oall")
            ssq = self.sb([128, TBS * 2 + 64], F32, "ssq")
            stgr = self.ring(1, [128, 1024], BF16, "stg")
            junkb = self.sb([128, 128], BF16, "junkb")
            WS = [128, 2, 2, 128]
            vnr = self.ring(2, [128, 2, 128], BF16, "vnew")
            o1t = self.sb([128, 2, 128], F32, "o1")
            ot = self.sb([128, 2, 128], F32, "o")
            cbanks = self.banks[2:8]

            def nqb():
                b = self.banks[cnt["Q"] % 5]
                cnt["Q"] += 1
                return b

            def nqb2():
                b = self.banks[5 + cnt["P2"] % 3]
                cnt["P2"] += 1
                return b

            def bc4(ap3):
                return ap3.unsqueeze(3).to_broadcast(WS)

            def bcj(ap3):
                return ap3.unsqueeze(2).to_broadcast(WS)

            def bcall(ap2):
                return ap2.unsqueeze(1).unsqueeze(1).to_broadcast(WS)

            def pe_quads(bank, fn):
                for ci in range(4):
                    emit, reads = fn(ci, bank.t[:, ci * 128:(ci + 1) * 128])
                    K.op("pe", emit, reads=reads, writes=[bank] if ci == 0 else (), pwrites=[bank] if ci else (), mark=(ci == 3))

            def w4(bank):
                return bank.t[:, :].rearrange("p (a b c) -> p a b c", a=2, b=2)

            for kh in range(KHN):
                mkP = self.mark()
                def abank():
                    b = self.banks[cnt["P"] % 8]
                    cnt["P"] += 1
                    return b

                def proj_chain(ci, w0, cch, delay):
                    for _ in range(delay):
                        yield
                    gs2 = [self.sb([128, 515], F32, "gs") for _ in range(2)]
                    t1 = self.sb([128, 512], F32, "t1")
                    sg = self.sb([128, 512], F32 if ci < 2 else BF16, "sg")
                    if ci < 2:
                        sq = self.sb([128, 512], BF16, "sq")
                        sd = self.sb([128, 512], F32, "sd")
                        rs = self.sb([128, 512], F32, "rs")
                    prev_gs = None
                    for tq in range(NQB):
                        bank = abank()
                        for kc in range(16):
                            K.mm(bank[:, :], wh[:, kc, w0:w0 + 128], hT[:, kc, tq * 512:(tq + 1) * 512], start=(kc == 0), stop=(kc == 15),
                                 reads=[wh] + hdeps[tq * 4:tq * 4 + 4], bank=bank)
                        gs = gs2[tq % 2]
                        K.op("act", lambda e, o=gs[:, 3:515], i=bank[:, :]: e.copy(out=o, in_=i), reads=[bank], writes=[gs])
                        if tq == 0:
                            K.op("dve", lambda e, o=gs[:, 0:3]: e.memset(o, 0.0), pwrites=[gs])
                        else:
                            K.op("dve", lambda e, o=gs[:, 0:3], i=prev_gs[:, 512:515]: e.tensor_copy(out=o, in_=i), reads=[prev_gs], pwrites=[gs])
                        prev_gs = gs
                        yield
                        K.op("dve", lambda e, o=t1[:, :], i=gs[:, 3:515], a=cwc[:, 3, cch:cch + 1]: e.tensor_scalar(out=o, in0=i, scalar1=a, scalar2=None, op0=ALU.mult),
                             reads=[gs, cwc], writes=[t1])
                        for k in (2, 1, 0):
                            K.op("dve", lambda e, o=t1[:, :], i=gs[:, k:k + 512], a=cwc[:, k, cch:cch + 1]:
                                 e.scalar_tensor_tensor(out=o, in0=i, scalar=a, in1=o, op0=ALU.mult, op1=ALU.add),
                                 reads=[gs, cwc], writes=[t1])
                        yield
                        if ci < 2:
                            dT = qT if ci == 0 else kT
                            K.op("act", lambda e, o=sg[:, :], i=t1[:, :]: e.activation(out=o, in_=i, func=AF.Silu), reads=[t1], writes=[sg])
                            K.op("act", lambda e, o=sq[:, :], i=sg[:, :]: e.activation(out=o, in_=i, func=AF.Square), reads=[sg], writes=[sq])
                            bank2 = abank()
                            K.mm(bank2[:, :], ones_bf[:, :], sq[:, :], start=True, stop=True, reads=[ones_bf, sq], bank=bank2)
                            K.op("act", lambda e, o=sd[:, :], i=bank2[:, :]: e.activation(out=o, in_=i, func=AF.Sqrt, bias=self.epsT[:, 0:1], scale=1.0),
                                 reads=[bank2, self.epsT], writes=[sd])
                            yield
                            K.op("dve", lambda e, o=rs[:, :], i=sd[:, :]: e.reciprocal(out=o, in_=i), reads=[sd], writes=[rs])
                            sc = (128.0 ** -0.5) if ci == 0 else 1.0
                            K.op("dve", lambda e, o=dT[:, tq * 512:(tq + 1) * 512], i0=sg[:, :], i1=rs[:, :], sc=sc:
                                 e.scalar_tensor_tensor(out=o, in0=i0, scalar=sc, in1=i1, op0=ALU.mult, op1=ALU.mult),
                                 reads=[sg, rs], pwrites=[dT])
                            if ci == 1:
                                bank3 = abank()
                                bvw = bank3.t[:, :].bitcast(BF16)
                                for tl in range(4):
                                    K.op("pe", lambda e, o=bvw[:, tl * 128:(tl + 1) * 128], i=kT[:, tq * 512 + tl * 128:tq * 512 + (tl + 1) * 128]:
                                         e.transpose(o, i, self.ident[:, :]), reads=[kT, self.ident],
                                         writes=[bank3] if tl == 0 else (), pwrites=[bank3] if tl else (), mark=(tl == 3))
                                K.op("act", lambda e, o=ktm[:, tq * 4:(tq + 1) * 4, :], i=bvw[:, 0:512].rearrange("p (a b) -> p a b", a=4): e.copy(out=o, in_=i),
                                     reads=[bank3], pwrites=[ktm])
                            yield
                        else:
                            K.op("act", lambda e, o=sg[:, :], i=t1[:, :]: e.activation(out=o, in_=i, func=AF.Silu), reads=[t1], writes=[sg])
                            bank3 = abank()
                            bvw = bank3.t[:, :].bitcast(BF16)
                            for tl in range(4):
                                K.op("pe", lambda e, o=bvw[:, tl * 128:(tl + 1) * 128], i=sg[:, tl * 128:(tl + 1) * 128]:
                                     e.transpose(o, i, self.ident[:, :]), reads=[sg, self.ident],
                                     writes=[bank3] if tl == 0 else (), pwrites=[bank3] if tl else (), mark=(tl == 3))
                            j = ci - 2
                            K.op("act", lambda e, o=vtm[:, tq * 4:(tq + 1) * 4, j * 128:(j + 1) * 128], i=bvw[:, 0:512].rearrange("p (a b) -> p a b", a=4): e.copy(out=o, in_=i),
                                 reads=[bank3], pwrites=[vtm])
                            yield
                            yield

                def z_chain():
                    for tb in range(TBS):
                        bank = abank()
                        for kc in range(16):
                            K.mm(bank[:, 0:256], hT[:, kc, tb * 128:(tb + 1) * 128], wh[:, kc, 512:768], start=(kc == 0), stop=(kc == 15),
                                 reads=[wh, hdeps[tb]], bank=bank)
                        K.op("act", lambda e, o=zs[:, tb, :], i=bank[:, 0:256]: e.activation(out=o, in_=i, func=AF.Silu), reads=[bank], pwrites=[zs])
                        K.op("dve", lambda e, o=zs[:, tb, :].rearrange("p (a b) -> p a b", a=2), i1=onw_bc[:, :].unsqueeze(1).to_broadcast([128, 2, 128]):
                             e.tensor_tensor(out=o, in0=o, in1=i1, op=ALU.mult), reads=[onw_bc], pwrites=[zs])
                        yield

                chains = [proj_chain(ci, w0, cch, ci) for ci, (w0, cch) in
                          enumerate(((0, kh), (128, 16 + kh), (256, 32 + 2 * kh), (384, 32 + 2 * kh + 1)))]
                chains.append(z_chain())
                live = list(chains)
                while live:
                    for g_ in list(live):
                        try:
                            next(g_)
                        except StopIteration:
                            live.remove(g_)
                self.release(mkP)
                if kh + 1 < KHN:
                    load_wh(kh + 1)
                if CSTOP <= 3:
                    continue
                mkR = self.mark()
                P1s = []
                for _ in range(2):
                    d_ = {n: self.sb(WS, F32, n) for n in ("GG", "dg", "tmp", "X", "XT", "Ra", "Rb", "Pa", "PTa")}
                    d_["Pb"], d_["PTb"] = d_["dg"], d_["GG"]
                    d_.update({n: self.sb(WS, BF16, n) for n in ("Rbf", "ke")})
                    P1s.append(d_)
                P2 = [dict(attnT=self.sb(WS, BF16, "attnT"), ub=self.sb(WS, F32, "ub"), wTb=self.sb(WS, BF16, "wTb"),
                           kd=self.sb(WS, BF16, "kd")) for _ in range(4)]
                hv0 = 2 * kh
                K.op("dve", lambda e: e.memset(ssq[:, :], 0.0), writes=[ssq])
                K.op("dve", lambda e: e.memset(S32[:, :, :], 0.0), writes=[S32])
                Sb0 = Sbr.next()
                K.op("dve", lambda e, o=Sb0[:, :, :]: e.memset(o, 0.0), writes=[Sb0])
                sbcur = [Sb0]

                def phase1(g, hv0=hv0):
                    tb0 = 2 * g
                    t = P1s[g % 2]
                    q = P2[g % 4]

                    def sv(name):
                        return S_[name][:, tb0:tb0 + 2, hv0:hv0 + 2]
                    bk = nqb()
                    for tbl in range(2):
                        ksl = slice((tb0 + tbl) * 128, (tb0 + tbl + 1) * 128)
                        K.op("pe", lambda e, o=bk.t[:, tbl * 128:(tbl + 1) * 128], l=kT[:, ksl]: e.matmul(o, l, l, start=True, stop=True),
                             reads=[kT], writes=[bk] if tbl == 0 else (), pwrites=[bk] if tbl else (), mark=False)
                    for tbl in range(2):
                        ksl = slice((tb0 + tbl) * 128, (tb0 + tbl + 1) * 128)
                        K.op("pe", lambda e, o=bk.t[:, (2 + tbl) * 128:(3 + tbl) * 128], l=kT[:, ksl], r=qT[:, ksl]: e.matmul(o, l, r, start=True, stop=True),
                             reads=[kT, qT], pwrites=[bk], mark=(tbl == 1))
                    K.op("act", lambda e, o=t["GG"][:, :, :, :], i=w4(bk): e.copy(out=o, in_=i), reads=[bk], writes=[t["GG"]])
                    yield
                    K.op("dve", lambda e, o=t["GG"][:, 0, :, :]: e.tensor_tensor(out=o, in0=o, in1=strict[:, :].unsqueeze(1).to_broadcast([128, 2, 128]), op=ALU.mult),
                         reads=[strict], writes=[t["GG"]])
                    K.op(POOLENG, lambda e, o=t["dg"][:, :, :, :], i1=bc4(sv("gcs")): e.tensor_tensor(out=o, in0=bcall(self.ident_f[:, :]), in1=i1, op=ALU.mult),
                         reads=[self.ident_f, S_["gcs"]], writes=[t["dg"]])
                    bk2 = nqb()
                    pe_quads(bk2, lambda ci, o: ((lambda e, o=o, r=t["dg"][:, ci // 2, ci % 2, :]: e.matmul(o, ones_f[:, :], r, start=True, stop=True)),
                                                 [ones_f, t["dg"]]))
                    K.op("dve", lambda e, o=t["tmp"][:, :, :, :], i0=w4(bk2): e.tensor_tensor(out=o, in0=i0, in1=bcall(nmask[:, :]), op=ALU.add),
                         reads=[bk2, nmask], writes=[t["tmp"]])
                    K.op("dve", lambda e, o=t["tmp"][:, :, :, :], i1=bc4(sv("gcs")): e.tensor_tensor(out=o, in0=o, in1=i1, op=ALU.subtract),
                         reads=[S_["gcs"]], writes=[t["tmp"]])
                    K.op("act", lambda e, o=t["tmp"][:, :, :, :]: e.activation(out=o, in_=o, func=AF.Exp), reads=[], writes=[t["tmp"]])
                    yield
                    E = t["tmp"]
                    K.op("dve", lambda e, o=q["attnT"][:, :, :, :], i0=bcj(t["GG"][:, 1, :, :]), i1=E[:, :, :, :]: e.tensor_tensor(out=o, in0=i0, in1=i1, op=ALU.mult),
                         reads=[t["GG"], E], writes=[q["attnT"]])
                    K.op("dve", lambda e, o=t["X"][:, :, :, :], i0=bcj(t["GG"][:, 0, :, :]), i1=E[:, :, :, :]: e.tensor_tensor(out=o, in0=i0, in1=i1, op=ALU.mult),
                         reads=[t["GG"], E], writes=[t["X"]])
                    K.op("dve", lambda e, o=t["X"][:, :, :, :], i1=bc4(sv("bet")): e.tensor_tensor(out=o, in0=o, in1=i1, op=ALU.mult),
                         reads=[S_["bet"]], writes=[t["X"]])
                    bk3 = nqb()
                    pe_quads(bk3, lambda ci, o: ((lambda e, o=o, i=t["X"][:, ci // 2, ci % 2, :]: e.transpose(o, i, self.ident_f[:, :])),
                                                 [t["X"], self.ident_f]))
                    K.op("act", lambda e, o=t["XT"][:, :, :, :], i=w4(bk3): e.copy(out=o, in_=i), reads=[bk3], writes=[t["XT"]])
                    K.op("dve", lambda e, o=t["Ra"][:, :, :, :], i1=t["X"][:, :, :, :]: e.tensor_tensor(out=o, in0=bcall(self.ident_f[:, :]), in1=i1, op=ALU.subtract),
                         reads=[self.ident_f, t["X"]], writes=[t["Ra"]])
                    yield
                    P, PT, R = t["X"], t["XT"], t["Ra"]

                    def square(P, PT, Pn, PTn, need_p):
                        bkT = nqb()
                        pe_quads(bkT, lambda ci, o: ((lambda e, o=o, l=P[:, ci // 2, ci % 2, :], r=PT[:, ci // 2, ci % 2, :]: e.matmul(o, l, r, start=True, stop=True)),
                                                     [P, PT]))
                        bkP = None
                        if need_p:
                            bkP = nqb()
                            pe_quads(bkP, lambda ci, o: ((lambda e, o=o, l=PT[:, ci // 2, ci % 2, :], r=P[:, ci // 2, ci % 2, :]: e.matmul(o, l, r, start=True, stop=True)),
                                                         [P, PT]))
                        return bkT, bkP

                    def evac(bkT, bkP, Pn, PTn):
                        K.op("act", lambda e, o=PTn[:, :, :, :], i=w4(bkT): e.copy(out=o, in_=i), reads=[bkT], writes=[PTn])
                        if bkP is not None:
                            K.op("act", lambda e, o=Pn[:, :, :, :], i=w4(bkP): e.copy(out=o, in_=i), reads=[bkP], writes=[Pn])
                    bkT, bkP = square(P, PT, t["Pa"], t["PTa"], True)
                    evac(bkT, bkP, t["Pa"], t["PTa"])
                    P, PT = t["Pa"], t["PTa"]
                    yield
                    for it in range(6):
                        Pn = t["Pb"] if it % 2 == 0 else t["Pa"]
                        PTn = t["PTb"] if it % 2 == 0 else t["PTa"]
                        Rn = t["Rb"] if it % 2 == 0 else t["Ra"]
                        bkR = nqb()
                        pe_quads(bkR, lambda ci, o: ((lambda e, o=o, l=PT[:, ci // 2, ci % 2, :], r=R[:, ci // 2, ci % 2, :]: e.matmul(o, l, r, start=True, stop=True)),
                                                     [PT, R]))
                        if it < 5:
                            bkT, bkP = square(P, PT, Pn, PTn, it < 4)
                        K.op("dve", lambda e, o=Rn[:, :, :, :], i0=w4(bkR), i1=R[:, :, :, :]: e.tensor_tensor(out=o, in0=i0, in1=i1, op=ALU.add),
                             reads=[bkR, R], writes=[Rn])
                        if it < 5:
                            evac(bkT, bkP, Pn, PTn)
                        P, PT, R = Pn, PTn, Rn
                        yield
                    K.op("act", lambda e, o=t["Rbf"][:, :, :, :], i=R[:, :, :, :]: e.copy(out=o, in_=i), reads=[R], writes=[t["Rbf"]])
                    K.op(POOLENG, lambda e, o=t["ke"][:, :, :, :], i0=bcj(ktm[:, tb0:tb0 + 2, :]), i1=bc4(sv("egc")): e.tensor_tensor(out=o, in0=i0, in1=i1, op=ALU.mult),
                         reads=[ktm, S_["egc"]], writes=[t["ke"]])
                    K.op(POOLENG, lambda e, o=q["kd"][:, :, :, :], i0=bcj(ktm[:, tb0:tb0 + 2, :]), i1=bc4(sv("ekd")): e.tensor_tensor(out=o, in0=i0, in1=i1, op=ALU.mult),
                         reads=[ktm, S_["ekd"]], writes=[q["kd"]])
                    yield
                    bkU = nqb()
                    pe_quads(bkU, lambda ci, o: ((lambda e, o=o, l=t["Rbf"][:, ci // 2, ci % 2, :], r=vtm[:, tb0 + ci // 2, (ci % 2) * 128:(ci % 2 + 1) * 128]:
                                                  e.matmul(o, l, r, start=True, stop=True)), [t["Rbf"], vtm]))
                    bkW = nqb()
                    pe_quads(bkW, lambda ci, o: ((lambda e, o=o, l=t["ke"][:, ci // 2, ci % 2, :], r=t["Rbf"][:, ci // 2, ci % 2, :]:
                                                  e.matmul(o, l, r, start=True, stop=True)), [t["Rbf"], t["ke"]]))
                    K.op("dve", lambda e, o=q["ub"][:, :, :, :], i0=w4(bkU), i1=bc4(sv("bet")): e.tensor_tensor(out=o, in0=i0, in1=i1, op=ALU.mult),
                         reads=[bkU, S_["bet"]], writes=[q["ub"]])
                    K.op("act", lambda e, o=q["wTb"][:, :, :, :], i=w4(bkW): e.copy(out=o, in_=i), reads=[bkW], writes=[q["wTb"]])
                    yield

                def phase2(g, hv0=hv0):
                    q = P2[g % 4]
                    for tbl in range(2):
                        tb = 2 * g + tbl
                        ksl = slice(tb * 128, (tb + 1) * 128)

                        def col(name, j, tb=tb):
                            return S_[name][:, tb, hv0 + j:hv0 + j + 1]
                        Sold = sbcur[0]
                        bkV = nqb2()
                        for j in range(2):
                            K.op("pe", lambda e, o=bkV.t[:, j * 128:(j + 1) * 128], l=q["wTb"][:, tbl, j, :], r=Sold[:, j, :]: e.matmul(o, l, r, start=True, stop=True),
                                 reads=[q["wTb"], Sold], writes=[bkV] if j == 0 else (), pwrites=[bkV] if j else (), mark=(j == 1))
                        vnew = vnr.next()
                        for j in range(2):
                            K.op("dve", lambda e, o=vnew[:, j, :], i0=bkV.t[:, j * 128:(j + 1) * 128], a=col("bet", j), i1=q["ub"][:, tbl, j, :]:
                                 e.scalar_tensor_tensor(out=o, in0=i0, scalar=a, in1=i1, op0=ALU.mult, op1=ALU.subtract),
                                 reads=[bkV, S_["bet"], q["ub"]], writes=[vnew] if j == 0 else (), pwrites=[vnew] if j else ())
                        bkS = bkV
                        for j in range(2):
                            K.op("pe", lambda e, o=bkS.t[:, j * 128:(j + 1) * 128], l=q["kd"][:, tbl, j, :], r=vnew[:, j, :]: e.matmul(o, l, r, start=True, stop=True),
                                 reads=[q["kd"], vnew], writes=[bkS] if j == 0 else (), pwrites=[bkS] if j else (), mark=(j == 1))
                        bkO = nqb2()
                        for j in range(2):
                            K.op("pe", lambda e, o=bkO.t[:, j * 128:(j + 1) * 128], l=qT[:, ksl], r=Sold[:, j, :]: e.matmul(o, l, r, start=True, stop=True),
                                 reads=[qT, Sold], writes=[bkO] if j == 0 else (), pwrites=[bkO] if j else (), mark=False)
                        for j in range(2):
                            K.op("pe", lambda e, o=bkO.t[:, (2 + j) * 128:(3 + j) * 128], l=q["attnT"][:, tbl, j, :], r=vnew[:, j, :]: e.matmul(o, l, r, start=True, stop=True),
                                 reads=[q["attnT"], vnew], pwrites=[bkO], mark=(j == 1))
                        for j in range(2):
                            K.op("dve", lambda e, o=S32[:, j, :], i1=bkS.t[:, j * 128:(j + 1) * 128], a=col("egl", j):
                                 e.scalar_tensor_tensor(out=o, in0=o, scalar=a, in1=i1, op0=ALU.mult, op1=ALU.subtract),
                                 reads=[bkS, S_["egl"]], writes=[S32])
                        Snew = Sbr.next()
                        K.op("act", lambda e, o=Snew[:, :, :], i=S32[:, :, :]: e.copy(out=o, in_=i), reads=[S32], writes=[Snew])
                        sbcur[0] = Snew
                        yield
                        for j in range(2):
                            K.op("act", lambda e, o=o1t[:, j, :], i=bkO.t[:, j * 128:(j + 1) * 128], a=col("egc", j): e.activation(out=o, in_=i, func=AF.Copy, scale=a),
                                 reads=[bkO, S_["egc"]], writes=[o1t] if j == 0 else (), pwrites=[o1t] if j else ())
                        K.op("dve", lambda e, o=oall[:, tb, :, :], i1=bkO.t[:, 256:512].rearrange("p (a b) -> p a b", a=2), i0=o1t[:, :, :]: e.tensor_tensor(out=o, in0=i0, in1=i1, op=ALU.subtract),
                             reads=[bkO, o1t], pwrites=[oall])
                        for j in range(2):
                            K.op("act", lambda e, o=junkb[:, :], i=oall[:, tb, j, :], a=ssq[:, 2 * tb + j:2 * tb + j + 1]: e.activation(out=o, in_=i, func=AF.Square, accum_out=a),
                                 reads=[oall], writes=[junkb], pwrites=[ssq])
                        yield

                NG = TBS // 2

                def lockstep(gens):
                    live = list(gens)
                    while live:
                        for g_ in list(live):
                            try:
                                next(g_)
                            except StopIteration:
                                live.remove(g_)
                        yield

                def chain(gens):
                    for g_ in gens:
                        yield from g_

                def delayed(gen_, n_):
                    for _ in range(n_):
                        yield
                    yield from gen_

                for _ in lockstep([phase1(0), delayed(phase1(1), STAG)]):
                    pass
                for p_ in range(NG // 2):
                    a = chain([phase2(2 * p_), phase2(2 * p_ + 1)])
                    b = lockstep([phase1(2 * p_ + 2), delayed(phase1(2 * p_ + 3), STAG)]) if 2 * p_ + 2 < NG else iter(())
                    a_live, b_live = True, True
                    while a_live or b_live:
                        if a_live:
                            try:
                                next(a)
                            except StopIteration:
                                a_live = False
                        for _ in range(ILV if a_live else 1000):
                            if b_live:
                                try:
                                    next(b)
                                except StopIteration:
                                    b_live = False
                            else:
                                break
                g0 = s * SEQ
                NC2 = TBS * 2
                K.op("act", lambda e, o=ssq[:, NC2:2 * NC2], i=ssq[:, 0:NC2]: e.activation(out=o, in_=i, func=AF.Ln, bias=self.epsT[:, 0:1], scale=1.0 / 128),
                     reads=[ssq, self.epsT], pwrites=[ssq])
                K.op("act", lambda e, o=ssq[:, NC2:2 * NC2]: e.activation(out=o, in_=o, func=AF.Exp, scale=-0.5), reads=[ssq], pwrites=[ssq])
                ov = oall[:, :, :, :].rearrange("p t j e -> p (t j) e")
                K.op("dve", lambda e, o=ov, i1=ssq[:, NC2:2 * NC2].unsqueeze(2).to_broadcast([128, NC2, 128]): e.tensor_tensor(out=o, in0=o, in1=i1, op=ALU.mult),
                     reads=[ssq], writes=[oall])
                K.op("dve", lambda e, o=oall[:, :, :, :].rearrange("p t j e -> p t (j e)"), i1=zs[:, :, :]: e.tensor_tensor(out=o, in0=o, in1=i1, op=ALU.mult),
                     reads=[zs], writes=[oall])
                for j in range(2):
                    hv = 2 * kh + j
                    G8 = min(8, TBS)
                    for t8 in range(TBS // G8):
                        bank = nqb2()
                        bvw = bank.t[:, :].bitcast(BF16)
                        for tl in range(G8):
                            K.op("pe", lambda e, o=bvw[:, tl * 128:(tl + 1) * 128], i=oall[:, t8 * G8 + tl, j, :]: e.transpose(o, i, self.ident[:, :]),
                                 reads=[oall, self.ident], writes=[bank] if tl == 0 else (), pwrites=[bank] if tl else (), mark=(tl == G8 - 1))
                        stg = stgr.next()
                        K.op("act", lambda e, o=stg[:, 0:G8 * 128], i=bvw[:, 0:G8 * 128]: e.copy(out=o, in_=i), reads=[bank], writes=[stg])
                        c0 = g0 + t8 * G8 * 128
                        K.dma("sp", self.OT[hv * 128:(hv + 1) * 128, c0:c0 + G8 * 128], stg[:, 0:G8 * 128], stg, reads=[stg],
                              pwrites=ot_deps[c0 // 512:(c0 + G8 * 128) // 512])
                self.release(mkR)
            self.release(mk)
        self.release(mk_top)
        mk = self.mark()
        wring = self.ring(2, [128, 32, 512], BF16, "wpan")
        uTr = self.ring(2, [128, 32, 512], BF16, "uT")
        xrr = self.ring(3, [128, 512], F32, "xr")
        xor_ = self.ring(3, [128, 512], F32, "xo")
        wov = self.c_w_out[0].rearrange("(kc p) n -> p kc n", p=128)
        otv = self.OT.rearrange("(kc p) t -> p kc t", p=128)
        for s in range(NSEQ):
            for tt in range(TBS // 4):
                g0 = s * SEQ + tt * 512
                uT = uTr.next()
                K.dma("sp", uT[:, :, :], otv[:, :, g0:g0 + 512], uT, reads=[ot_deps[g0 // 512]], writes=[uT])
                self.out_proj_tile(wov, 32, uT, 4, g0 // 128, src, dst, wring, xrr, xor_)
        self.release(mk)

ALL_SUBLAYERS = [("a", 0), ("ffn", 0), ("b", 1), ("ffn", 1), ("c", 2), ("ffn", 2), ("a", 3), ("ffn", 3)]


def make_consts():
    c = np.zeros((128, 1024), np.float32)
    c[:, 0:128] = np.eye(128, dtype=np.float32)
    c[:, 128:256] = np.tril(np.ones((128, 128), np.float32))
    c[:, 256:384] = np.triu(np.ones((128, 128), np.float32))
    return c


def build_nc(sublayers=ALL_SUBLAYERS, ntb_seq=16, nseq=2):
    nc = bass.Bass("TRN2", target_bir_lowering=False)
    with ExitStack() as es:
        p = Prog(nc, es, sublayers, ntb_seq, nseq)
        p.build()
    return nc


def kernel(**inputs):
    x = np.ascontiguousarray(inputs["x"], dtype=np.float32)
    nc = build_nc()
    consts = make_consts()
    shared = {k: np.ascontiguousarray(v, dtype=np.float32) for k, v in inputs.items() if k != "x"}
    shared["consts"] = consts
    in_maps = []
    for c in range(NCORES):
        m = dict(shared)
        m["x"] = x[2 * c:2 * c + 2].reshape(NT, D)
        in_maps.append(m)
    res = run_bass_kernel_spmd(nc, in_maps, core_ids=list(range(NCORES)))
    out = np.stack([r["out"].reshape(2, SEQ, D) for r in res.results], axis=0).reshape(16, SEQ, D)
    return out.astype(np.float32)
```

```python
import numpy as np
from contextlib import ExitStack
import concourse.bass as bass
import concourse.mybir as mybir
from concourse.bass_utils import run_bass_kernel_spmd

F32, BF16 = mybir.dt.float32, mybir.dt.bfloat16
AF = mybir.ActivationFunctionType
ALU = mybir.AluOpType
AX = mybir.AxisListType

NCORES = 8
D = 2048
NT = 4096
SEQ = 2048
FF = 5632
NFC = FF // 128
EPS = 1e-6
DEBUG = False
CSTOP = 99
ASTOP = 99
QMODE = 1
LAZY = False
POOLENG = "dve"
ILV = 1
STAG = 3
KHN = 16


class Dep:
    __slots__ = ("w", "r")

    def __init__(self):
        self.w = {}
        self.r = {}


class T(Dep):
    __slots__ = ("t", "dsem")

    def __init__(self, t):
        Dep.__init__(self)
        self.t = t
        self.dsem = None

    def __getitem__(self, k):
        return self.t[k]


class Eng:
    def __init__(self, name, sem, is_pe=False):
        self.name, self.sem, self.is_pe = name, sem, is_pe
        self.count = 0
        self.seen = {}
        self.q = []


class DSem:
    def __init__(self, sem):
        self.sem = sem
        self.count = 0


class Ring:
    def __init__(self, tiles):
        self.tiles = tiles
        self.i = 0

    def next(self):
        t = self.tiles[self.i % len(self.tiles)]
        self.i += 1
        return t


def _merge(a, b):
    for k, v in b.items():
        if a.get(k, 0) < v:
            a[k] = v


class KB:
    def __init__(self, nc, es):
        self.nc, self.es = nc, es
        self.sems = {}
        self.E = {}
        for n in ("pe", "act", "dve", "pool", "sp"):
            self.E[n] = Eng(n, self.newsem("prog_" + n), is_pe=(n == "pe"))
        self.init_arena()

    def newsem(self, name):
        name = f"{name}_{len(self.sems)}"
        s = self.es.enter_context(self.nc.semaphore(name))
        self.sems[id(s)] = s
        return s

    ARENA_BYTES = 212480

    def init_arena(self):
        self.arena = self.es.enter_context(self.nc.sbuf_tensor("arena", [128, self.ARENA_BYTES // 2], BF16))
        self.off = 0
        self.dsem_pool = []
        self.live = []
        self.reuse = {}

    def sb(self, shape, dtype, name=None):
        esz = 4 if dtype == F32 else 2
        n = 1
        for d in shape[1:]:
            n *= d
        nbytes = (n * esz + 63) // 64 * 64
        assert self.off + nbytes <= self.ARENA_BYTES, f"SBUF arena overflow {name} {self.off + nbytes}"
        ap = self.arena[0:shape[0], self.off // 2:(self.off + n * esz) // 2]
        self.off += nbytes
        if dtype == F32:
            ap = ap.bitcast(F32)
        if len(shape) == 3:
            ap = ap.rearrange("p (a b) -> p a b", a=shape[1])
        elif len(shape) == 4:
            ap = ap.rearrange("p (a b c) -> p a b c", a=shape[1], b=shape[2])
        t = T(ap)
        t.r = dict(self.reuse)
        self.live.append(t)
        return t

    def ring(self, n, shape, dtype, name=None):
        return Ring([self.sb(shape, dtype, name) for _ in range(n)])

    def ps(self, name):
        t = self.es.enter_context(self.nc.psum_tensor(name, [128, 512], F32))
        return T(t)

    def get_dsem(self):
        if self.dsem_pool:
            return self.dsem_pool.pop()
        return DSem(self.newsem("d"))

    def _waits(self, E, reads, writes, pwrites, own_sid=None):
        need = {}
        for d in reads:
            _merge(need, d.w)
        for d in writes:
            _merge(need, d.w)
            _merge(need, d.r)
        for d in pwrites:
            _merge(need, d.r)
            for k, v in d.w.items():
                if k != own_sid and need.get(k, 0) < v:
                    need[k] = v
        for sid, val in need.items():
            if E.seen.get(sid, 0) >= val:
                continue
            if E.is_pe and sid == id(E.sem):
                continue
            E.seen[sid] = val
            E.q.append(("w", self.sems[sid], val))

    def _stamp(self, sid, val, reads, writes, pwrites):
        for d in reads:
            if d.r.get(sid, 0) < val:
                d.r[sid] = val
        for d in writes:
            d.w = {sid: val}
            d.r = {}
        for d in pwrites:
            if d.w.get(sid, 0) < val:
                d.w[sid] = val

    def op(self, en, emit, reads=(), writes=(), pwrites=(), mark=True):
        E = self.E[en]
        self._waits(E, reads, writes, pwrites, id(E.sem))
        if mark:
            E.count += 1
            E.q.append(("i", emit, E.sem, 1))
            val = E.count
        else:
            E.q.append(("i", emit, None, 0))
            val = E.count + 1
        self._stamp(id(E.sem), val, reads, writes, pwrites)

    def dma(self, qn, out, in_, sem_tile, reads=(), writes=(), pwrites=(), slow=False):
        E = self.E[qn]
        if sem_tile.dsem is None:
            sem_tile.dsem = self.get_dsem()
        ds = sem_tile.dsem
        self._waits(E, reads, writes, pwrites, id(ds.sem))
        ds.count += 16
        if slow:
            E.q.append(("i", lambda e, o=out, i=in_: e.dma_start(out=o, in_=i, allow_slow_non_contiguous=True), ds.sem, 16))
        else:
            E.q.append(("i", lambda e, o=out, i=in_: e.dma_start(out=o, in_=i), ds.sem, 16))
        self._stamp(id(ds.sem), ds.count, reads, writes, pwrites)

    def mm(self, out, lhsT, rhs, start, stop, reads, bank, mark=None):
        if mark is None:
            mark = stop
        self.op("pe", lambda e: e.matmul(out, lhsT, rhs, start=start, stop=stop),
                reads=reads, pwrites=[bank] if not start else (), writes=[bank] if start else (), mark=mark)

    def finish(self, final_deps):
        E = self.E["sp"]
        need = {}
        for d in final_deps:
            _merge(need, d.w)
        for sid, val in need.items():
            E.q.append(("w", self.sems[sid], val))
        nc = self.nc
        with nc.Block() as block:
            def run(E, e):
                for it in E.q:
                    if it[0] == "w":
                        e.wait_ge(it[1], it[2])
                    else:
                        ins = it[1](e)
                        if it[2] is not None:
                            ins.then_inc(it[2], it[3])

            @block.sync
            def _(e):
                run(self.E["sp"], e)

            @block.scalar
            def _(e):
                run(self.E["act"], e)

            @block.vector
            def _(e):
                run(self.E["dve"], e)

            @block.gpsimd
            def _(e):
                run(self.E["pool"], e)

            @block.tensor
            def _(e):
                run(self.E["pe"], e)


class Resid:
    def __init__(self, ap):
        self.ap = ap
        self.deps = [Dep() for _ in range(NT // 128)]


class Prog:
    def __init__(self, nc, es, sublayers, ntb_seq=16, nseq=2):
        self.nc, self.es = nc, es
        self.K = KB(nc, es)
        self.sublayers = sublayers
        self.TBS = ntb_seq
        self.NSEQ = nseq
        self.declare()

    def declare(self):
        nc = self.nc

        def inp(name, shape):
            return nc.dram_tensor(name, list(shape), F32, kind="ExternalInput").ap()

        self.x = inp("x", [NT, D])
        self.norm_mix = inp("norm_mix", [4, D])
        self.norm_ffn = inp("norm_ffn", [4, D])
        self.ffn_w_gate = inp("ffn_w_gate", [4, D, FF])
        self.ffn_w_up = inp("ffn_w_up", [4, D, FF])
        self.ffn_conv_w = inp("ffn_conv_w", [4, 3, FF])
        self.ffn_conv_b = inp("ffn_conv_b", [4, FF])
        self.ffn_w_down = inp("ffn_w_down", [4, FF, D])
        self.a_w_in = inp("a_w_in", [2, D, 2 * D])
        self.a_b_in = inp("a_b_in", [2, 2 * D])
        self.a_v_norm = inp("a_v_norm", [2, D])
        self.a_w_s = inp("a_w_s", [2, 16, 128, 128])
        self.a_b_s = inp("a_b_s", [2, 16, 128])
        self.a_w_out = inp("a_w_out", [2, D, D])
        self.b_w_in = inp("b_w_in", [1, D, 8208])
        self.b_b_f = inp("b_b_f", [1, 16])
        self.b_q_norm = inp("b_q_norm", [1, 128])
        self.b_k_norm = inp("b_k_norm", [1, 128])
        self.b_w_out = inp("b_w_out", [1, D, D])
        self.c_w_in = inp("c_w_in", [1, D, 12352])
        self.c_conv_w = inp("c_conv_w", [1, 4, 8192])
        self.c_a_log = inp("c_a_log", [1, 32])
        self.c_dt_bias = inp("c_dt_bias", [1, 32])
        self.c_out_norm = inp("c_out_norm", [1, 128])
        self.c_w_out = inp("c_w_out", [1, 2 * D, D])
        self.consts = inp("consts", [128, 1024])
        self.out = nc.dram_tensor("out", [NT, D], F32, kind="ExternalOutput").ap()
        self.XA = nc.dram_tensor("XA", [NT, D], F32).ap()
        self.AT = nc.dram_tensor("AT", [FF, NT], BF16).ap()
        self.OT = nc.dram_tensor("OT", [2 * D, NT], BF16).ap()
        self.CS = nc.dram_tensor("CS", [6, 16, SEQ], BF16).ap()
        if DEBUG:
            self.dbgf = nc.dram_tensor("dbgf", [8, 128, 2048], F32, kind="ExternalOutput").ap()
            self.dbgb = nc.dram_tensor("dbgb", [8, 128, 2048], BF16, kind="ExternalOutput").ap()
            self.dbg_deps = []

    def dump(self, t, ap, idx, f32=False):
        if not DEBUG:
            return
        dst = self.dbgf if f32 else self.dbgb
        shp = ap.shape
        d = Dep()
        self.dbg_deps.append(d)
        self.K.dma("sp", dst[idx, 0:shp[0], 0:shp[1]], ap, t, reads=[t], writes=[d])

    def build(self):
        K = self.K
        self.banks = [K.ps(f"bank{i}") for i in range(8)]
        self.ident_f = K.sb([128, 128], F32, "identf")
        self.ident = K.sb([128, 128], BF16, "ident")
        self.epsT = K.sb([128, 1], F32, "eps")
        K.dma("sp", self.ident_f[:, :], self.consts[:, 0:128], self.ident_f, writes=[self.ident_f])
        K.op("dve", lambda e: e.tensor_copy(out=self.ident[:, :], in_=self.ident_f[:, :]),
             reads=[self.ident_f], writes=[self.ident])
        K.op("dve", lambda e: e.memset(self.epsT[:, :], EPS), writes=[self.epsT])

        res_in = Resid(self.x)
        res_a = Resid(self.XA)
        res_out = Resid(self.out)
        n = len(self.sublayers)
        cur = res_in
        for idx, (kind, li) in enumerate(self.sublayers):
            dst = res_out if idx == n - 1 else res_a
            mk = self.mark()
            if kind == "ffn":
                self.ffn(li, cur, dst)
            elif kind == "a":
                self.mixer_a(li // 3, li, cur, dst)
            elif kind == "b":
                self.mixer_b(li, cur, dst)
            elif kind == "c":
                self.mixer_c(li, cur, dst)
            else:
                raise NotImplementedError(kind)
            self.release(mk)
            cur = dst
        K.finish(res_out.deps + (self.dbg_deps if DEBUG else []))

    def mark(self):
        return (self.K.off, len(self.K.live))

    def release(self, mk):
        K = self.K
        if not LAZY:
            self.barrier_free(mk[1])
            K.off = mk[0]
            return
        dead = K.live[mk[1]:]
        del K.live[mk[1]:]
        for t in dead:
            _merge(K.reuse, t.w)
            _merge(K.reuse, t.r)
            if t.dsem is not None:
                sid = id(t.dsem.sem)
                if K.reuse.get(sid, 0) < t.dsem.count:
                    K.reuse[sid] = t.dsem.count
                K.dsem_pool.append(t.dsem)
                t.dsem = None
        K.off = mk[0]

    def barrier_free(self, mark_live):
        K = self.K
        dead = K.live[mark_live:]
        del K.live[mark_live:]
        need = {}
        for O in K.E.values():
            if O.count:
                need[id(O.sem)] = O.count
        for t in dead:
            _merge(need, t.w)
            _merge(need, t.r)
            if t.dsem is not None:
                need[id(t.dsem.sem)] = max(need.get(id(t.dsem.sem), 0), t.dsem.count)
                K.dsem_pool.append(t.dsem)
                t.dsem = None
        for en, E in K.E.items():
            for sid, val in need.items():
                if sid == id(E.sem):
                    continue
                if E.seen.get(sid, 0) < val:
                    E.seen[sid] = val
                    E.q.append(("w", K.sems[sid], val))

    def sb(self, shape, dtype, name=None):
        return self.K.sb(shape, dtype, name)

    def ring(self, n, shape, dtype, name=None):
        return self.K.ring(n, shape, dtype, name)

    def norm_setup(self, gamma_row):
        K = self.K
        st = {}
        st["gam"] = self.sb([128, D], F32, "gam")
        K.dma("sp", st["gam"][:, :], gamma_row.partition_broadcast(128), st["gam"], writes=[st["gam"]])
        st["xs"] = self.ring(2, [128, D], F32, "xs")
        st["hn"] = self.ring(2, [128, D], BF16, "hn")
        st["junk"] = self.sb([128, D], BF16, "junk")
        st["ss"] = self.ring(4, [128, 4], F32, "ss")
        return st

    def norm_T(self, st, src, tb_glob, hT, hdep, tcol):
        K = self.K
        xs = st["xs"].next()
        hn = st["hn"].next()
        ss = st["ss"].next()
        gam, junk = st["gam"], st["junk"]
        K.dma("sp", xs[:, :], src.ap[tb_glob * 128:(tb_glob + 1) * 128, :], xs,
              reads=[src.deps[tb_glob]], writes=[xs])
        K.op("dve", lambda e: e.memset(ss[:, :], 0.0), writes=[ss])
        K.op("act", lambda e: e.activation(out=junk[:, :], in_=xs[:, :], func=AF.Square, accum_out=ss[:, 0:1]),
             reads=[xs], writes=[junk], pwrites=[ss])
        K.op("act", lambda e: e.activation(out=ss[:, 1:2], in_=ss[:, 0:1], func=AF.Sqrt,
                                           bias=self.epsT[:, 0:1], scale=1.0 / D),
             reads=[ss, self.epsT], pwrites=[ss])
        K.op("dve", lambda e: e.reciprocal(out=ss[:, 2:3], in_=ss[:, 1:2]), reads=[ss], pwrites=[ss])
        K.op("dve", lambda e: e.scalar_tensor_tensor(out=hn[:, :], in0=xs[:, :], scalar=ss[:, 2:3], in1=gam[:, :],
                                                      op0=ALU.mult, op1=ALU.mult),
             reads=[xs, ss, gam], writes=[hn])
        for half in range(2):
            bank = self.banks[self._tbank % 8]
            self._tbank += 1
            bv = bank.t[:, :].bitcast(BF16)
            for j in range(8):
                kc = half * 8 + j
                K.op("pe", lambda e, o=bv[:, j * 128:(j + 1) * 128], i=hn[:, kc * 128:(kc + 1) * 128]:
                     e.transpose(o, i, self.ident[:, :]),
                     reads=[hn, self.ident], writes=[bank] if j == 0 else (), pwrites=[bank] if j else (),
                     mark=(j == 7))
            eng = "act" if half == 0 else "dve"
            o = hT[:, half * 8:(half + 1) * 8, tcol:tcol + 128]
            i = bv.rearrange("p (k t) -> p k t", k=8)
            if eng == "act":
                K.op("act", lambda e, o=o, i=i: e.copy(out=o, in_=i), reads=[bank], pwrites=[hdep])
            else:
                K.op("dve", lambda e, o=o, i=i: e.tensor_copy(out=o, in_=i), reads=[bank], pwrites=[hdep])

    _tbank = 0

    def ffn(self, li, src, dst):
        K = self.K
        TBS, NSEQ = self.TBS, self.NSEQ
        TS = TBS * 128
        HW = min(1024, TS)
        NH = TS // HW
        mk0 = self.mark()
        st = self.norm_setup(self.norm_ffn[li, :])
        hT = self.sb([128, 16, TS], BF16, "hT")
        hdeps = [Dep() for _ in range(TBS)]
        cw = self.sb([128, 3, NFC], F32, "cw")
        cb = self.sb([128, NFC], F32, "cb")
        for k in range(3):
            K.dma("sp", cw[:, k, :], self.ffn_conv_w[li, k, :].rearrange("(c p) -> p c", p=128), cw, pwrites=[cw], slow=True)
        K.dma("sp", cb[:, :], self.ffn_conv_b[li, :].rearrange("(c p) -> p c", p=128), cb, writes=[cb], slow=True)
        PW = 256
        wgr = self.ring(2, [128, 16, PW], BF16, "wg")
        wur = self.ring(2, [128, 16, PW], BF16, "wu")
        gsr = self.ring(2, [128, HW + 2], F32, "gs")
        t1r = self.ring(2, [128, HW], F32, "t1")
        sgr = self.ring(2, [128, HW], F32, "sg")
        aTr = self.ring(3, [128, HW], BF16, "aT")
        NTT = NT // 256
        at_deps = [Dep() for _ in range(NTT)]
        wgv = self.ffn_w_gate[li].rearrange("(kc p) n -> p kc n", p=128)
        wuv = self.ffn_w_up[li].rearrange("(kc p) n -> p kc n", p=128)
        bset = 0
        for s in range(NSEQ):
            for tb in range(TBS):
                self.norm_T(st, src, s * 16 + tb, hT, hdeps[tb], tb * 128)
            for fp in range(FF // PW):
                wg, wu = wgr.next(), wur.next()
                K.dma("pool", wg[:, :, :], wgv[:, :, fp * PW:(fp + 1) * PW], wg, writes=[wg])
                K.dma("pool", wu[:, :, :], wuv[:, :, fp * PW:(fp + 1) * PW], wu, writes=[wu])
                for fcl in range(PW // 128):
                    fc = fp * (PW // 128) + fcl
                    prev_gs = None
                    for h in range(NH):
                        nb = HW // 512
                        bks = self.banks[bset * 4:(bset + 1) * 4]
                        bset ^= 1
                        psG, psU = bks[0:nb], bks[2:2 + nb]
                        for (w, ps) in ((wg, psG), (wu, psU)):
                            for kc in range(16):
                                for b in range(nb):
                                    t0 = h * HW + b * 512
                                    K.mm(ps[b][:, :], w[:, kc, fcl * 128:(fcl + 1) * 128], hT[:, kc, t0:t0 + 512],
                                         start=(kc == 0), stop=(kc == 15),
                                         reads=[w] + hdeps[t0 // 128:t0 // 128 + 4], bank=ps[b])
                        gs, t1, sg, aT = gsr.next(), t1r.next(), sgr.next(), aTr.next()
                        for b in range(nb):
                            K.op("act", lambda e, o=gs[:, 2 + b * 512:2 + (b + 1) * 512], i=psG[b][:, :]: e.copy(out=o, in_=i),
                                 reads=[psG[b]], writes=[gs] if b == 0 else (), pwrites=[gs] if b else ())
                        if h == 0:
                            K.op("dve", lambda e, o=gs[:, 0:2]: e.memset(o, 0.0), pwrites=[gs])
                        else:
                            K.op("dve", lambda e, o=gs[:, 0:2], i=prev_gs[:, HW:HW + 2]: e.tensor_copy(out=o, in_=i),
                                 reads=[prev_gs], pwrites=[gs])
                        prev_gs = gs
                        K.op("dve", lambda e, o=t1[:, :], i=gs[:, 2:HW + 2], a=cw[:, 2, fc:fc + 1], b_=cb[:, fc:fc + 1]:
                             e.tensor_scalar(out=o, in0=i, scalar1=a, scalar2=b_, op0=ALU.mult, op1=ALU.add),
                             reads=[gs, cw, cb], writes=[t1])
                        K.op("dve", lambda e, o=t1[:, :], i=gs[:, 1:HW + 1], a=cw[:, 1, fc:fc + 1]:
                             e.scalar_tensor_tensor(out=o, in0=i, scalar=a, in1=o, op0=ALU.mult, op1=ALU.add),
                             reads=[gs, cw], writes=[t1])
                        K.op("dve", lambda e, o=t1[:, :], i=gs[:, 0:HW], a=cw[:, 0, fc:fc + 1]:
                             e.scalar_tensor_tensor(out=o, in0=i, scalar=a, in1=o, op0=ALU.mult, op1=ALU.add),
                             reads=[gs, cw], writes=[t1])
                        K.op("act", lambda e, o=sg[:, :], i=t1[:, :]: e.activation(out=o, in_=i, func=AF.Silu),
                             reads=[t1], writes=[sg])
                        for b in range(nb):
                            K.op("dve", lambda e, o=aT[:, b * 512:(b + 1) * 512], i0=psU[b][:, :], i1=sg[:, b * 512:(b + 1) * 512]:
                                 e.tensor_tensor(out=o, in0=i0, in1=i1, op=ALU.mult),
                                 reads=[psU[b], sg], writes=[aT] if b == 0 else (), pwrites=[aT] if b else ())
                        g0 = s * SEQ + h * HW
                        K.dma("sp", self.AT[fc * 128:(fc + 1) * 128, g0:g0 + HW], aT[:, :], aT,
                              reads=[aT], pwrites=at_deps[g0 // 256:(g0 + HW) // 256])
        self.release(mk0)
        wdr = self.ring(2, [128, NFC, 512], BF16, "wd")
        atr = self.ring(2, [128, NFC, 256], BF16, "at")
        xrr = self.ring(3, [128, 512], F32, "xr")
        xor_ = self.ring(3, [128, 512], F32, "xo")
        wdv = self.ffn_w_down[li].rearrange("(fc p) n -> p fc n", p=128)
        atv = self.AT.rearrange("(fc p) t -> p fc t", p=128)
        bi = 0
        for dp in range(4):
            wd = wdr.next()
            for q in range(4):
                K.dma("pool", wd[:, q * 11:(q + 1) * 11, :], wdv[:, q * 11:(q + 1) * 11, dp * 512:(dp + 1) * 512], wd,
                      writes=[wd] if q == 0 else (), pwrites=[wd] if q else ())
            for s in range(NSEQ):
                for tt in range(TS // 256):
                    g0 = s * SEQ + tt * 256
                    at = atr.next()
                    K.dma("sp", at[:, :, :], atv[:, :, g0:g0 + 256], at, reads=[at_deps[g0 // 256]], writes=[at])
                    for tb in range(2):
                        gb = g0 // 128 + tb
                        bank = self.banks[bi % 8]
                        bi += 1
                        for fc in range(NFC):
                            K.mm(bank[:, :], at[:, fc, tb * 128:(tb + 1) * 128], wd[:, fc, :], start=(fc == 0),
                                 stop=(fc == NFC - 1), reads=[at, wd], bank=bank)
                        xr, xo = xrr.next(), xor_.next()
                        K.dma("sp", xr[:, :], src.ap[gb * 128:(gb + 1) * 128, dp * 512:(dp + 1) * 512], xr,
                              reads=[src.deps[gb]], writes=[xr])
                        K.op("dve", lambda e, o=xo[:, :], i0=bank[:, :], i1=xr[:, :]: e.tensor_tensor(out=o, in0=i0, in1=i1, op=ALU.add),
                             reads=[bank, xr], writes=[xo])
                        K.dma("act", dst.ap[gb * 128:(gb + 1) * 128, dp * 512:(dp + 1) * 512], xo[:, :], xo,
                              reads=[xo], pwrites=[dst.deps[gb]])

    def out_proj_tile(self, wv, KC, uT, ntb, g_tb0, src, dst, wring, xrr, xor_):
        K = self.K
        for p in range(4):
            w = wring.next()
            K.dma("pool", w[:, 0:KC, :], wv[:, :, p * 512:(p + 1) * 512], w, writes=[w])
            for tb in range(ntb):
                gb = g_tb0 + tb
                bank = self.banks[self._tbank % 8]
                self._tbank += 1
                for kc in range(KC):
                    K.mm(bank[:, :], uT[:, kc, tb * 128:(tb + 1) * 128], w[:, kc, :], start=(kc == 0), stop=(kc == KC - 1),
                         reads=[uT, w], bank=bank)
                xr, xo = xrr.next(), xor_.next()
                K.dma("sp", xr[:, :], src.ap[gb * 128:(gb + 1) * 128, p * 512:(p + 1) * 512], xr,
                      reads=[src.deps[gb]], writes=[xr])
                K.op("dve", lambda e, o=xo[:, :], i0=bank[:, :], i1=xr[:, :]: e.tensor_tensor(out=o, in0=i0, in1=i1, op=ALU.add),
                     reads=[bank, xr], writes=[xo])
                K.dma("act", dst.ap[gb * 128:(gb + 1) * 128, p * 512:(p + 1) * 512], xo[:, :], xo,
                      reads=[xo], pwrites=[dst.deps[gb]])

    def mixer_a(self, j, li, src, dst):
        K = self.K
        TBS, NSEQ = self.TBS, self.NSEQ
        st = self.norm_setup(self.norm_mix[li, :])
        hT = self.sb([128, 16, 512], BF16, "hT")
        hdeps = [Dep() for _ in range(4)]
        uT = self.sb([128, 16, 512], BF16, "uT")
        vr = [self.sb([128, D], BF16, "v") for _ in range(4)]
        wring = self.ring(2, [128, 16, 512], BF16, "wpan")
        bs_bc = self.sb([128, D], F32, "bs_bc")
        bv_bc = self.sb([128, D], F32, "bv_bc")
        b_u = self.sb([128, 16], F32, "b_u")
        vn_col = self.sb([128, 16], F32, "vn_col")
        WcT = self.sb([128, D], BF16, "WcT")
        WcSr = self.ring(2, [128, D], BF16, "WcS")
        triu = self.sb([128, 128], F32, "triu")
        xbr = self.ring(2, [128, 512], F32, "xb")
        v32r = self.ring(2, [128, 512], F32, "v32")
        t32r = self.ring(2, [128, 512], F32, "t32")
        xrr = self.ring(3, [128, 512], F32, "xr")
        xor_ = self.ring(3, [128, 512], F32, "xo")
        ssr = self.ring(4, [128, 8], F32, "ssv")
        junk = st["junk"]
        K.dma("sp", bs_bc[:, :], self.a_b_s[j].rearrange("g t -> (g t)").partition_broadcast(128), bs_bc, writes=[bs_bc])
        K.dma("sp", bv_bc[:, :], self.a_b_in[j, D:2 * D].partition_broadcast(128), bv_bc, writes=[bv_bc])
        K.dma("sp", b_u[:, :], self.a_b_in[j, 0:D].rearrange("(c p) -> p c", p=128), b_u, writes=[b_u], slow=True)
        K.dma("sp", vn_col[:, :], self.a_v_norm[j, :].rearrange("(c p) -> p c", p=128), vn_col, writes=[vn_col], slow=True)
        K.dma("sp", triu[:, :], self.consts[:, 256:384], triu, writes=[triu])
        mk = self.mark()
        wsraw = self.sb([128, 16, 128], F32, "wsraw")
        K.dma("sp", wsraw[:, :, :], self.a_w_s[j].rearrange("g t s -> t g s"), wsraw, writes=[wsraw])
        for g in range(16):
            bank = self.banks[self._tbank % 8]
            self._tbank += 1
            K.op("pe", lambda e, o=bank[:, 0:128], i=wsraw[:, g, :]: e.transpose(o, i, self.ident_f[:, :]),
                 reads=[wsraw, self.ident_f], writes=[bank])
            K.op("dve", lambda e, o=WcT[:, g * 128:(g + 1) * 128], i0=bank[:, 0:128], i1=triu[:, :]:
                 e.tensor_tensor(out=o, in0=i0, in1=i1, op=ALU.mult), reads=[bank, triu], pwrites=[WcT])
        self.release(mk)
        wiv = self.a_w_in[j].rearrange("(kc p) n -> p kc n", p=128)
        wov = self.a_w_out[j].rearrange("(kc p) n -> p kc n", p=128)
        for s in range(NSEQ):
            for tt in range(TBS // 4):
                gtb0 = s * 16 + tt * 4
                for tb in range(4):
                    self.norm_T(st, src, gtb0 + tb, hT, hdeps[tb], tb * 128)
                for p in range(4):
                    w = wring.next()
                    K.dma("pool", w[:, :, :], wiv[:, :, p * 512:(p + 1) * 512], w, writes=[w])
                    for fcl in range(4):
                        fc = p * 4 + fcl
                        bank = self.banks[self._tbank % 8]
                        self._tbank += 1
                        for kc in range(16):
                            K.mm(bank[:, :], w[:, kc, fcl * 128:(fcl + 1) * 128], hT[:, kc, :], start=(kc == 0), stop=(kc == 15),
                                 reads=[w] + hdeps, bank=bank)
                        K.op("act", lambda e, o=uT[:, fc, :], i=bank[:, :], b=b_u[:, fc:fc + 1]:
                             e.activation(out=o, in_=i, func=AF.Gelu_apprx_tanh, bias=b),
                             reads=[bank, b_u], pwrites=[uT])
                sss = [ssr.next() for _ in range(4)]
                for tb in range(4):
                    K.op("dve", lambda e, o=sss[tb][:, :]: e.memset(o, 0.0), writes=[sss[tb]])
                for p in range(4):
                    w = wring.next()
                    K.dma("pool", w[:, :, :], wiv[:, :, D + p * 512:D + (p + 1) * 512], w, writes=[w])
                    for tb in range(4):
                        bank = self.banks[self._tbank % 8]
                        self._tbank += 1
                        for kc in range(16):
                            K.mm(bank[:, :], hT[:, kc, tb * 128:(tb + 1) * 128], w[:, kc, :], start=(kc == 0), stop=(kc == 15),
                                 reads=[w, hdeps[tb]], bank=bank)
                        xb, v32 = xbr.next(), v32r.next()
                        K.op("dve", lambda e, o=xb[:, :], i0=bank[:, :], i1=bv_bc[:, p * 512:(p + 1) * 512]:
                             e.tensor_tensor(out=o, in0=i0, in1=i1, op=ALU.add), reads=[bank, bv_bc], writes=[xb])
                        K.op("act", lambda e, o=v32[:, :], i=xb[:, :]: e.activation(out=o, in_=i, func=AF.Gelu_apprx_tanh),
                             reads=[xb], writes=[v32])
                        K.op("act", lambda e, o=junk[:, 0:512], i=v32[:, :], a=sss[tb][:, p:p + 1]:
                             e.activation(out=o, in_=i, func=AF.Square, accum_out=a),
                             reads=[v32], writes=[junk], pwrites=[sss[tb]])
                        K.op("dve", lambda e, o=vr[tb][:, p * 512:(p + 1) * 512], i=v32[:, :]: e.tensor_copy(out=o, in_=i),
                             reads=[v32], pwrites=[vr[tb]])
                for tb in range(4):
                    ss = sss[tb]
                    K.op("dve", lambda e, o=ss[:, 4:5], i=ss[:, 0:4]: e.reduce_sum(out=o, in_=i, axis=AX.X),
                         reads=[ss], pwrites=[ss])
                    K.op("act", lambda e, o=ss[:, 5:6], i=ss[:, 4:5]: e.activation(out=o, in_=i, func=AF.Sqrt,
                                                                                 bias=self.epsT[:, 0:1], scale=1.0 / D),
                         reads=[ss, self.epsT], pwrites=[ss])
                    K.op("dve", lambda e, o=ss[:, 6:7], i=ss[:, 5:6]: e.reciprocal(out=o, in_=i), reads=[ss], pwrites=[ss])
                    WcS = WcSr.next()
                    K.op("dve", lambda e, o=WcS[:, :], i=WcT[:, :], a=ss[:, 6:7]: e.tensor_scalar(out=o, in0=i, scalar1=a, scalar2=None, op0=ALU.mult),
                         reads=[WcT, ss], writes=[WcS])
                    for gq in range(4):
                        bank = self.banks[self._tbank % 8]
                        self._tbank += 1
                        t32 = t32r.next()
                        for gl in range(4):
                            g = gq * 4 + gl
                            K.op("pe", lambda e, o=bank[:, gl * 128:(gl + 1) * 128], l=vr[tb][:, g * 128:(g + 1) * 128], r=WcS[:, g * 128:(g + 1) * 128]:
                                 e.matmul(o, l, r, start=True, stop=True),
                                 reads=[vr[tb], WcS], writes=[bank] if gl == 0 else (), pwrites=[bank] if gl else (), mark=(gl == 3))
                        for gl in range(4):
                            g = gq * 4 + gl
                            K.op("dve", lambda e, o=t32[:, gl * 128:(gl + 1) * 128], i0=bank[:, gl * 128:(gl + 1) * 128], a=vn_col[:, g:g + 1], i1=bs_bc[:, g * 128:(g + 1) * 128]:
                                 e.scalar_tensor_tensor(out=o, in0=i0, scalar=a, in1=i1, op0=ALU.mult, op1=ALU.add),
                                 reads=[bank, vn_col, bs_bc], writes=[t32] if gl == 0 else (), pwrites=[t32] if gl else ())
                        uv = uT[:, gq * 4:(gq + 1) * 4, tb * 128:(tb + 1) * 128]
                        K.op("dve", lambda e, o=uv, i0=uv, i1=t32[:, :].rearrange("p (g t) -> p g t", g=4):
                             e.tensor_tensor(out=o, in0=i0, in1=i1, op=ALU.mult), reads=[t32, uT], pwrites=[uT])
                self.out_proj_tile(wov, 16, uT, 4, gtb0, src, dst, wring, xrr, xor_)

    def mixer_b(self, li, src, dst):
        K = self.K
        TBS, NSEQ = self.TBS, self.NSEQ
        TS = TBS * 128
        NQB = TS // 512
        bS, bP, bO = self.banks[0:2], self.banks[2:4], self.banks[4:8]
        cnt = {"S": 0, "P": 0}

        def nbank(role):
            lst = bS if role == "S" else bP
            b = lst[cnt[role] % 2]
            cnt[role] += 1
            return b

        hT = self.sb([128, 16, TS], BF16, "hT")
        hdeps = [Dep() for _ in range(TBS)]
        qn_col = self.sb([128, 1], F32, "qn_col")
        kn_col = self.sb([128, 1], F32, "kn_col")
        negbf = self.sb([16, 1], F32, "negbf")
        ones_bf = self.sb([128, 128], BF16, "ones")
        negmask = self.sb([128, 128], F32, "negmask")
        K.dma("sp", qn_col[:, :], self.b_q_norm[0, :].rearrange("(p o) -> p o", o=1), qn_col, writes=[qn_col], slow=True)
        K.dma("sp", kn_col[:, :], self.b_k_norm[0, :].rearrange("(p o) -> p o", o=1), kn_col, writes=[kn_col], slow=True)
        K.dma("sp", negbf[:, :], self.b_b_f[0, :].rearrange("(p o) -> p o", o=1), negbf, writes=[negbf], slow=True)
        K.dma("sp", negmask[:, :], self.consts[:, 256:384], negmask, writes=[negmask])
        K.op("dve", lambda e: e.tensor_scalar(out=qn_col[:, :], in0=qn_col[:, :], scalar1=128.0 ** -0.5, scalar2=None, op0=ALU.mult),
             reads=[qn_col], writes=[qn_col])
        K.op("dve", lambda e: e.tensor_scalar(out=negbf[:, :], in0=negbf[:, :], scalar1=-1.0, scalar2=None, op0=ALU.mult),
             reads=[negbf], writes=[negbf])
        K.op("dve", lambda e: e.tensor_scalar(out=negmask[:, :], in0=negmask[:, :], scalar1=-1.0, scalar2=30000.0, op0=ALU.add, op1=ALU.mult),
             reads=[negmask], writes=[negmask])
        K.op("dve", lambda e: e.memset(ones_bf[:, :], 1.0), writes=[ones_bf])
        wiv = self.b_w_in[0].rearrange("(kc p) n -> p kc n", p=128)
        cs_dep = Dep()
        ot_deps = [Dep() for _ in range(NT // 512)]
        for s in range(NSEQ):
            mk = self.mark()
            st = self.norm_setup(self.norm_mix[li, :])
            for tb in range(TBS):
                self.norm_T(st, src, s * 16 + tb, hT, hdeps[tb], tb * 128)
            self.release(mk)
            mk = self.mark()
            wfl = self.sb([128, 16, 16], BF16, "wfl")
            K.dma("pool", wfl[:, :, :], wiv[:, :, 8192:8208], wfl, writes=[wfl])
            spt = self.sb([16, TS], F32, "spt")
            ct = self.sb([16, TS], F32, "ct")
            r1 = self.sb([16, TS], F32, "r1")
            parts = [self.sb([16, TS], BF16, "cp") for _ in range(6)]
            for tq in range(NQB):
                bank = nbank("P")
                for kc in range(16):
                    K.mm(bank[0:16, :], wfl[:, kc, :], hT[:, kc, tq * 512:(tq + 1) * 512], start=(kc == 0), stop=(kc == 15),
                         reads=[wfl] + hdeps[tq * 4:tq * 4 + 4], bank=bank)
                K.op("act", lambda e, o=spt[:, tq * 512:(tq + 1) * 512], i=bank[0:16, :]:
                     e.activation(out=o, in_=i, func=AF.Softplus, bias=negbf[:, 0:1], scale=-1.0),
                     reads=[bank, negbf], pwrites=[spt])
            K.op("dve", lambda e: e.tensor_scalar(out=spt[:, :], in0=spt[:, :], scalar1=-0.5, scalar2=None, op0=ALU.mult),
                 reads=[spt], writes=[spt])
            K.op("dve", lambda e: e.tensor_tensor_scan(out=ct[:, :], data0=spt[:, :], data1=spt[:, :], initial=0.0, op0=ALU.add, op1=ALU.add),
                 reads=[spt], writes=[ct])
            self.dump(ct, ct[:, :], 0, f32=True)
            cur = ct
            for i3 in range(3):
                p_ = parts[3 + i3]
                K.op("dve", lambda e, o=p_[:, :], i=cur[:, :]: e.tensor_copy(out=o, in_=i), reads=[cur], writes=[p_])
                K.op("dve", lambda e, o=parts[i3][:, :], i=p_[:, :]: e.tensor_scalar(out=o, in0=i, scalar1=-1.0, scalar2=None, op0=ALU.mult),
                     reads=[p_], writes=[parts[i3]])
                if i3 < 2:
                    K.op("dve", lambda e, o=r1[:, :], i0=cur[:, :], i1=p_[:, :]: e.tensor_tensor(out=o, in0=i0, in1=i1, op=ALU.subtract),
                         reads=[cur, p_], writes=[r1])
                    cur = r1
            for i6 in range(6):
                K.dma("sp", self.CS[i6, :, 0:TS], parts[i6][:, :], parts[i6], reads=[parts[i6]],
                      writes=[cs_dep] if i6 == 0 else (), pwrites=[cs_dep] if i6 else ())
            self.release(mk)
            mk = self.mark()
            whr = self.ring(2, [128, 4, 16, 128], BF16, "wh")
            LKr = self.ring(2, [6, TS], BF16, "LK")
            RQr = self.ring(2, [6, TS], BF16, "RQ")
            for t_ in LKr.tiles + RQr.tiles:
                K.op("dve", lambda e, o=t_[:, :]: e.memset(o, 1.0), writes=[t_])
            qTr = self.ring(2, [128, TS], BF16, "qT")
            kTr = self.ring(2, [128, TS], BF16, "kT")
            vaugr = self.ring(2, [128, TBS, 129], BF16, "vaug")
            for t_ in vaugr.tiles:
                K.op("dve", lambda e, o=t_[:, :, :]: e.memset(o, 1.0), writes=[t_])
            sgr = self.ring(2, [128, TBS, 128], BF16, "sg")
            PTr = self.ring(4, [128, 512], BF16, "PT")
            oThr = self.ring(2, [128, TS], BF16, "oTh")
            sqr = self.ring(4, [128, 512], BF16, "sq")
            sdr = self.ring(4, [128, 512], F32, "sd")
            rsr = self.ring(4, [128, 512], F32, "rs")
            dtr = self.ring(2, [128, 128], F32, "dtmp")
            recr = self.ring(4, [128, 1], F32, "rec")
            ogor = self.ring(2, [128, 128], BF16, "ogo")
            for h in range(16):
                wh = whr.next()
                for m in range(4):
                    K.dma("pool", wh[:, m, :, :], wiv[:, :, m * 2048 + h * 128:m * 2048 + (h + 1) * 128], wh,
                          writes=[wh] if m == 0 else (), pwrites=[wh] if m else ())
                LK, RQ = LKr.next(), RQr.next()
                K.dma("sp", LK[0:3, :], self.CS[0:3, h, 0:TS], LK, reads=[cs_dep], pwrites=[LK])
                K.dma("sp", RQ[3:6, :], self.CS[3:6, h, 0:TS], RQ, reads=[cs_dep], pwrites=[RQ])
                qT, kT = qTr.next(), kTr.next()

                def qk_chain(m, dT, col, bks, delay):
                    for _ in range(delay):
                        yield
                    for tq in range(NQB):
                        bank, bank2 = bks
                        for kc in range(16):
                            K.mm(bank[:, :], wh[:, m, kc, :], hT[:, kc, tq * 512:(tq + 1) * 512], start=(kc == 0), stop=(kc == 15),
                                 reads=[wh] + hdeps[tq * 4:tq * 4 + 4], bank=bank)
                        sq, sd, rs = sqr.next(), sdr.next(), rsr.next()
                        K.op("act", lambda e, o=sq[:, :], i=bank[:, :]: e.activation(out=o, in_=i, func=AF.Square),
                             reads=[bank], writes=[sq])
                        yield
                        K.mm(bank2[:, :], ones_bf[:, :], sq[:, :], start=True, stop=True, reads=[ones_bf, sq], bank=bank2)
                        K.op("act", lambda e, o=sd[:, :], i=bank2[:, :]: e.activation(out=o, in_=i, func=AF.Sqrt, bias=self.epsT[:, 0:1], scale=1.0 / 128),
                             reads=[bank2, self.epsT], writes=[sd])
                        yield
                        K.op("dve", lambda e, o=rs[:, :], i=sd[:, :]: e.reciprocal(out=o, in_=i), reads=[sd], writes=[rs])
                        K.op("dve", lambda e, o=dT[:, tq * 512:(tq + 1) * 512], i0=bank[:, :], a=col[:, 0:1], i1=rs[:, :]:
                             e.scalar_tensor_tensor(out=o, in0=i0, scalar=a, in1=i1, op0=ALU.mult, op1=ALU.mult),
                             reads=[bank, col, rs], pwrites=[dT])
                        yield
                live = [qk_chain(0, qT, qn_col, (bP[0], bP[1]), 0), qk_chain(1, kT, kn_col, (bO[0], bO[1]), 1)]
                while live:
                    for g_ in list(live):
                        try:
                            next(g_)
                        except StopIteration:
                            live.remove(g_)
                vaug, sg = vaugr.next(), sgr.next()
                vcnt = 0
                for (m, which) in ((2, "v"), (3, "g")):
                    for tb4 in range(TBS // 4):
                        bank = bO[2 + vcnt % 2]
                        vcnt += 1
                        for tl in range(4):
                            tb = tb4 * 4 + tl
                            for kc in range(16):
                                K.op("pe", lambda e, o=bank[:, tl * 128:(tl + 1) * 128], l=hT[:, kc, tb * 128:(tb + 1) * 128], r=wh[:, m, kc, :], st_=(kc == 0), sp_=(kc == 15):
                                     e.matmul(o, l, r, start=st_, stop=sp_),
                                     reads=[wh, hdeps[tb]], writes=[bank] if (tl == 0 and kc == 0) else (),
                                     pwrites=() if (tl == 0 and kc == 0) else [bank], mark=(tl == 3 and kc == 15))
                        bv = bank[:, :].rearrange("p (a b) -> p a b", a=4)
                        if which == "v":
                            K.op("act", lambda e, o=vaug[:, tb4 * 4:(tb4 + 1) * 4, 0:128], i=bv: e.copy(out=o, in_=i),
                                 reads=[bank], pwrites=[vaug])
                        else:
                            K.op("act", lambda e, o=sg[:, tb4 * 4:(tb4 + 1) * 4, :], i=bv: e.activation(out=o, in_=i, func=AF.Sigmoid),
                                 reads=[bank], pwrites=[sg])
                oTh = oThr.next()
                for qb in range(NQB):
                    def front(kb, qb=qb):
                        o_ = max(0, kb - 4 * qb)
                        c0 = o_ * 128
                        sbk = nbank("S")
                        K.mm(sbk[:, c0:512], kT[:, kb * 128:(kb + 1) * 128], qT[:, qb * 512 + c0:(qb + 1) * 512], start=True, stop=False,
                             reads=[kT, qT], bank=sbk)
                        K.mm(sbk[:, c0:512], LK[0:6, kb * 128:(kb + 1) * 128], RQ[0:6, qb * 512 + c0:(qb + 1) * 512], start=False, stop=True,
                             reads=[LK, RQ], bank=sbk)
                        PT = PTr.next()
                        if kb >= 4 * qb:
                            dt_ = dtr.next()
                            K.op("dve", lambda e, o=dt_[:, :], i0=sbk[:, c0:c0 + 128]: e.tensor_tensor(out=o, in0=i0, in1=negmask[:, :], op=ALU.add),
                                 reads=[sbk, negmask], writes=[dt_])
                            K.op("act", lambda e, o=PT[:, c0:c0 + 128], i=dt_[:, :]: e.activation(out=o, in_=i, func=AF.Exp),
                                 reads=[dt_], writes=[PT])
                            if c0 + 128 < 512:
                                K.op("act", lambda e, o=PT[:, c0 + 128:512], i=sbk[:, c0 + 128:512]: e.activation(out=o, in_=i, func=AF.Exp),
                                     reads=[sbk], pwrites=[PT])
                        else:
                            K.op("act", lambda e, o=PT[:, :], i=sbk[:, :]: e.activation(out=o, in_=i, func=AF.Exp),
                                 reads=[sbk], writes=[PT])
                        return PT

                    def back(kb, PT, qb=qb):
                        for qs in range(4):
                            if kb <= 4 * qb + qs:
                                K.mm(bO[qs][:, 0:129], PT[:, qs * 128:(qs + 1) * 128], vaug[:, kb, :], start=(kb == 0), stop=(kb == 4 * qb + qs),
                                     reads=[PT, vaug], bank=bO[qs])
                    pend = None
                    for kb in range(4 * qb + 4):
                        PT = front(kb)
                        if pend is not None:
                            back(*pend)
                        pend = (kb, PT)
                    back(*pend)
                    for qs in range(4):
                        tb = 4 * qb + qs
                        rec, ogo = recr.next(), ogor.next()
                        K.op("dve", lambda e, o=rec[:, :], i=bO[qs][:, 128:129]: e.reciprocal(out=o, in_=i), reads=[bO[qs]], writes=[rec])
                        K.op("dve", lambda e, o=ogo[:, :], i0=bO[qs][:, 0:128], a=rec[:, 0:1], i1=sg[:, tb, :]:
                             e.scalar_tensor_tensor(out=o, in0=i0, scalar=a, in1=i1, op0=ALU.mult, op1=ALU.mult),
                             reads=[bO[qs], rec, sg], writes=[ogo])
                        if h == 0 and s == 0 and tb == 0:
                            self.dump(ogo, ogo[:, :], 3)
                        bank = nbank("P")
                        bvw = bank.t[:, :].bitcast(BF16)
                        K.op("pe", lambda e, o=bvw[:, 0:128], i=ogo[:, :]: e.transpose(o, i, self.ident[:, :]),
                             reads=[ogo, self.ident], writes=[bank])
                        K.op("act", lambda e, o=oTh[:, tb * 128:(tb + 1) * 128], i=bvw[:, 0:128]: e.copy(out=o, in_=i),
                             reads=[bank], pwrites=[oTh])
                g0 = s * SEQ
                K.dma("sp", self.OT[h * 128:(h + 1) * 128, g0:g0 + TS], oTh[:, :], oTh, reads=[oTh],
                      pwrites=ot_deps[g0 // 512:(g0 + TS) // 512])
            self.release(mk)
        mk = self.mark()
        wring = self.ring(2, [128, 16, 512], BF16, "wpan")
        uTr = self.ring(2, [128, 16, 512], BF16, "uT")
        xrr = self.ring(3, [128, 512], F32, "xr")
        xor_ = self.ring(3, [128, 512], F32, "xo")
        wov = self.b_w_out[0].rearrange("(kc p) n -> p kc n", p=128)
        otv = self.OT.rearrange("(kc p) t -> p kc t", p=128)
        for s in range(NSEQ):
            for tt in range(TBS // 4):
                g0 = s * SEQ + tt * 512
                uT = uTr.next()
                K.dma("sp", uT[:, :, :], otv[:, 0:16, g0:g0 + 512], uT, reads=[ot_deps[g0 // 512]], writes=[uT])
                self.out_proj_tile(wov, 16, uT, 4, g0 // 128, src, dst, wring, xrr, xor_)
        self.release(mk)

    def mixer_c(self, li, src, dst):
        K = self.K
        TBS, NSEQ = self.TBS, self.NSEQ
        TS = TBS * 128
        NQB = TS // 512
        bP = self.banks[0:2]
        cnt = {"P": 0, "Q": 0, "P2": 0}

        def nbank():
            b = bP[cnt["P"] % 2]
            cnt["P"] += 1
            return b

        qdeps = [[Dep() for _ in range(4)] for _ in range(8)]

        def nq():
            if QMODE == 1:
                i = cnt["Q"] % 6
                cnt["Q"] += 1
                return self.banks[2 + i].t[:, 0:128], qdeps[2 + i][0]
            i = cnt["Q"] % 24
            cnt["Q"] += 1
            b, q = 2 + i // 4, i % 4
            return self.banks[b].t[:, q * 128:(q + 1) * 128], qdeps[b][q]

        wiv = self.c_w_in[0].rearrange("(kc p) n -> p kc n", p=128)
        mk_top = self.mark()
        hT = self.sb([128, 16, TS], BF16, "hT")
        hdeps = [Dep() for _ in range(TBS)]
        ones_bf = self.sb([128, 128], BF16, "ones")
        ones_f = self.sb([128, 128], F32, "onesf")
        triu = self.sb([128, 128], F32, "triu")
        nmask = self.sb([128, 128], F32, "nmask")
        strict = self.sb([128, 128], F32, "strict")
        onw_bc = self.sb([128, 128], F32, "onw")
        dtb_bc = self.sb([128, 32], F32, "dtb")
        nea_bc = self.sb([128, 32], F32, "nea")
        cwc = self.sb([128, 4, 64], F32, "cwc")
        K.op("dve", lambda e: e.memset(ones_bf[:, :], 1.0), writes=[ones_bf])
        K.op("dve", lambda e: e.memset(ones_f[:, :], 1.0), writes=[ones_f])
        K.dma("sp", triu[:, :], self.consts[:, 256:384], triu, writes=[triu])
        K.op("dve", lambda e: e.tensor_scalar(out=nmask[:, :], in0=triu[:, :], scalar1=-1.0, scalar2=30000.0, op0=ALU.add, op1=ALU.mult),
             reads=[triu], writes=[nmask])
        K.op("dve", lambda e: e.tensor_tensor(out=strict[:, :], in0=triu[:, :], in1=self.ident_f[:, :], op=ALU.subtract),
             reads=[triu, self.ident_f], writes=[strict])
        K.dma("sp", onw_bc[:, :], self.c_out_norm[0, :].partition_broadcast(128), onw_bc, writes=[onw_bc])
        K.dma("sp", dtb_bc[:, :], self.c_dt_bias[0, :].partition_broadcast(128), dtb_bc, writes=[dtb_bc])
        K.dma("sp", nea_bc[:, :], self.c_a_log[0, :].partition_broadcast(128), nea_bc, writes=[nea_bc])
        K.op("act", lambda e: e.activation(out=nea_bc[:, :], in_=nea_bc[:, :], func=AF.Exp), reads=[nea_bc], writes=[nea_bc])
        K.op("dve", lambda e: e.tensor_scalar(out=nea_bc[:, :], in0=nea_bc[:, :], scalar1=-1.0, scalar2=None, op0=ALU.mult),
             reads=[nea_bc], writes=[nea_bc])
        mkc = self.mark()
        tmpw = self.sb([64, 4, 128], F32, "tmpw")
        K.dma("sp", tmpw[:, :, :], self.c_conv_w[0].rearrange("k (c p) -> c k p", p=128), tmpw, writes=[tmpw])
        for k in range(4):
            bank = nbank()
            K.op("pe", lambda e, o=bank[:, 0:64], i=tmpw[:, k, :]: e.transpose(o, i, self.ident_f[0:64, 0:64]),
                 reads=[tmpw, self.ident_f], writes=[bank])
            K.op("act", lambda e, o=cwc[:, k, :], i=bank[:, 0:64]: e.copy(out=o, in_=i), reads=[bank], pwrites=[cwc])
        self.release(mkc)
        ot_deps = [Dep() for _ in range(NT // 512)]
        for s in range(NSEQ):
            mk = self.mark()
            st = self.norm_setup(self.norm_mix[li, :])
            for tb in range(TBS):
                self.norm_T(st, src, s * 16 + tb, hT, hdeps[tb], tb * 128)
            self.release(mk)
            mk = self.mark()
            if CSTOP <= 1:
                self.release(mk)
                continue
            names = ("bet", "gcs", "egc", "ekd", "egl")
            S_ = {n: self.sb([128, TBS, 32], F32, n) for n in names}
            tmp32 = self.ring(2, [128, 32], F32, "tmp32")
            g32 = self.ring(2, [128, 32], F32, "g32")
            mkW = self.mark()
            wab = self.sb([128, 16, 64], BF16, "wab")
            K.dma("pool", wab[:, :, :], wiv[:, :, 12288:12352], wab, writes=[wab])
            for tb in range(TBS):
                bank = nbank()
                for kc in range(16):
                    K.mm(bank[:, 0:64], hT[:, kc, tb * 128:(tb + 1) * 128], wab[:, kc, :], start=(kc == 0), stop=(kc == 15),
                         reads=[wab, hdeps[tb]], bank=bank)
                K.op("act", lambda e, o=S_["bet"][:, tb, :], i=bank[:, 0:32]: e.activation(out=o, in_=i, func=AF.Sigmoid),
                     reads=[bank], pwrites=[S_["bet"]])
                t_ = tmp32.next()
                K.op("dve", lambda e, o=t_[:, :], i0=bank[:, 32:64]: e.tensor_tensor(out=o, in0=i0, in1=dtb_bc[:, :], op=ALU.add),
                     reads=[bank, dtb_bc], writes=[t_])
                K.op("act", lambda e, o=t_[:, :]: e.activation(out=o, in_=o, func=AF.Softplus), reads=[t_], writes=[t_])
                gt = g32.next()
                K.op("dve", lambda e, o=gt[:, :], i0=t_[:, :]: e.tensor_tensor(out=o, in0=i0, in1=nea_bc[:, :], op=ALU.mult),
                     reads=[t_, nea_bc], writes=[gt])
                bank2 = nbank()
                K.op("pe", lambda e, o=bank2[:, 0:32], r=gt[:, :]: e.matmul(o, triu[:, :], r, start=True, stop=True),
                     reads=[triu, gt], writes=[bank2], mark=False)
                K.op("pe", lambda e, o=bank2[:, 32:64], r=gt[:, :]: e.matmul(o, ones_f[:, :], r, start=True, stop=True),
                     reads=[ones_f, gt], pwrites=[bank2])
                K.op("act", lambda e, o=S_["gcs"][:, tb, :], i=bank2[:, 0:32]: e.copy(out=o, in_=i), reads=[bank2], pwrites=[S_["gcs"]])
                K.op("act", lambda e, o=S_["egc"][:, tb, :], i=bank2[:, 0:32]: e.activation(out=o, in_=i, func=AF.Exp), reads=[bank2], pwrites=[S_["egc"]])
                K.op("act", lambda e, o=S_["egl"][:, tb, :], i=bank2[:, 32:64]: e.activation(out=o, in_=i, func=AF.Exp), reads=[bank2], pwrites=[S_["egl"]])
                t2 = tmp32.next()
                K.op("dve", lambda e, o=t2[:, :], i0=bank2[:, 32:64], i1=S_["gcs"][:, tb, :]: e.tensor_tensor(out=o, in0=i0, in1=i1, op=ALU.subtract),
                     reads=[bank2, S_["gcs"]], writes=[t2])
                K.op("act", lambda e, o=S_["ekd"][:, tb, :], i=t2[:, :]: e.activation(out=o, in_=i, func=AF.Exp), reads=[t2], pwrites=[S_["ekd"]])
            self.release(mkW)
            if CSTOP <= 2:
                self.release(mk)
                continue
            wh = self.sb([128, 16, 768], BF16, "wh")

            def load_wh(kh):
                for (c0, c1, w0) in ((kh * 128, kh * 128 + 128, 0), (2048 + kh * 128, 2048 + kh * 128 + 128, 128),
                                     (4096 + kh * 256, 4096 + kh * 256 + 256, 256), (8192 + kh * 256, 8192 + kh * 256 + 256, 512)):
                    K.dma("pool", wh[:, :, w0:w0 + (c1 - c0)], wiv[:, :, c0:c1], wh,
                          writes=[wh] if w0 == 0 else (), pwrites=[wh] if w0 else ())
            load_wh(0)
            qT = self.sb([128, TS], BF16, "qT")
            kT = self.sb([128, TS], BF16, "kT")
            ktm = self.sb([128, TBS, 128], BF16, "ktm")
            vtm = self.sb([128, TBS, 256], BF16, "vtm")
            zs = self.sb([128, TBS, 256], BF16, "zs")
            S32 = self.sb([128, 2, 128], F32, "S32")
            Sbr = self.ring(2, [128, 2, 128], BF16, "Sb")
            oall = self.sb([128, TBS, 2, 128], BF16, "oall")
            ssq = self.sb([128, TBS * 2 + 64], F32, "ssq")
            stgr = self.ring(1, [128, 1024], BF16, "stg")
            junkb = self.sb([128, 128], BF16, "junkb")
            WS = [128, 2, 2, 128]
            vnr = self.ring(2, [128, 2, 128], BF16, "vnew")
            o1t = self.sb([128, 2, 128], F32, "o1")
            ot = self.sb([128, 2, 128], F32, "o")
            cbanks = self.banks[2:8]

            def nqb():
                b = self.banks[cnt["Q"] % 5]
                cnt["Q"] += 1
                return b

            def nqb2():
                b = self.banks[5 + cnt["P2"] % 3]
                cnt["P2"] += 1
                return b

            def bc4(ap3):
                return ap3.unsqueeze(3).to_broadcast(WS)

            def bcj(ap3):
                return ap3.unsqueeze(2).to_broadcast(WS)

            def bcall(ap2):
                return ap2.unsqueeze(1).unsqueeze(1).to_broadcast(WS)

            def pe_quads(bank, fn):
                for ci in range(4):
                    emit, reads = fn(ci, bank.t[:, ci * 128:(ci + 1) * 128])
                    K.op("pe", emit, reads=reads, writes=[bank] if ci == 0 else (), pwrites=[bank] if ci else (), mark=(ci == 3))

            def w4(bank):
                return bank.t[:, :].rearrange("p (a b c) -> p a b c", a=2, b=2)

            for kh in range(KHN):
                mkP = self.mark()
                def abank():
                    b = self.banks[cnt["P"] % 8]
                    cnt["P"] += 1
                    return b

                def proj_chain(ci, w0, cch, delay):
                    for _ in range(delay):
                        yield
                    gs2 = [self.sb([128, 515], F32, "gs") for _ in range(2)]
                    t1 = self.sb([128, 512], F32, "t1")
                    sg = self.sb([128, 512], F32 if ci < 2 else BF16, "sg")
                    if ci < 2:
                        sq = self.sb([128, 512], BF16, "sq")
                        sd = self.sb([128, 512], F32, "sd")
                        rs = self.sb([128, 512], F32, "rs")
                    prev_gs = None
                    for tq in range(NQB):
                        bank = abank()
                        for kc in range(16):
                            K.mm(bank[:, :], wh[:, kc, w0:w0 + 128], hT[:, kc, tq * 512:(tq + 1) * 512], start=(kc == 0), stop=(kc == 15),
                                 reads=[wh] + hdeps[tq * 4:tq * 4 + 4], bank=bank)
                        gs = gs2[tq % 2]
                        K.op("act", lambda e, o=gs[:, 3:515], i=bank[:, :]: e.copy(out=o, in_=i), reads=[bank], writes=[gs])
                        if tq == 0:
                            K.op("dve", lambda e, o=gs[:, 0:3]: e.memset(o, 0.0), pwrites=[gs])
                        else:
                            K.op("dve", lambda e, o=gs[:, 0:3], i=prev_gs[:, 512:515]: e.tensor_copy(out=o, in_=i), reads=[prev_gs], pwrites=[gs])
                        prev_gs = gs
                        yield
                        K.op("dve", lambda e, o=t1[:, :], i=gs[:, 3:515], a=cwc[:, 3, cch:cch + 1]: e.tensor_scalar(out=o, in0=i, scalar1=a, scalar2=None, op0=ALU.mult),
                             reads=[gs, cwc], writes=[t1])
                        for k in (2, 1, 0):
                            K.op("dve", lambda e, o=t1[:, :], i=gs[:, k:k + 512], a=cwc[:, k, cch:cch + 1]:
                                 e.scalar_tensor_tensor(out=o, in0=i, scalar=a, in1=o, op0=ALU.mult, op1=ALU.add),
                                 reads=[gs, cwc], writes=[t1])
                        yield
                        if ci < 2:
                            dT = qT if ci == 0 else kT
                            K.op("act", lambda e, o=sg[:, :], i=t1[:, :]: e.activation(out=o, in_=i, func=AF.Silu), reads=[t1], writes=[sg])
                            K.op("act", lambda e, o=sq[:, :], i=sg[:, :]: e.activation(out=o, in_=i, func=AF.Square), reads=[sg], writes=[sq])
                            bank2 = abank()
                            K.mm(bank2[:, :], ones_bf[:, :], sq[:, :], start=True, stop=True, reads=[ones_bf, sq], bank=bank2)
                            K.op("act", lambda e, o=sd[:, :], i=bank2[:, :]: e.activation(out=o, in_=i, func=AF.Sqrt, bias=self.epsT[:, 0:1], scale=1.0),
                                 reads=[bank2, self.epsT], writes=[sd])
                            yield
                            K.op("dve", lambda e, o=rs[:, :], i=sd[:, :]: e.reciprocal(out=o, in_=i), reads=[sd], writes=[rs])
                            sc = (128.0 ** -0.5) if ci == 0 else 1.0
                            K.op("dve", lambda e, o=dT[:, tq * 512:(tq + 1) * 512], i0=sg[:, :], i1=rs[:, :], sc=sc:
                                 e.scalar_tensor_tensor(out=o, in0=i0, scalar=sc, in1=i1, op0=ALU.mult, op1=ALU.mult),
                                 reads=[sg, rs], pwrites=[dT])
                            if ci == 1:
                                bank3 = abank()
                                bvw = bank3.t[:, :].bitcast(BF16)
                                for tl in range(4):
                                    K.op("pe", lambda e, o=bvw[:, tl * 128:(tl + 1) * 128], i=kT[:, tq * 512 + tl * 128:tq * 512 + (tl + 1) * 128]:
                                         e.transpose(o, i, self.ident[:, :]), reads=[kT, self.ident],
                                         writes=[bank3] if tl == 0 else (), pwrites=[bank3] if tl else (), mark=(tl == 3))
                                K.op("act", lambda e, o=ktm[:, tq * 4:(tq + 1) * 4, :], i=bvw[:, 0:512].rearrange("p (a b) -> p a b", a=4): e.copy(out=o, in_=i),
                                     reads=[bank3], pwrites=[ktm])
                            yield
                        else:
                            K.op("act", lambda e, o=sg[:, :], i=t1[:, :]: e.activation(out=o, in_=i, func=AF.Silu), reads=[t1], writes=[sg])
                            bank3 = abank()
                            bvw = bank3.t[:, :].bitcast(BF16)
                            for tl in range(4):
                                K.op("pe", lambda e, o=bvw[:, tl * 128:(tl + 1) * 128], i=sg[:, tl * 128:(tl + 1) * 128]:
                                     e.transpose(o, i, self.ident[:, :]), reads=[sg, self.ident],
                                     writes=[bank3] if tl == 0 else (), pwrites=[bank3] if tl else (), mark=(tl == 3))
                            j = ci - 2
                            K.op("act", lambda e, o=vtm[:, tq * 4:(tq + 1) * 4, j * 128:(j + 1) * 128], i=bvw[:, 0:512].rearrange("p (a b) -> p a b", a=4): e.copy(out=o, in_=i),
                                 reads=[bank3], pwrites=[vtm])
                            yield
                            yield

                def z_chain():
                    for tb in range(TBS):
                        bank = abank()
                        for kc in range(16):
                            K.mm(bank[:, 0:256], hT[:, kc, tb * 128:(tb + 1) * 128], wh[:, kc, 512:768], start=(kc == 0), stop=(kc == 15),
                                 reads=[wh, hdeps[tb]], bank=bank)
                        K.op("act", lambda e, o=zs[:, tb, :], i=bank[:, 0:256]: e.activation(out=o, in_=i, func=AF.Silu), reads=[bank], pwrites=[zs])
                        K.op("dve", lambda e, o=zs[:, tb, :].rearrange("p (a b) -> p a b", a=2), i1=onw_bc[:, :].unsqueeze(1).to_broadcast([128, 2, 128]):
                             e.tensor_tensor(out=o, in0=o, in1=i1, op=ALU.mult), reads=[onw_bc], pwrites=[zs])
                        yield

                chains = [proj_chain(ci, w0, cch, ci) for ci, (w0, cch) in
                          enumerate(((0, kh), (128, 16 + kh), (256, 32 + 2 * kh), (384, 32 + 2 * kh + 1)))]
                chains.append(z_chain())
                live = list(chains)
                while live:
                    for g_ in list(live):
                        try:
                            next(g_)
                        except StopIteration:
                            live.remove(g_)
                self.release(mkP)
                if kh + 1 < KHN:
                    load_wh(kh + 1)
                if CSTOP <= 3:
                    continue
                mkR = self.mark()
                P1s = []
                for _ in range(2):
                    d_ = {n: self.sb(WS, F32, n) for n in ("GG", "dg", "tmp", "X", "XT", "Ra", "Rb", "Pa", "PTa")}
                    d_["Pb"], d_["PTb"] = d_["dg"], d_["GG"]
                    d_.update({n: self.sb(WS, BF16, n) for n in ("Rbf", "ke")})
                    P1s.append(d_)
                P2 = [dict(attnT=self.sb(WS, BF16, "attnT"), ub=self.sb(WS, F32, "ub"), wTb=self.sb(WS, BF16, "wTb"),
                           kd=self.sb(WS, BF16, "kd")) for _ in range(4)]
                hv0 = 2 * kh
                K.op("dve", lambda e: e.memset(ssq[:, :], 0.0), writes=[ssq])
                K.op("dve", lambda e: e.memset(S32[:, :, :], 0.0), writes=[S32])
                Sb0 = Sbr.next()
                K.op("dve", lambda e, o=Sb0[:, :, :]: e.memset(o, 0.0), writes=[Sb0])
                sbcur = [Sb0]

                def phase1(g, hv0=hv0):
                    tb0 = 2 * g
                    t = P1s[g % 2]
                    q = P2[g % 4]

                    def sv(name):
                        return S_[name][:, tb0:tb0 + 2, hv0:hv0 + 2]
                    bk = nqb()
                    for tbl in range(2):
                        ksl = slice((tb0 + tbl) * 128, (tb0 + tbl + 1) * 128)
                        K.op("pe", lambda e, o=bk.t[:, tbl * 128:(tbl + 1) * 128], l=kT[:, ksl]: e.matmul(o, l, l, start=True, stop=True),
                             reads=[kT], writes=[bk] if tbl == 0 else (), pwrites=[bk] if tbl else (), mark=False)
                    for tbl in range(2):
                        ksl = slice((tb0 + tbl) * 128, (tb0 + tbl + 1) * 128)
                        K.op("pe", lambda e, o=bk.t[:, (2 + tbl) * 128:(3 + tbl) * 128], l=kT[:, ksl], r=qT[:, ksl]: e.matmul(o, l, r, start=True, stop=True),
                             reads=[kT, qT], pwrites=[bk], mark=(tbl == 1))
                    K.op("act", lambda e, o=t["GG"][:, :, :, :], i=w4(bk): e.copy(out=o, in_=i), reads=[bk], writes=[t["GG"]])
                    yield
                    K.op("dve", lambda e, o=t["GG"][:, 0, :, :]: e.tensor_tensor(out=o, in0=o, in1=strict[:, :].unsqueeze(1).to_broadcast([128, 2, 128]), op=ALU.mult),
                         reads=[strict], writes=[t["GG"]])
                    K.op(POOLENG, lambda e, o=t["dg"][:, :, :, :], i1=bc4(sv("gcs")): e.tensor_tensor(out=o, in0=bcall(self.ident_f[:, :]), in1=i1, op=ALU.mult),
                         reads=[self.ident_f, S_["gcs"]], writes=[t["dg"]])
                    bk2 = nqb()
                    pe_quads(bk2, lambda ci, o: ((lambda e, o=o, r=t["dg"][:, ci // 2, ci % 2, :]: e.matmul(o, ones_f[:, :], r, start=True, stop=True)),
                                                 [ones_f, t["dg"]]))
                    K.op("dve", lambda e, o=t["tmp"][:, :, :, :], i0=w4(bk2): e.tensor_tensor(out=o, in0=i0, in1=bcall(nmask[:, :]), op=ALU.add),
                         reads=[bk2, nmask], writes=[t["tmp"]])
                    K.op("dve", lambda e, o=t["tmp"][:, :, :, :], i1=bc4(sv("gcs")): e.tensor_tensor(out=o, in0=o, in1=i1, op=ALU.subtract),
                         reads=[S_["gcs"]], writes=[t["tmp"]])
                    K.op("act", lambda e, o=t["tmp"][:, :, :, :]: e.activation(out=o, in_=o, func=AF.Exp), reads=[], writes=[t["tmp"]])
                    yield
                    E = t["tmp"]
                    K.op("dve", lambda e, o=q["attnT"][:, :, :, :], i0=bcj(t["GG"][:, 1, :, :]), i1=E[:, :, :, :]: e.tensor_tensor(out=o, in0=i0, in1=i1, op=ALU.mult),
                         reads=[t["GG"], E], writes=[q["attnT"]])
                    K.op("dve", lambda e, o=t["X"][:, :, :, :], i0=bcj(t["GG"][:, 0, :, :]), i1=E[:, :, :, :]: e.tensor_tensor(out=o, in0=i0, in1=i1, op=ALU.mult),
                         reads=[t["GG"], E], writes=[t["X"]])
                    K.op("dve", lambda e, o=t["X"][:, :, :, :], i1=bc4(sv("bet")): e.tensor_tensor(out=o, in0=o, in1=i1, op=ALU.mult),
                         reads=[S_["bet"]], writes=[t["X"]])
                    bk3 = nqb()
                    pe_quads(bk3, lambda ci, o: ((lambda e, o=o, i=t["X"][:, ci // 2, ci % 2, :]: e.transpose(o, i, self.ident_f[:, :])),
                                                 [t["X"], self.ident_f]))
                    K.op("act", lambda e, o=t["XT"][:, :, :, :], i=w4(bk3): e.copy(out=o, in_=i), reads=[bk3], writes=[t["XT"]])
                    K.op("dve", lambda e, o=t["Ra"][:, :, :, :], i1=t["X"][:, :, :, :]: e.tensor_tensor(out=o, in0=bcall(self.ident_f[:, :]), in1=i1, op=ALU.subtract),
                         reads=[self.ident_f, t["X"]], writes=[t["Ra"]])
                    yield
                    P, PT, R = t["X"], t["XT"], t["Ra"]

                    def square(P, PT, Pn, PTn, need_p):
                        bkT = nqb()
                        pe_quads(bkT, lambda ci, o: ((lambda e, o=o, l=P[:, ci // 2, ci % 2, :], r=PT[:, ci // 2, ci % 2, :]: e.matmul(o, l, r, start=True, stop=True)),
                                                     [P, PT]))
                        bkP = None
                        if need_p:
                            bkP = nqb()
                            pe_quads(bkP, lambda ci, o: ((lambda e, o=o, l=PT[:, ci // 2, ci % 2, :], r=P[:, ci // 2, ci % 2, :]: e.matmul(o, l, r, start=True, stop=True)),
                                                         [P, PT]))
                        return bkT, bkP

                    def evac(bkT, bkP, Pn, PTn):
                        K.op("act", lambda e, o=PTn[:, :, :, :], i=w4(bkT): e.copy(out=o, in_=i), reads=[bkT], writes=[PTn])
                        if bkP is not None:
                            K.op("act", lambda e, o=Pn[:, :, :, :], i=w4(bkP): e.copy(out=o, in_=i), reads=[bkP], writes=[Pn])
                    bkT, bkP = square(P, PT, t["Pa"], t["PTa"], True)
                    evac(bkT, bkP, t["Pa"], t["PTa"])
                    P, PT = t["Pa"], t["PTa"]
                    yield
                    for it in range(6):
                        Pn = t["Pb"] if it % 2 == 0 else t["Pa"]
                        PTn = t["PTb"] if it % 2 == 0 else t["PTa"]
                        Rn = t["Rb"] if it % 2 == 0 else t["Ra"]
                        bkR = nqb()
                        pe_quads(bkR, lambda ci, o: ((lambda e, o=o, l=PT[:, ci // 2, ci % 2, :], r=R[:, ci // 2, ci % 2, :]: e.matmul(o, l, r, start=True, stop=True)),
                                                     [PT, R]))
                        if it < 5:
                            bkT, bkP = square(P, PT, Pn, PTn, it < 4)
                        K.op("dve", lambda e, o=Rn[:, :, :, :], i0=w4(bkR), i1=R[:, :, :, :]: e.tensor_tensor(out=o, in0=i0, in1=i1, op=ALU.add),
                             reads=[bkR, R], writes=[Rn])
                        if it < 5:
                            evac(bkT, bkP, Pn, PTn)
                        P, PT, R = Pn, PTn, Rn
                        yield
                    K.op("act", lambda e, o=t["Rbf"][:, :, :, :], i=R[:, :, :, :]: e.copy(out=o, in_=i), reads=[R], writes=[t["Rbf"]])
                    K.op(POOLENG, lambda e, o=t["ke"][:, :, :, :], i0=bcj(ktm[:, tb0:tb0 + 2, :]), i1=bc4(sv("egc")): e.tensor_tensor(out=o, in0=i0, in1=i1, op=ALU.mult),
                         reads=[ktm, S_["egc"]], writes=[t["ke"]])
                    K.op(POOLENG, lambda e, o=q["kd"][:, :, :, :], i0=bcj(ktm[:, tb0:tb0 + 2, :]), i1=bc4(sv("ekd")): e.tensor_tensor(out=o, in0=i0, in1=i1, op=ALU.mult),
                         reads=[ktm, S_["ekd"]], writes=[q["kd"]])
                    yield
                    bkU = nqb()
                    pe_quads(bkU, lambda ci, o: ((lambda e, o=o, l=t["Rbf"][:, ci // 2, ci % 2, :], r=vtm[:, tb0 + ci // 2, (ci % 2) * 128:(ci % 2 + 1) * 128]:
                                                  e.matmul(o, l, r, start=True, stop=True)), [t["Rbf"], vtm]))
                    bkW = nqb()
                    pe_quads(bkW, lambda ci, o: ((lambda e, o=o, l=t["ke"][:, ci // 2, ci % 2, :], r=t["Rbf"][:, ci // 2, ci % 2, :]:
                                                  e.matmul(o, l, r, start=True, stop=True)), [t["Rbf"], t["ke"]]))
                    K.op("dve", lambda e, o=q["ub"][:, :, :, :], i0=w4(bkU), i1=bc4(sv("bet")): e.tensor_tensor(out=o, in0=i0, in1=i1, op=ALU.mult),
                         reads=[bkU, S_["bet"]], writes=[q["ub"]])
                    K.op("act", lambda e, o=q["wTb"][:, :, :, :], i=w4(bkW): e.copy(out=o, in_=i), reads=[bkW], writes=[q["wTb"]])
                    yield

                def phase2(g, hv0=hv0):
                    q = P2[g % 4]
                    for tbl in range(2):
                        tb = 2 * g + tbl
                        ksl = slice(tb * 128, (tb + 1) * 128)

                        def col(name, j, tb=tb):
                            return S_[name][:, tb, hv0 + j:hv0 + j + 1]
                        Sold = sbcur[0]
                        bkV = nqb2()
                        for j in range(2):
                            K.op("pe", lambda e, o=bkV.t[:, j * 128:(j + 1) * 128], l=q["wTb"][:, tbl, j, :], r=Sold[:, j, :]: e.matmul(o, l, r, start=True, stop=True),
                                 reads=[q["wTb"], Sold], writes=[bkV] if j == 0 else (), pwrites=[bkV] if j else (), mark=(j == 1))
                        yield
                        vnew = vnr.next()
                        for j in range(2):
                            K.op("dve", lambda e, o=vnew[:, j, :], i0=bkV.t[:, j * 128:(j + 1) * 128], a=col("bet", j), i1=q["ub"][:, tbl, j, :]:
                                 e.scalar_tensor_tensor(out=o, in0=i0, scalar=a, in1=i1, op0=ALU.mult, op1=ALU.subtract),
                                 reads=[bkV, S_["bet"], q["ub"]], writes=[vnew] if j == 0 else (), pwrites=[vnew] if j else ())
                        yield
                        bkS = bkV
                        for j in range(2):
                            K.op("pe", lambda e, o=bkS.t[:, j * 128:(j + 1) * 128], l=q["kd"][:, tbl, j, :], r=vnew[:, j, :]: e.matmul(o, l, r, start=True, stop=True),
                                 reads=[q["kd"], vnew], writes=[bkS] if j == 0 else (), pwrites=[bkS] if j else (), mark=(j == 1))
                        bkO = nqb2()
                        for j in range(2):
                            K.op("pe", lambda e, o=bkO.t[:, j * 128:(j + 1) * 128], l=qT[:, ksl], r=Sold[:, j, :]: e.matmul(o, l, r, start=True, stop=True),
                                 reads=[qT, Sold], writes=[bkO] if j == 0 else (), pwrites=[bkO] if j else (), mark=False)
                        for j in range(2):
                            K.op("pe", lambda e, o=bkO.t[:, (2 + j) * 128:(3 + j) * 128], l=q["attnT"][:, tbl, j, :], r=vnew[:, j, :]: e.matmul(o, l, r, start=True, stop=True),
                                 reads=[q["attnT"], vnew], pwrites=[bkO], mark=(j == 1))
                        yield
                        for j in range(2):
                            K.op("dve", lambda e, o=S32[:, j, :], i1=bkS.t[:, j * 128:(j + 1) * 128], a=col("egl", j):
                                 e.scalar_tensor_tensor(out=o, in0=o, scalar=a, in1=i1, op0=ALU.mult, op1=ALU.subtract),
                                 reads=[bkS, S_["egl"]], writes=[S32])
                        Snew = Sbr.next()
                        K.op("act", lambda e, o=Snew[:, :, :], i=S32[:, :, :]: e.copy(out=o, in_=i), reads=[S32], writes=[Snew])
                        sbcur[0] = Snew
                        yield
                        for j in range(2):
                            K.op("act", lambda e, o=o1t[:, j, :], i=bkO.t[:, j * 128:(j + 1) * 128], a=col("egc", j): e.activation(out=o, in_=i, func=AF.Copy, scale=a),
                                 reads=[bkO, S_["egc"]], writes=[o1t] if j == 0 else (), pwrites=[o1t] if j else ())
                        K.op("dve", lambda e, o=oall[:, tb, :, :], i1=bkO.t[:, 256:512].rearrange("p (a b) -> p a b", a=2), i0=o1t[:, :, :]: e.tensor_tensor(out=o, in0=i0, in1=i1, op=ALU.subtract),
                             reads=[bkO, o1t], pwrites=[oall])
                        for j in range(2):
                            K.op("act", lambda e, o=junkb[:, :], i=oall[:, tb, j, :], a=ssq[:, 2 * tb + j:2 * tb + j + 1]: e.activation(out=o, in_=i, func=AF.Square, accum_out=a),
                                 reads=[oall], writes=[junkb], pwrites=[ssq])
                        yield

                NG = TBS // 2

                def lockstep(gens):
                    live = list(gens)
                    while live:
                        for g_ in list(live):
                            try:
                                next(g_)
                            except StopIteration:
                                live.remove(g_)
                        yield

                def chain(gens):
                    for g_ in gens:
                        yield from g_

                def delayed(gen_, n_):
                    for _ in range(n_):
                        yield
                    yield from gen_

                for _ in lockstep([phase1(0), delayed(phase1(1), STAG)]):
                    pass
                for p_ in range(NG // 2):
                    a = chain([phase2(2 * p_), phase2(2 * p_ + 1)])
                    b = lockstep([phase1(2 * p_ + 2), delayed(phase1(2 * p_ + 3), STAG)]) if 2 * p_ + 2 < NG else iter(())
                    a_live, b_live = True, True
                    while a_live or b_live:
                        if a_live:
                            try:
                                next(a)
                            except StopIteration:
                                a_live = False
                        for _ in range(ILV if a_live else 1000):
                            if b_live:
                                try:
                                    next(b)
                                except StopIteration:
                                    b_live = False
                            else:
                                break
                g0 = s * SEQ
                NC2 = TBS * 2
                K.op("act", lambda e, o=ssq[:, NC2:2 * NC2], i=ssq[:, 0:NC2]: e.activation(out=o, in_=i, func=AF.Ln, bias=self.epsT[:, 0:1], scale=1.0 / 128),
                     reads=[ssq, self.epsT], pwrites=[ssq])
                K.op("act", lambda e, o=ssq[:, NC2:2 * NC2]: e.activation(out=o, in_=o, func=AF.Exp, scale=-0.5), reads=[ssq], pwrites=[ssq])
                ov = oall[:, :, :, :].rearrange("p t j e -> p (t j) e")
                K.op("dve", lambda e, o=ov, i1=ssq[:, NC2:2 * NC2].unsqueeze(2).to_broadcast([128, NC2, 128]): e.tensor_tensor(out=o, in0=o, in1=i1, op=ALU.mult),
                     reads=[ssq], writes=[oall])
                K.op("dve", lambda e, o=oall[:, :, :, :].rearrange("p t j e -> p t (j e)"), i1=zs[:, :, :]: e.tensor_tensor(out=o, in0=o, in1=i1, op=ALU.mult),
                     reads=[zs], writes=[oall])
                for j in range(2):
                    hv = 2 * kh + j
                    G8 = min(8, TBS)
                    for t8 in range(TBS // G8):
                        bank = nqb2()
                        bvw = bank.t[:, :].bitcast(BF16)
                        for tl in range(G8):
                            K.op("pe", lambda e, o=bvw[:, tl * 128:(tl + 1) * 128], i=oall[:, t8 * G8 + tl, j, :]: e.transpose(o, i, self.ident[:, :]),
                                 reads=[oall, self.ident], writes=[bank] if tl == 0 else (), pwrites=[bank] if tl else (), mark=(tl == G8 - 1))
                        stg = stgr.next()
                        K.op("act", lambda e, o=stg[:, 0:G8 * 128], i=bvw[:, 0:G8 * 128]: e.copy(out=o, in_=i), reads=[bank], writes=[stg])
                        c0 = g0 + t8 * G8 * 128
                        K.dma("sp", self.OT[hv * 128:(hv + 1) * 128, c0:c0 + G8 * 128], stg[:, 0:G8 * 128], stg, reads=[stg],
                              pwrites=ot_deps[c0 // 512:(c0 + G8 * 128) // 512])
                self.release(mkR)
            self.release(mk)
        self.release(mk_top)
        mk = self.mark()
        wring = self.ring(2, [128, 32, 512], BF16, "wpan")
        uTr = self.ring(2, [128, 32, 512], BF16, "uT")
        xrr = self.ring(3, [128, 512], F32, "xr")
        xor_ = self.ring(3, [128, 512], F32, "xo")
        wov = self.c_w_out[0].rearrange("(kc p) n -> p kc n", p=128)
        otv = self.OT.rearrange("(kc p) t -> p kc t", p=128)
        for s in range(NSEQ):
            for tt in range(TBS // 4):
                g0 = s * SEQ + tt * 512
                uT = uTr.next()
                K.dma("sp", uT[:, :, :], otv[:, :, g0:g0 + 512], uT, reads=[ot_deps[g0 // 512]], writes=[uT])
                self.out_proj_tile(wov, 32, uT, 4, g0 // 128, src, dst, wring, xrr, xor_)
        self.release(mk)

ALL_SUBLAYERS = [("a", 0), ("ffn", 0), ("b", 1), ("ffn", 1), ("c", 2), ("ffn", 2), ("a", 3), ("ffn", 3)]


def make_consts():
    c = np.zeros((128, 1024), np.float32)
    c[:, 0:128] = np.eye(128, dtype=np.float32)
    c[:, 128:256] = np.tril(np.ones((128, 128), np.float32))
    c[:, 256:384] = np.triu(np.ones((128, 128), np.float32))
    return c


def build_nc(sublayers=ALL_SUBLAYERS, ntb_seq=16, nseq=2):
    nc = bass.Bass("TRN2", target_bir_lowering=False)
    with ExitStack() as es:
        p = Prog(nc, es, sublayers, ntb_seq, nseq)
        p.build()
    return nc


def kernel(**inputs):
    x = np.ascontiguousarray(inputs["x"], dtype=np.float32)
    nc = build_nc()
    consts = make_consts()
    shared = {k: np.ascontiguousarray(v, dtype=np.float32) for k, v in inputs.items() if k != "x"}
    shared["consts"] = consts
    in_maps = []
    for c in range(NCORES):
        m = dict(shared)
        m["x"] = x[2 * c:2 * c + 2].reshape(NT, D)
        in_maps.append(m)
    res = run_bass_kernel_spmd(nc, in_maps, core_ids=list(range(NCORES)))
    out = np.stack([r["out"].reshape(2, SEQ, D) for r in res.results], axis=0).reshape(16, SEQ, D)
    return out.astype(np.float32)
```
